# Optimizing a Trainium2 kernel written in Bass

```python
import math
import jax, jax.numpy as jnp
from jax import lax
import numpy as np

D_MODEL = 1024
BATCH = 32
SEQ = 256
DEPTH = 2
DEC_BATCH = 8
DEC_SEQ = 4096
PAST_LEN = 256

GRID_W = 64
D_A = 512
SSM_P = 16
SSM_G = D_A // SSM_P
SSM_N = 64
D_B = 512
SGU_HEADS = 8
SGU_HD = D_B // SGU_HEADS
CHUNK = 128
N_HEADS = 8
N_KV = 2
GQ = N_HEADS // N_KV
HEAD_DIM = 64
D_C = N_HEADS * HEAD_DIM
KV_W = N_KV * HEAD_DIM
WINDOW = 128
BLOCK = 128
ROPE_BASE = 10000.0
D_FF = 2816
CONV_W = 3
N_BRANCH = 3
EPS = 1e-6
OFF_B = D_A
OFF_Q = OFF_B + 2 * D_B
OFF_K = OFF_Q + D_C
OFF_V = OFF_K + KV_W
OFF_G = OFF_V + KV_W
D_IN = OFF_G + N_BRANCH * D_MODEL

kernel_name = 'hybrid_s5_sgu_swa_prefix_diffusion_step'

F32 = jnp.float32


def rmsnorm(x, g):
    xf = x.astype(F32)
    y = xf * lax.rsqrt(jnp.mean(xf * xf, axis=-1, keepdims=True) + EPS)
    return (y * g.astype(F32)).astype(x.dtype)


def rope_2d(x):
    T = x.shape[1]
    rows = T // GRID_W
    row = jnp.repeat(jnp.arange(rows), GRID_W)
    col = jnp.tile(jnp.arange(GRID_W), rows)
    half = HEAD_DIM // 2
    inv = ROPE_BASE ** (-jnp.arange(0, half, 2, dtype=F32) / half)

    def rot(xa, pos):
        ang = pos.astype(F32)[:, None] * inv[None, :]
        cos = jnp.cos(ang)[None, :, None, :]
        sin = jnp.sin(ang)[None, :, None, :]
        x1, x2 = jnp.split(xa.astype(F32), 2, axis=-1)
        return jnp.concatenate([x1 * cos - x2 * sin, x2 * cos + x1 * sin], axis=-1)

    out = jnp.concatenate([rot(x[..., :half], row), rot(x[..., half:], col)], axis=-1)
    return out.astype(x.dtype)


def _cplx_combine(e1, e2):
    a1r, a1i, b1r, b1i = e1
    a2r, a2i, b2r, b2i = e2
    return (a2r * a1r - a2i * a1i,
            a2r * a1i + a2i * a1r,
            a2r * b1r - a2i * b1i + b2r,
            a2r * b1i + a2i * b1r + b2i)


def _s5_scan(uf, lam_re, lam_im, log_step, b_re, b_im, reverse, h0_re=None, h0_im=None):
    T = uf.shape[1]
    lr = lam_re.astype(F32)
    li = lam_im.astype(F32)
    dt = jnp.exp(log_step.astype(F32))[:, None]
    ar, ai = lr * dt, li * dt
    mag = jnp.exp(ar)
    lb_re, lb_im = mag * jnp.cos(ai), mag * jnp.sin(ai)
    den = lr * lr + li * li
    f_re = ((lb_re - 1.0) * lr + lb_im * li) / den
    f_im = (lb_im * lr - (lb_re - 1.0) * li) / den
    br, bi = b_re.astype(F32), b_im.astype(F32)
    bb_re = f_re[..., None] * br - f_im[..., None] * bi
    bb_im = f_re[..., None] * bi + f_im[..., None] * br
    x_re = jnp.einsum('btgp,gnp->btgn', uf, bb_re)
    x_im = jnp.einsum('btgp,gnp->btgn', uf, bb_im)
    a_re = jnp.broadcast_to(lb_re, (1, T) + lb_re.shape)
    a_im = jnp.broadcast_to(lb_im, (1, T) + lb_im.shape)
    _, _, s_re, s_im = lax.associative_scan(_cplx_combine, (a_re, a_im, x_re, x_im),
                                            reverse=reverse, axis=1)
    if h0_re is not None:
        k = (T - jnp.arange(T)) if reverse else (jnp.arange(T) + 1)
        kf = k.astype(F32)[:, None, None]
        pm = jnp.exp(kf * ar)
        p_re, p_im = pm * jnp.cos(kf * ai), pm * jnp.sin(kf * ai)
        h_re = h0_re.astype(F32)[:, None]
        h_im = h0_im.astype(F32)[:, None]
        s_re = s_re + p_re * h_re - p_im * h_im
        s_im = s_im + p_re * h_im + p_im * h_re
    return s_re, s_im


def s5_branch(u, lp, h0_re=None, h0_im=None):
    B, T, _ = u.shape
    uf = u.astype(F32).reshape(B, T, SSM_G, SSM_P)
    y = lp['d_skip'].astype(F32).reshape(SSM_G, SSM_P) * uf
    fin_re, fin_im = [], []
    for d, rev in enumerate((False, True)):
        s_re, s_im = _s5_scan(uf, lp['lam_re'][d], lp['lam_im'][d], lp['log_step'][d],
                              lp['b_re'][d], lp['b_im'][d], rev,
                              None if h0_re is None else h0_re[:, d],
                              None if h0_im is None else h0_im[:, d])
        y = y + jnp.einsum('btgn,gpn->btgp', s_re, lp['c_re'][d].astype(F32)) \
              - jnp.einsum('btgn,gpn->btgp', s_im, lp['c_im'][d].astype(F32))
        if h0_re is None:
            idx = 0 if rev else T - 1
            fin_re.append(s_re[:, idx])
            fin_im.append(s_im[:, idx])
    y = jax.nn.gelu(y.reshape(B, T, D_A)).astype(u.dtype)
    z1, z2 = jnp.split(y @ lp['w_glu'], 2, axis=-1)
    out = z1 * jax.nn.sigmoid(z2)
    if h0_re is None:
        return out, jnp.stack(fin_re, axis=1), jnp.stack(fin_im, axis=1)
    return out


def chunk_sgu(uv, lp):
    B, T, _ = uv.shape
    u, v = jnp.split(jax.nn.gelu(uv), 2, axis=-1)
    vf = v.astype(F32)
    mu = jnp.mean(vf, axis=-1, keepdims=True)
    var = jnp.mean((vf - mu) ** 2, axis=-1, keepdims=True)
    vn = ((vf - mu) * lax.rsqrt(var + EPS) * lp['g_sgu'].astype(F32)).astype(v.dtype)
    vc = vn.reshape(B, T // CHUNK, CHUNK, SGU_HEADS, SGU_HD)
    mixed = jnp.einsum('hqs,bnshd->bnqhd', lp['w_spatial'], vc) + lp['b_spatial'].T[:, :, None]
    return (u * mixed.reshape(B, T, D_B)) @ lp['w_b_out']


def context_attention(q, k, v, sink):
    B, L = q.shape[0], q.shape[1]
    nq = L // BLOCK
    scale = HEAD_DIM ** -0.5
    qb = q.reshape(B, nq, BLOCK, N_KV, GQ, HEAD_DIM).transpose(1, 0, 2, 3, 4, 5)
    sk = sink.astype(F32).reshape(N_KV, GQ)[None, :, :, None, None]

    def one(qblk):
        s = jnp.einsum('bqkgd,bskd->bkgqs', qblk, k, preferred_element_type=F32) * scale
        m = jnp.maximum(jnp.max(s, axis=-1, keepdims=True), sk)
        p = jnp.exp(s - m)
        inv = 1.0 / (jnp.sum(p, axis=-1, keepdims=True) + jnp.exp(sk - m))
        return jnp.einsum('bkgqs,bskd->bqkgd', (p * inv).astype(v.dtype), v)

    out = lax.map(one, qb)
    return out.transpose(1, 0, 2, 3, 4, 5).reshape(B, L, D_C)


def window_attention(q, k, v, ck, cv, sink):
    B, T = q.shape[0], q.shape[1]
    nb = T // BLOCK
    scale = HEAD_DIM ** -0.5
    qb = q.reshape(B, nb, BLOCK, N_KV, GQ, HEAD_DIM)

    def band(x):
        xp = jnp.pad(x, ((0, 0), (BLOCK, BLOCK), (0, 0), (0, 0)))
        xp = xp.reshape(B, nb + 2, BLOCK, N_KV, HEAD_DIM)
        return jnp.concatenate([xp[:, :-2], xp[:, 1:-1], xp[:, 2:]], axis=2)

    kb, vb = band(k), band(v)
    qi = jnp.arange(BLOCK)[:, None]
    kj = jnp.arange(3 * BLOCK)[None, :]
    key_pos = jnp.arange(nb)[:, None, None] * BLOCK - BLOCK + kj[None]
    mask = (jnp.abs(kj - BLOCK - qi) <= WINDOW)[None] & (key_pos >= 0) & (key_pos < T)
    s_loc = jnp.einsum('bnqkgd,bnskd->bnkgqs', qb, kb, preferred_element_type=F32) * scale
    s_loc = jnp.where(mask[None, :, None, None], s_loc, -jnp.inf)
    s_ctx = jnp.einsum('bnqkgd,bskd->bnkgqs', qb, ck, preferred_element_type=F32) * scale
    sk = sink.astype(F32).reshape(N_KV, GQ)[None, None, :, :, None, None]
    m = jnp.maximum(jnp.maximum(jnp.max(s_loc, axis=-1, keepdims=True),
                                jnp.max(s_ctx, axis=-1, keepdims=True)), sk)
    p_loc = jnp.exp(s_loc - m)
    p_ctx = jnp.exp(s_ctx - m)
    inv = 1.0 / (jnp.sum(p_loc, axis=-1, keepdims=True) + jnp.sum(p_ctx, axis=-1, keepdims=True)
                 + jnp.exp(sk - m))
    o = jnp.einsum('bnkgqs,bnskd->bnqkgd', (p_loc * inv).astype(vb.dtype), vb) \
        + jnp.einsum('bnkgqs,bskd->bnqkgd', (p_ctx * inv).astype(cv.dtype), cv)
    return o.reshape(B, T, D_C)


def conv_ffn(h, lp):
    z = h @ lp['w_up']
    z = lax.conv_general_dilated(z, lp['conv_w'].astype(z.dtype)[:, None, :], window_strides=(1,),
                                 padding=((CONV_W // 2, CONV_W // 2),),
                                 dimension_numbers=('NWC', 'WIO', 'NWC'),
                                 feature_group_count=2 * D_FF) + lp['conv_b']
    g, val = jnp.split(z, 2, axis=-1)
    return (jax.nn.silu(g) * val) @ lp['w_down']


def token_mixer(h, lp, ctx):
    B, T, _ = h.shape
    a_in, uv, q, k, v, gates = jnp.split(h @ lp['w_in'], [OFF_B, OFF_Q, OFF_K, OFF_V, OFF_G], axis=-1)
    q = q.reshape(B, T, N_HEADS, HEAD_DIM)
    k = k.reshape(B, T, N_KV, HEAD_DIM)
    v = v.reshape(B, T, N_KV, HEAD_DIM)
    if ctx is None:
        y_a, fin_re, fin_im = s5_branch(a_in, lp)
        y_c = context_attention(q, k, v, lp['sink'])
        new = (k, v, fin_re, fin_im)
    else:
        ck, cv, h0_re, h0_im = ctx
        y_a = s5_branch(a_in, lp, h0_re, h0_im)
        y_c = window_attention(rope_2d(q), rope_2d(k), v, ck, cv, lp['sink'])
        new = None
    y_b = chunk_sgu(uv, lp)
    y_c = y_c @ lp['w_c_out']
    g_a, g_b, g_c = jnp.split(jax.nn.sigmoid(gates), N_BRANCH, axis=-1)
    merged = g_a * y_a + g_b * y_b + g_c * y_c
    return merged @ lp['w_o'], new


def layer(x, mod, lp, ctx):
    sh1, sc1, g1, sh2, sc2, g2 = jnp.split(mod, 6, axis=-1)
    h = rmsnorm(x, lp['g_pre_mix']) * (1.0 + sc1) + sh1
    m, new = token_mixer(h, lp, ctx)
    x = x + g1 * rmsnorm(m, lp['g_post_mix'])
    h = rmsnorm(x, lp['g_pre_ffn']) * (1.0 + sc2) + sh2
    x = x + g2 * rmsnorm(conv_ffn(h, lp), lp['g_post_ffn'])
    return x, new


def setup_inputs(seed: int = 0) -> dict:
    key = jax.random.key(seed)
    ks = iter(jax.random.split(key, 48))

    def nrm(shape, scale):
        return jax.random.normal(next(ks), shape, F32) * scale

    L = DEPTH
    lam_im_base = math.pi * jnp.arange(SSM_N, dtype=F32)
    return {
        'x_prompt': nrm((BATCH, SEQ, D_MODEL), 1.0),
        'x_sample': nrm((DEC_BATCH, DEC_SEQ, D_MODEL), 1.0),
        'cache_k': nrm((DEC_BATCH, DEPTH, PAST_LEN, N_KV, HEAD_DIM), 1.0),
        'cache_v': nrm((DEC_BATCH, DEPTH, PAST_LEN, N_KV, HEAD_DIM), 1.0),
        'state_ssm_re': nrm((DEC_BATCH, DEPTH, 2, SSM_G, SSM_N), 0.5),
        'state_ssm_im': nrm((DEC_BATCH, DEPTH, 2, SSM_G, SSM_N), 0.5),
        'c': nrm((DEC_BATCH, D_MODEL), 1.0),
        'c_ctx': nrm((D_MODEL,), 1.0),
        'w_mod': nrm((L, D_MODEL, 6 * D_MODEL), 0.5 * D_MODEL ** -0.5),
        'b_mod': nrm((L, 6 * D_MODEL), 0.1),
        'g_pre_mix': 1.0 + nrm((L, D_MODEL), 0.02),
        'g_post_mix': 1.0 + nrm((L, D_MODEL), 0.02),
        'g_pre_ffn': 1.0 + nrm((L, D_MODEL), 0.02),
        'g_post_ffn': 1.0 + nrm((L, D_MODEL), 0.02),
        'w_in': nrm((L, D_MODEL, D_IN), D_MODEL ** -0.5),
        'lam_re': -0.5 + nrm((L, 2, SSM_G, SSM_N), 0.01),
        'lam_im': lam_im_base + nrm((L, 2, SSM_G, SSM_N), 0.01),
        'log_step': jax.random.uniform(next(ks), (L, 2, SSM_G), F32, math.log(1e-3), math.log(1e-1)),
        'b_re': nrm((L, 2, SSM_G, SSM_N, SSM_P), (2 * SSM_P) ** -0.5),
        'b_im': nrm((L, 2, SSM_G, SSM_N, SSM_P), (2 * SSM_P) ** -0.5),
        'c_re': nrm((L, 2, SSM_G, SSM_P, SSM_N), (2 * SSM_N) ** -0.5),
        'c_im': nrm((L, 2, SSM_G, SSM_P, SSM_N), (2 * SSM_N) ** -0.5),
        'd_skip': nrm((L, D_A), 0.5),
        'w_glu': nrm((L, D_A, 2 * D_MODEL), D_A ** -0.5),
        'g_sgu': 1.0 + nrm((L, D_B), 0.02),
        'w_spatial': nrm((L, SGU_HEADS, CHUNK, CHUNK), CHUNK ** -0.5),
        'b_spatial': 1.0 + nrm((L, SGU_HEADS, CHUNK), 0.01),
        'w_b_out': nrm((L, D_B, D_MODEL), D_B ** -0.5),
        'sink': nrm((L, N_HEADS), 0.5),
        'w_c_out': nrm((L, D_C, D_MODEL), D_C ** -0.5),
        'w_o': nrm((L, D_MODEL, D_MODEL), D_MODEL ** -0.5),
        'w_up': nrm((L, D_MODEL, 2 * D_FF), D_MODEL ** -0.5),
        'conv_w': nrm((L, CONV_W, 2 * D_FF), CONV_W ** -0.5),
        'conv_b': nrm((L, 2 * D_FF), 0.01),
        'w_down': nrm((L, D_FF, D_MODEL), D_FF ** -0.5),
    }


def reference(x_prompt, x_sample, cache_k, cache_v, state_ssm_re, state_ssm_im, c, c_ctx,
              w_mod, b_mod, g_pre_mix, g_post_mix, g_pre_ffn, g_post_ffn, w_in,
              lam_re, lam_im, log_step, b_re, b_im, c_re, c_im, d_skip, w_glu,
              g_sgu, w_spatial, b_spatial, w_b_out, sink, w_c_out, w_o,
              w_up, conv_w, conv_b, w_down):
    xp, xs = x_prompt, x_sample
    new_k, new_v, new_re, new_im = [], [], [], []
    for l in range(DEPTH):
        lp = {
            'g_pre_mix': g_pre_mix[l], 'g_post_mix': g_post_mix[l],
            'g_pre_ffn': g_pre_ffn[l], 'g_post_ffn': g_post_ffn[l],
            'w_in': w_in[l], 'lam_re': lam_re[l], 'lam_im': lam_im[l], 'log_step': log_step[l],
            'b_re': b_re[l], 'b_im': b_im[l], 'c_re': c_re[l], 'c_im': c_im[l],
            'd_skip': d_skip[l], 'w_glu': w_glu[l], 'g_sgu': g_sgu[l],
            'w_spatial': w_spatial[l], 'b_spatial': b_spatial[l], 'w_b_out': w_b_out[l],
            'sink': sink[l], 'w_c_out': w_c_out[l], 'w_o': w_o[l],
            'w_up': w_up[l], 'conv_w': conv_w[l], 'conv_b': conv_b[l], 'w_down': w_down[l],
        }
        mod_p = (jax.nn.silu(c_ctx) @ w_mod[l] + b_mod[l])[None, None, :]
        xp, (k_l, v_l, fr_l, fi_l) = layer(xp, mod_p, lp, None)
        new_k.append(k_l)
        new_v.append(v_l)
        new_re.append(fr_l)
        new_im.append(fi_l)
        mod_s = (jax.nn.silu(c) @ w_mod[l] + b_mod[l])[:, None, :]
        xs, _ = layer(xs, mod_s, lp, (cache_k[:, l], cache_v[:, l],
                                      state_ssm_re[:, l], state_ssm_im[:, l]))
    new_cache_k = jnp.stack(new_k, axis=1)
    new_cache_v = jnp.stack(new_v, axis=1)
    new_state_ssm_re = jnp.stack(new_re, axis=1)
    new_state_ssm_im = jnp.stack(new_im, axis=1)
    return (xp, xs, new_cache_k, new_cache_v, new_state_ssm_re, new_state_ssm_im)
```

```python
import contextlib
import math
import numpy as np
import concourse.bass as bass
import concourse.mybir as mybir
from concourse.bass_utils import run_bass_kernel_spmd

F32 = mybir.dt.float32
BF16 = mybir.dt.bfloat16
AF = mybir.ActivationFunctionType
ALU = mybir.AluOpType
AX = mybir.AxisListType

COMPUTE = ("pe", "act", "dve", "pool")
N_DMA_SEMS = 12

D = 1024
NTOK = 5120
TS = 4096
NBLK = 10
D_IN = 5376
OFF_B, OFF_Q, OFF_K, OFF_V, OFF_G = 512, 1536, 2048, 2176, 2304
D_FF = 2816
EPS = 1e-6
NL = 2


class Sched:
    def __init__(self):
        self.streams = {e: [] for e in ("pe", "act", "dve", "pool", "sp")}
        self.cnt = {e: 0 for e in COMPUTE}
        self.known = {e: {} for e in self.streams}
        self.last_w = {}
        self.readers = {}
        self.dma_cnt = {}
        self.dma_rr = {q: 0 for q in ("sp", "act", "pool", "bg")}
        self.n_ins = 0

    def _need(self, eng, tickets):
        kn = self.known[eng]
        best = {}
        for t in tickets:
            if t is None:
                continue
            k, v = t
            if kn.get(k, 0) >= v:
                continue
            if best.get(k, 0) < v:
                best[k] = v
        for k, v in best.items():
            kn[k] = v
        return list(best.items())

    def _deps(self, eng, reads, writes, is_dma=False):
        ts = []
        for r in reads:
            ts.append(self.last_w.get(r))
        for w in writes:
            ts.append(self.last_w.get(w))
            for t in self.readers.get(w, ()):
                if (not is_dma) and t[0] == eng:
                    continue
                ts.append(t)
        if eng == "pe" and not is_dma:
            ts = [t for t in ts if t is not None and t[0] != "pe"]
        return ts

    def _commit(self, ticket, reads, writes):
        for r in reads:
            self.readers.setdefault(r, []).append(ticket)
        for w in writes:
            self.last_w[w] = ticket
            self.readers[w] = []

    def op(self, eng, fn, reads=(), writes=()):
        waits = self._need(eng, self._deps(eng, reads, writes))
        self.cnt[eng] += 1
        ticket = (eng, self.cnt[eng])
        self.streams[eng].append(("op", fn, waits, eng))
        self._commit(ticket, reads, writes)
        self.n_ins += 1
        return ticket

    def dma(self, q, fn, reads=(), writes=(), bg=False):
        qq = "bg" if bg else q
        i = self.dma_rr[qq]
        self.dma_rr[qq] = (i + 1) % N_DMA_SEMS
        key = ("dma", qq, i)
        n = self.dma_cnt.get(key, 0)
        deps = self._deps(q, reads, writes, is_dma=True)
        if n > 0:
            deps.append((key, 16 * n))
        waits = self._need(q, deps)
        self.dma_cnt[key] = n + 1
        ticket = (key, 16 * (n + 1))
        self.streams[q].append(("dma", fn, waits, key))
        self._commit(ticket, reads, writes)
        self.n_ins += 1
        return ticket

    def wait_all(self, eng, include_bg=True):
        ts = []
        for e in COMPUTE:
            if self.cnt[e]:
                ts.append((e, self.cnt[e]))
        for key, n in self.dma_cnt.items():
            if key[1] == "bg" and not include_bg:
                continue
            ts.append((key, 16 * n))
        waits = self._need(eng, ts)
        self.streams[eng].append(("wait", None, waits, None))

    def barrier(self):
        for e in ("pe", "act", "dve", "pool", "sp"):
            self.wait_all(e, include_bg=False)

    def emit(self, nc, stack):
        sems = {}
        for e in COMPUTE:
            sems[e] = stack.enter_context(nc.semaphore("s_" + e))
        for q in ("sp", "act", "pool", "bg"):
            for i in range(N_DMA_SEMS):
                if ("dma", q, i) in self.dma_cnt:
                    sems[("dma", q, i)] = stack.enter_context(nc.semaphore("d_%s%d" % (q, i)))
        block = stack.enter_context(nc.Block())
        streams = self.streams

        def run(engobj, name):
            for kind, fn, waits, key in streams[name]:
                for k, v in waits:
                    engobj.wait_ge(sems[k], v)
                if kind == "op":
                    fn(engobj).then_inc(sems[key], 1)
                elif kind == "dma":
                    fn(engobj).then_inc(sems[key], 16)

        @block.sync
        def _(e):
            run(e, "sp")

        @block.tensor
        def _(e):
            run(e, "pe")

        @block.scalar
        def _(e):
            run(e, "act")

        @block.vector
        def _(e):
            run(e, "dve")

        @block.gpsimd
        def _(e):
            run(e, "pool")


class Prog:
    def __init__(self, nc, dbg=None):
        self.nc = nc
        self.S = Sched()
        self.dbg = dbg or {}
        self.uid = 0
        self.ps_rr = 0

    def name(self, p):
        self.uid += 1
        return "%s_%d" % (p, self.uid)


def build(stage=99, dbg_names=()):
    nc = bass.Bass("TRN2", target_bir_lowering=False)
    P = Prog(nc)
    S = P.S

    def din(name, shape, dt=F32):
        return nc.dram_tensor(name, list(shape), dt, kind="ExternalInput").ap()

    def dout(name, shape, dt=F32):
        return nc.dram_tensor(name, list(shape), dt, kind="ExternalOutput").ap()

    def dscr(name, shape, dt=F32):
        return nc.dram_tensor(name, list(shape), dt, kind="Internal").ap()

    x0 = din("x0", [NTOK, D])
    ckv = din("ckv", [NL, 2, 256, 128])
    cvec = din("cvec", [2, D])
    w_mod = din("w_mod", [NL, D, 6 * D]); b_mod = din("b_mod", [NL, 6 * D])
    gvec = din("gvec", [NL, 4, D])
    w_in = din("w_in", [NL, D, D_IN])
    w_glu = din("w_glu", [NL, 512, 2048])
    w_b_out = din("w_b_out", [NL, 512, D]); w_c_out = din("w_c_out", [NL, 512, D])
    w_o = din("w_o", [NL, D, D])
    w_up = din("w_up", [NL, D, 2 * D_FF]); w_down = din("w_down", [NL, D_FF, D])
    conv_w = din("conv_w", [NL, 3, 2 * D_FF]); conv_b = din("conv_b", [NL, 2 * D_FF])
    g_sgu = din("g_sgu", [NL, 512])
    w_spT = din("w_spT", [NL, 8, 128, 128])
    b_sp = din("b_sp", [NL, 8, 128])
    sink = din("sink", [NL, 8])
    ropec = din("ropec", [128, 32, 64]); ropes = din("ropes", [128, 32, 64])

    s5p = din("s5p", [NL, 2, 128, 1104])
    dskT = din("dskT", [NL, 128, 4])
    nst = dout("nst", [NL, 2, 4, 128, 32])
    y = dout("y", [NTOK, D])
    nkv = dout("nkv", [NL, 2, 1024, 128])

    wb_mod = dscr("wb_mod", [NL, D, 6 * D], BF16)
    wb_in = dscr("wb_in", [NL, D, D_IN], BF16)
    wb_glu = dscr("wb_glu", [NL, 512, 2048], BF16)
    wb_b = dscr("wb_b", [NL, 512, D], BF16); wb_c = dscr("wb_c", [NL, 512, D], BF16)
    wb_o = dscr("wb_o", [NL, D, D], BF16)
    wb_up = dscr("wb_up", [NL, D, 2 * D_FF], BF16); wb_down = dscr("wb_down", [NL, D_FF, D], BF16)
    modrows = dscr("modrows", [NL, 2, 6, D])
    yfs = dscr("yfs", [4, 128, NTOK])
    yas = dscr("yas", [4, 128, NTOK], BF16)
    x1 = dscr("x1", [NTOK, D])
    xmid = dscr("xmid", [NTOK, D])

    dbg = {}
    for ent in dbg_names:
        dbg[ent[0]] = dout("dbg_" + ent[0], ent[1], ent[2] if len(ent) > 2 else F32)

    with contextlib.ExitStack() as top:
        def sb(name, shape, dt=F32, stack=top):
            return stack.enter_context(nc.sbuf_tensor(name, list(shape), dt))

        def psum(name, shape, dt=F32, stack=top):
            return stack.enter_context(nc.psum_tensor(name, list(shape), dt))

        ident = sb("ident", [128, 128], BF16)
        ones_bf = sb("ones_bf", [128, 128], BF16)
        S.op("pool", lambda e: e.memset(ident[:], 0.0), writes=["ident"])
        S.op("pool", lambda e: e.affine_select(out=ident[:], in_=ident[:], pattern=[[-1, 128]],
                                               compare_op=ALU.not_equal, fill=1.0, base=0,
                                               channel_multiplier=1), reads=["ident"], writes=["ident"])
        S.op("pool", lambda e: e.memset(ones_bf[:], 1.0), writes=["ones_bf"])

        psT = [psum("psT%d" % i, [128, 8, 128], BF16) for i in range(2)]
        psF = [psum("psF%d" % i, [128, 512], F32) for i in range(6)]
        st_rr = {"T": 0, "F": 0, "W": 0}

        def next_psT():
            i = st_rr["T"]; st_rr["T"] = (i + 1) % 2
            return psT[i], "psT%d" % i

        def next_psF(wide=False):
            if wide:
                i = st_rr["W"]; st_rr["W"] = (i + 1) % 6
                return psF[i], "psF%d" % i
            i = st_rr["F"]; st_rr["F"] = (i + 1) % 4
            return psF[2 + i], "psF%d" % (2 + i)

        scr_keys = {}

        def SK(dst, l):
            return scr_keys[(id(dst), l)]

        def conv_w2(dst, src, rows, l):
            ks = scr_keys.setdefault((id(dst), l), [])
            for r0 in range(0, rows, 256):
                r1 = min(rows, r0 + 256)
                ks.append(("scr", id(dst), l, r0))
                S.dma("pool", lambda e, l=l, r0=r0, r1=r1: e.dma_start(out=dst[l, r0:r1, :], in_=src[l, r0:r1, :]),
                      writes=[("scr", id(dst), l, r0)], bg=True)
        for l in range(NL):
            conv_w2(wb_mod, w_mod, D, l)
        for l in range(NL):
            conv_w2(wb_in, w_in, D, l)
            conv_w2(wb_glu, w_glu, 512, l)
            conv_w2(wb_b, w_b_out, 512, l)
            conv_w2(wb_c, w_c_out, 512, l)
            conv_w2(wb_o, w_o, D, l)
            conv_w2(wb_up, w_up, D, l)
            conv_w2(wb_down, w_down, D_FF, l)

        WSLOT = 8 * 512
        NW = 4
        G = {}
        wr = {"i": 0}

        def alloc_wring(stack):
            G["wring"] = [sb(P.name("wring"), [128, WSLOT], BF16, stack) for i in range(NW)]

        def alloc_rope(stack):
            G["ropec_t"] = sb(P.name("ropec_t"), [128, 32, 64], F32, stack)
            G["ropes_t"] = sb(P.name("ropes_t"), [128, 32, 64], F32, stack)
            rc, rs = G["ropec_t"], G["ropes_t"]
            S.dma("sp", lambda e: e.dma_start(out=rc[:], in_=ropec), writes=["ropec"])
            S.dma("sp", lambda e: e.dma_start(out=rs[:], in_=ropes), writes=["ropes"])

        def load_w(src2d, kt, ncols, scr_key):
            i = wr["i"]; wr["i"] = (i + 1) % NW
            t = G["wring"][i]
            view = t[:, 0:kt * ncols].rearrange("p (k n) -> p k n", k=kt)
            S.dma("pool", lambda e: e.dma_start(out=view, in_=src2d.rearrange("(k p) n -> p k n", p=128)),
                  reads=list(scr_key), writes=["wring%d" % i])
            return view, "wring%d" % i

        def phase_mod(l):
            with contextlib.ExitStack() as ph:
                alloc_wring(ph)
                cT = sb(P.name("cT"), [128, 2, 8], F32, ph)
                scT = sb(P.name("scT"), [128, 2, 8], F32, ph)
                lrep = sb(P.name("lrep"), [128, 2, 8, 128], BF16, ph)
                mch = sb(P.name("mch"), [128, D], F32, ph)
                bch = sb(P.name("bch"), [128, D], F32, ph)
                gch = sb(P.name("gch"), [128, D], F32, ph)
                och = sb(P.name("och"), [128, D], F32, ph)
                k_c, k_sc, k_l, k_m, k_b, k_g, k_o = [P.name("k") for _ in range(7)]
                S.dma("sp", lambda e: e.dma_start(out=cT[:], in_=cvec.rearrange("r (k p) -> p r k", p=128),
                                                  allow_slow_non_contiguous=True), writes=[k_c])
                S.op("act", lambda e: e.activation(out=scT[:], in_=cT[:], func=AF.Silu), reads=[k_c], writes=[k_sc])
                for r in range(2):
                    for k in range(8):
                        S.op("dve", lambda e, r=r, k=k: e.tensor_scalar(out=lrep[:, r, k, :], in0=ones_bf[:],
                                                                          scalar1=scT[:, r, k:k + 1], scalar2=None,
                                                                          op0=ALU.mult),
                             reads=[k_sc, "ones_bf"], writes=[k_l])
                plan = [(1, 0, 0, True), (0, 1, None, False), (2, 2, 1, False), (4, 3, 2, True), (3, 4, None, False), (5, 5, 3, False)]
                for r in range(2):
                    for (mi, oi, gi, plus1) in plan:
                        S.dma("sp", lambda e, mi=mi: e.dma_start(out=bch[:], in_=b_mod[l:l + 1, mi * D:(mi + 1) * D].broadcast_to([128, D])), writes=[k_b])
                        if gi is not None:
                            S.dma("sp", lambda e, gi=gi: e.dma_start(out=gch[:], in_=gvec[l, gi:gi + 1, :].broadcast_to([128, D])), writes=[k_g])
                        for hh in range(2):
                            cb = mi * 2 + hh
                            wv, wk = load_w(wb_mod[l, :, cb * 512:(cb + 1) * 512], 8, 512, SK(wb_mod, l))
                            pt, pk = next_psF()
                            for k in range(8):
                                S.op("pe", lambda e, r=r, k=k, wv=wv, pt=pt: e.matmul(pt[:], lhsT=lrep[:, r, k, :], rhs=wv[:, k, :],
                                                                                       start=(k == 0), stop=(k == 7)),
                                     reads=[k_l, wk], writes=[pk])
                            S.op("dve", lambda e, hh=hh, pt=pt: e.tensor_tensor(out=mch[:, hh * 512:(hh + 1) * 512], in0=pt[:],
                                                                                in1=bch[:, hh * 512:(hh + 1) * 512], op=ALU.add),
                                 reads=[pk, k_b], writes=[k_m])
                        if gi is None:
                            S.op("dve", lambda e: e.tensor_copy(out=och[:], in_=mch[:]), reads=[k_m], writes=[k_o])
                        elif plus1:
                            S.op("dve", lambda e: e.scalar_tensor_tensor(out=och[:], in0=mch[:], scalar=1.0, in1=gch[:], op0=ALU.add, op1=ALU.mult),
                                 reads=[k_m, k_g], writes=[k_o])
                        else:
                            S.op("dve", lambda e: e.tensor_tensor(out=och[:], in0=mch[:], in1=gch[:], op=ALU.mult), reads=[k_m, k_g], writes=[k_o])
                        S.dma("sp", lambda e, r=r, oi=oi: e.dma_start(out=modrows[l, r, oi:oi + 1, :], in_=och[0:1, :]),
                              reads=[k_o], writes=["modrows"])

        xt_ring = [sb("xt%d" % i, [128, D], F32) for i in range(2)]
        junk = sb("junk", [128, D], BF16)
        tmpf = sb("tmpf", [128, D], F32)
        hb_ring = [sb("hb%d" % i, [128, D], BF16) for i in range(2)]
        ss_t = sb("ss_t", [128, 8], F32)
        rows = sb("rows", [128, 3, D], F32)
        hT = sb("hT", [128, 8, 512], BF16)
        rr = {"xt": 0, "hb": 0}
        cur = {"rows": None}

        def load_rows(l, path, half=0):
            if cur["rows"] == (l, path, half):
                return
            cur["rows"] = (l, path, half)
            S.dma("sp", lambda e: e.dma_start(out=rows[:], in_=modrows[l, path:path + 1, 3 * half:3 * half + 3, :].broadcast_to([128, 3, D])),
                  reads=["modrows"], writes=["rows"])

        def rms_rstd(src_ap, src_key, col):
            S.op("act", lambda e: e.activation(out=junk[:], in_=src_ap, func=AF.Square, accum_out=ss_t[:, col:col + 1]),
                 reads=[src_key], writes=["junk", ("ss", col)])
            S.op("dve", lambda e: e.tensor_scalar(out=ss_t[:, col:col + 1], in0=ss_t[:, col:col + 1], scalar1=1.0 / D,
                                                  scalar2=EPS, op0=ALU.mult, op1=ALU.add),
                 reads=[("ss", col)], writes=[("ss", col)])
            S.op("act", lambda e: e.activation(out=ss_t[:, col:col + 1], in_=ss_t[:, col:col + 1], func=AF.Sqrt),
                 reads=[("ss", col)], writes=[("ss", col)])
            S.op("dve", lambda e: e.reciprocal(out=ss_t[:, col:col + 1], in_=ss_t[:, col:col + 1]),
                 reads=[("ss", col)], writes=[("ss", col)])

        def norm_block(src, tok0, ia, ib, nchunks=4, dstT=None, dst_key="hT", col0=0):
            dstT = hT if dstT is None else dstT
            stt = {}

            def s1(c):
                i = rr["xt"]; rr["xt"] = 1 - i
                xt = xt_ring[i]; xk = "xt%d" % i
                t0 = tok0 + c * 128
                S.dma("sp", lambda e, xt=xt, t0=t0: e.dma_start(out=xt[:], in_=src[t0:t0 + 128, :]),
                      reads=[("dram", id(src))], writes=[xk])
                col = 4 + c % 2
                rms_rstd(xt[:], xk, col)
                stt[c] = (xt, xk, col)

            def s2(c):
                xt, xk, col = stt[c]
                S.op("dve", lambda e, xt=xt, col=col: e.scalar_tensor_tensor(out=tmpf[:], in0=xt[:], scalar=ss_t[:, col:col + 1], in1=rows[:, ia, :],
                                                                            op0=ALU.mult, op1=ALU.mult),
                     reads=[xk, ("ss", col), "rows"], writes=["tmpf"])
                j = rr["hb"]; rr["hb"] = 1 - j
                hb = hb_ring[j]; hk = "hb%d" % j
                S.op("dve", lambda e, hb=hb: e.tensor_tensor(out=hb[:], in0=tmpf[:], in1=rows[:, ib, :], op=ALU.add),
                     reads=["tmpf", "rows"], writes=[hk])
                pt, pk = next_psT()
                for k in range(8):
                    S.op("pe", lambda e, k=k, hb=hb, pt=pt: e.transpose(out=pt[:, k, :], in_=hb[:, k * 128:(k + 1) * 128], identity=ident[:]),
                         reads=[hk, "ident"], writes=[pk])
                cc = col0 + c * 128
                S.op("act", lambda e, pt=pt, cc=cc: e.activation(out=dstT[:, :, cc:cc + 128], in_=pt[:], func=AF.Copy),
                     reads=[pk], writes=[dst_key])

            s1(0)
            for c in range(nchunks):
                if c + 1 < nchunks:
                    s1(c + 1)
                s2(c)

        uT_all = sb("uT_all", [128, 4, NTOK], BF16)
        KT_all = sb("KT_all", [128, NTOK], BF16)
        V_all = sb("V_all", [128, 40, 130], BF16)
        cKT = sb("cKT", [128, 256], BF16)
        cV = sb("cV", [128, 2, 130], BF16)
        S.op("pool", lambda e: e.memset(V_all[:], 1.0), writes=["V_all"])
        S.op("pool", lambda e: e.memset(cV[:], 1.0), writes=["cV"])
        kvf = sb("kvf", [128, 256], F32)
        ropt1 = sb("ropt1", [128, 512], F32)
        ropt2 = sb("ropt2", [128, 512], F32)
        kb = sb("kb", [128, 128], BF16)

        def rope(src_ap, src_key, H, cs, out_ap, out_key):
            n = H * 64
            cosb = G["ropec_t"][:, cs:cs + 1, :].broadcast_to([128, H, 64])
            s3 = src_ap.rearrange("p (h e) -> p h e", e=64)
            t1 = ropt1[:, 0:n].rearrange("p (h e) -> p h e", e=64)
            S.op("dve", lambda e: e.tensor_tensor(out=t1, in0=s3, in1=cosb, op=ALU.mult),
                 reads=[src_key, "ropec"], writes=["ropt1"])
            s5 = src_ap.rearrange("p (h b f e) -> p h b f e", b=2, f=2, e=16)
            t5 = ropt2[:, 0:n].rearrange("p (h b f e) -> p h b f e", b=2, f=2, e=16)
            sn5 = G["ropes_t"][:, cs:cs + 1, :].broadcast_to([128, H, 64]).rearrange("p h (b f e) -> p h b f e", b=2, f=2, e=16)
            for f in range(2):
                S.op("pool", lambda e, f=f: e.tensor_tensor(out=t5[:, :, :, f, :], in0=s5[:, :, :, 1 - f, :], in1=sn5[:, :, :, f, :], op=ALU.mult),
                     reads=[src_key, "ropes"], writes=[("ropt2", f)])
            S.op("dve", lambda e: e.tensor_tensor(out=out_ap, in0=ropt1[:, 0:n], in1=ropt2[:, 0:n], op=ALU.add),
                 reads=["ropt1", ("ropt2", 0), ("ropt2", 1)], writes=[out_key])

        def phase_A(l):
            src = x0 if l == 0 else x1
            with contextlib.ExitStack() as ph:
                alloc_rope(ph)
                wA = sb(P.name("wA"), [128, 8, 768], BF16, ph)
                kA = P.name("kA")
                S.dma("sp", lambda e: e.dma_start(out=wA[:, :, 0:512], in_=wb_in[l, :, 0:512].rearrange("(k p) n -> p k n", p=128)),
                      reads=SK(wb_in, l), writes=[kA])
                S.dma("sp", lambda e: e.dma_start(out=wA[:, :, 512:768], in_=wb_in[l, :, OFF_K:OFF_K + 256].rearrange("(k p) n -> p k n", p=128)),
                      reads=SK(wb_in, l), writes=[kA])
                for kv in range(2):
                    for t in range(2):
                        S.dma("sp", lambda e, kv=kv, t=t: e.dma_start(out=kvf[:, 0:128], in_=ckv[l, kv, t * 128:(t + 1) * 128, :]), writes=["kvf"])
                        if kv == 0:
                            S.op("dve", lambda e: e.tensor_copy(out=kb[:], in_=kvf[:, 0:128]), reads=["kvf"], writes=["kb"])
                            pt, pk = next_psT()
                            S.op("pe", lambda e, pt=pt: e.transpose(out=pt[:, 0, :], in_=kb[:], identity=ident[:]), reads=["kb", "ident"], writes=[pk])
                            S.op("act", lambda e, pt=pt, t=t: e.activation(out=cKT[:, t * 128:(t + 1) * 128], in_=pt[:, 0, :], func=AF.Copy),
                                 reads=[pk], writes=["cKT"])
                        else:
                            S.op("dve", lambda e, t=t: e.tensor_copy(out=cV[:, t, :].rearrange("p (h e) -> p h e", e=65)[:, :, 0:64],
                                                                      in_=kvf[:, 0:128].rearrange("p (h e) -> p h e", e=64)),
                                 reads=["kvf"], writes=["cV"])
                for blk in range(NBLK):
                    path = 0 if blk < 8 else 1
                    load_rows(l, path)
                    tok0 = blk * 512
                    norm_block(src, tok0, 0, 1)
                    for m in range(4):
                        pt, pk = next_psF()
                        for k in range(8):
                            S.op("pe", lambda e, m=m, k=k, pt=pt: e.matmul(pt[:], lhsT=wA[:, k, m * 128:(m + 1) * 128], rhs=hT[:, k, :],
                                                                          start=(k == 0), stop=(k == 7)),
                                 reads=[kA, "hT"], writes=[pk])
                        S.op("act", lambda e, m=m, pt=pt, tok0=tok0: e.activation(out=uT_all[:, m, tok0:tok0 + 512], in_=pt[:], func=AF.Copy),
                             reads=[pk], writes=["uT_all"])
                    for c in range(4):
                        cg = blk * 4 + c
                        pt, pk = next_psF()
                        for k in range(8):
                            S.op("pe", lambda e, c=c, k=k, pt=pt: e.matmul(pt[:, 0:256], lhsT=hT[:, k, c * 128:(c + 1) * 128], rhs=wA[:, k, 512:768],
                                                                          start=(k == 0), stop=(k == 7)),
                                 reads=[kA, "hT"], writes=[pk])
                        S.op("act", lambda e, pt=pt: e.activation(out=kvf[:], in_=pt[:, 0:256], func=AF.Copy), reads=[pk], writes=["kvf"])
                        if path == 1:
                            pt0 = (cg - 32) * 128
                            for kv in range(2):
                                S.dma("sp", lambda e, kv=kv, pt0=pt0: e.dma_start(out=nkv[l, kv, pt0:pt0 + 128, :], in_=kvf[:, kv * 128:(kv + 1) * 128]),
                                      reads=["kvf"], writes=["nkv"])
                            S.op("dve", lambda e: e.tensor_copy(out=kb[:], in_=kvf[:, 0:128]), reads=["kvf"], writes=["kb"])
                        else:
                            rope(kvf[:, 0:128], "kvf", 2, cg, kb[:], "kb")
                        ptT, pkT = next_psT()
                        S.op("pe", lambda e, ptT=ptT: e.transpose(out=ptT[:, 0, :], in_=kb[:], identity=ident[:]), reads=["kb", "ident"], writes=[pkT])
                        S.op("act", lambda e, ptT=ptT, cg=cg: e.activation(out=KT_all[:, cg * 128:(cg + 1) * 128], in_=ptT[:, 0, :], func=AF.Copy),
                             reads=[pkT], writes=["KT_all"])
                        S.op("dve", lambda e, cg=cg: e.tensor_copy(out=V_all[:, cg, :].rearrange("p (h e) -> p h e", e=65)[:, :, 0:64],
                                                                    in_=kvf[:, 128:256].rearrange("p (h e) -> p h e", e=64)),
                             reads=["kvf"], writes=["V_all"])

        NCOL = TS + 2 + 4 * 258
        h2s = dscr("h2s", [8, 128, NCOL], BF16)

        def seq_col(cg):
            if cg < 32:
                return 1 + cg * 128
            p_ = (cg - 32) // 2
            return TS + 2 + p_ * 258 + 1 + ((cg - 32) % 2) * 128

        maskp = sb("maskp", [128, 128], BF16)
        maskn = sb("maskn", [128, 128], BF16)
        S.op("pool", lambda e: e.memset(maskp[:], 1.0), writes=["maskp"])
        S.op("pool", lambda e: e.affine_select(out=maskp[:], in_=maskp[:], pattern=[[-1, 128]], compare_op=ALU.is_ge, fill=0.0,
                                               base=0, channel_multiplier=1), reads=["maskp"], writes=["maskp"])
        S.op("pool", lambda e: e.memset(maskn[:], 1.0), writes=["maskn"])
        S.op("pool", lambda e: e.affine_select(out=maskn[:], in_=maskn[:], pattern=[[1, 128]], compare_op=ALU.is_ge, fill=0.0,
                                               base=0, channel_multiplier=-1), reads=["maskn"], writes=["maskn"])

        def phase_C(l):
            src = x0 if l == 0 else x1
            with contextlib.ExitStack() as ph:
                alloc_wring(ph)
                alloc_rope(ph)
                yaT = sb(P.name("yaT"), [128, 4, 512], BF16, ph)
                ybT = sb(P.name("ybT"), [128, 4, 512], BF16, ph)
                oT = sb(P.name("oT"), [128, 4, 512], BF16, ph)
                qT = sb(P.name("qT"), [128, 4, 128], BF16, ph)
                pT = [sb(P.name("pT"), [128, 512], BF16, ph) for _ in range(2)]
                fA = sb(P.name("fA"), [128, 512], F32, ph)
                fB = sb(P.name("fB"), [128, 512], F32, ph)
                fC = sb(P.name("fC"), [128, 512], F32, ph)
                bA = sb(P.name("bA"), [128, 512], BF16, ph)
                bB = sb(P.name("bB"), [128, 512], BF16, ph)
                macc = sb(P.name("macc"), [128, 8, 512], F32, ph)
                mgT = sb(P.name("mgT"), [128, 8, 512], BF16, ph)
                wsp = sb(P.name("wsp"), [128, 8, 128], BF16, ph)
                bspT = sb(P.name("bspT"), [128, 8], F32, ph)
                gsg = sb(P.name("gsg"), [128, 512], BF16, ph)
                snk = sb(P.name("snk"), [128, 8], F32, ph)
                st = sb(P.name("st"), [128, 16], F32, ph)
                rows2 = sb(P.name("rows2"), [128, 2, D], BF16, ph)
                kk = {n: P.name(n) for n in ("rows2", "yaT", "ybT", "oT", "qT", "pT0", "pT1", "fA", "fB", "fC", "bA", "bB", "macc", "mgT",
                                             "wsp", "wspf", "bspT", "gsg", "snk", "st")}
                S.dma("pool", lambda e: e.dma_start(out=wsp[:], in_=w_spT[l].rearrange("h s q -> s h q")), writes=[kk["wsp"]])
                S.dma("sp", lambda e: e.dma_start(out=bspT[:], in_=b_sp[l].rearrange("h q -> q h"), allow_slow_non_contiguous=True), writes=[kk["bspT"]])
                S.dma("pool", lambda e: e.dma_start(out=gsg[:], in_=g_sgu[l:l + 1, :].broadcast_to([128, 512])), writes=[kk["gsg"]])
                S.dma("sp", lambda e: e.dma_start(out=snk[:], in_=sink[l:l + 1, :].broadcast_to([128, 8])), writes=[kk["snk"]])
                S.op("act", lambda e: e.activation(out=snk[:], in_=snk[:], func=AF.Exp), reads=[kk["snk"]], writes=[kk["snk"]])
                if stage < 5:
                    S.op("pool", lambda e: e.memset(yaT[:], 0.0), writes=[kk["yaT"]])
                scr_in = SK(wb_in, l)

                for blk in range(NBLK):
                    path = 0 if blk < 8 else 1
                    load_rows(l, path, 0)
                    if blk in (0, 8):
                        S.dma("pool", lambda e, path=path: e.dma_start(out=rows2[:], in_=modrows[l, path:path + 1, 3:5, :].broadcast_to([128, 2, D])),
                              reads=["modrows"], writes=[kk["rows2"]])
                    tok0 = blk * 512
                    norm_block(src, tok0, 0, 1)
                    if stage >= 5:
                        for k4 in range(4):
                            S.dma("sp", lambda e, k4=k4, tok0=tok0: e.dma_start(out=yaT[:, k4, :], in_=yas[k4, :, tok0:tok0 + 512]), reads=["yas"], writes=[kk["yaT"]])
                    wu, wuk = load_w(wb_in[l, :, 512:1024], 8, 512, scr_in)
                    wv, wvk = load_w(wb_in[l, :, 1024:1536], 8, 512, scr_in)
                    uvp = {}

                    def sgu_proj(c, wu=wu, wv=wv, wuk=wuk, wvk=wvk):
                        pu, puk = next_psF(True)
                        pv, pvk = next_psF(True)
                        for k in range(8):
                            S.op("pe", lambda e, c=c, k=k, pv=pv, wv=wv: e.matmul(pv[:], lhsT=hT[:, k, c * 128:(c + 1) * 128], rhs=wv[:, k, :], start=(k == 0), stop=(k == 7)),
                                 reads=["hT", wvk], writes=[pvk])
                        for k in range(8):
                            S.op("pe", lambda e, c=c, k=k, pu=pu, wu=wu: e.matmul(pu[:], lhsT=hT[:, k, c * 128:(c + 1) * 128], rhs=wu[:, k, :], start=(k == 0), stop=(k == 7)),
                                 reads=["hT", wuk], writes=[puk])
                        uvp[c] = (pu, puk, pv, pvk)
                    sgu_proj(0)
                    for c in range(4):
                        if c + 1 < 4:
                            sgu_proj(c + 1)
                        pu, puk, pv, pvk = uvp[c]
                        S.op("act", lambda e, pv=pv: e.activation(out=fA[:], in_=pv[:], func=AF.Gelu, accum_out=st[:, 0:1]), reads=[pvk], writes=[kk["fA"], kk["st"]])
                        S.op("act", lambda e: e.activation(out=junk[:, 0:512], in_=fA[:], func=AF.Square, accum_out=st[:, 1:2]), reads=[kk["fA"]], writes=["junk", kk["st"]])
                        S.op("dve", lambda e: e.tensor_scalar(out=st[:, 2:3], in0=st[:, 0:1], scalar1=1.0 / 512, scalar2=None, op0=ALU.mult), reads=[kk["st"]], writes=[kk["st"]])
                        S.op("dve", lambda e: e.tensor_tensor(out=st[:, 3:4], in0=st[:, 2:3], in1=st[:, 2:3], op=ALU.mult), reads=[kk["st"]], writes=[kk["st"]])
                        S.op("dve", lambda e: e.scalar_tensor_tensor(out=st[:, 4:5], in0=st[:, 1:2], scalar=1.0 / 512, in1=st[:, 3:4], op0=ALU.mult, op1=ALU.subtract), reads=[kk["st"]], writes=[kk["st"]])
                        S.op("dve", lambda e: e.tensor_scalar(out=st[:, 4:5], in0=st[:, 4:5], scalar1=EPS, scalar2=None, op0=ALU.add), reads=[kk["st"]], writes=[kk["st"]])
                        S.op("act", lambda e: e.activation(out=st[:, 4:5], in_=st[:, 4:5], func=AF.Sqrt), reads=[kk["st"]], writes=[kk["st"]])
                        S.op("dve", lambda e: e.reciprocal(out=st[:, 5:6], in_=st[:, 4:5]), reads=[kk["st"]], writes=[kk["st"]])
                        S.op("dve", lambda e: e.scalar_tensor_tensor(out=st[:, 6:7], in0=st[:, 2:3], scalar=-1.0, in1=st[:, 5:6], op0=ALU.mult, op1=ALU.mult), reads=[kk["st"]], writes=[kk["st"]])
                        S.op("act", lambda e: e.activation(out=fB[:], in_=fA[:], func=AF.Identity, bias=st[:, 6:7], scale=st[:, 5:6]), reads=[kk["fA"], kk["st"]], writes=[kk["fB"]])
                        S.op("dve", lambda e: e.tensor_tensor(out=bA[:], in0=fB[:], in1=gsg[:], op=ALU.mult), reads=[kk["fB"], kk["gsg"]], writes=[kk["bA"]])
                        pm, pmk = next_psF(True)
                        for h in range(8):
                            S.op("pe", lambda e, h=h, pm=pm: e.matmul(pm[:, h * 64:(h + 1) * 64], lhsT=wsp[:, h, :], rhs=bA[:, h * 64:(h + 1) * 64], start=True, stop=True),
                                 reads=[kk["wsp"], kk["bA"]], writes=[pmk])
                        S.op("act", lambda e, pu=pu: e.activation(out=fC[:], in_=pu[:], func=AF.Gelu), reads=[puk], writes=[kk["fC"]])
                        S.op("dve", lambda e, pm=pm: e.tensor_tensor(out=fB[:].rearrange("p (h e) -> p h e", e=64), in0=pm[:].rearrange("p (h e) -> p h e", e=64),
                                                                     in1=bspT[:, :].unsqueeze(2).broadcast_to([128, 8, 64]), op=ALU.add),
                             reads=[pmk, kk["bspT"]], writes=[kk["fB"]])
                        S.op("dve", lambda e: e.tensor_tensor(out=bB[:], in0=fB[:], in1=fC[:], op=ALU.mult), reads=[kk["fB"], kk["fC"]], writes=[kk["bB"]])
                        ptT, pkT = next_psT()
                        for j in range(4):
                            S.op("pe", lambda e, j=j, ptT=ptT: e.transpose(out=ptT[:, j, :], in_=bB[:, j * 128:(j + 1) * 128], identity=ident[:]), reads=[kk["bB"], "ident"], writes=[pkT])
                        S.op("act", lambda e, ptT=ptT, c=c: e.activation(out=ybT[:, :, c * 128:(c + 1) * 128], in_=ptT[:, 0:4, :], func=AF.Copy), reads=[pkT], writes=[kk["ybT"]])
                    wq, wqk = load_w(wb_in[l, :, OFF_Q:OFF_Q + 512], 8, 512, scr_in)
                    qTs = [qT[:], junk[:, 0:512].rearrange("p (g t) -> p g t", g=4)]
                    qTk = [kk["qT"], "junk"]

                    def qprep_a(c, blk=blk, path=path, wq=wq, wqk=wqk):
                        cg = blk * 4 + c
                        pq, pqk = next_psF()
                        for k in range(8):
                            S.op("pe", lambda e, c=c, k=k, pq=pq, wq=wq: e.matmul(pq[:], lhsT=hT[:, k, c * 128:(c + 1) * 128], rhs=wq[:, k, :], start=(k == 0), stop=(k == 7)),
                                 reads=["hT", wqk], writes=[pqk])
                        S.op("act", lambda e, pq=pq: e.activation(out=fA[:], in_=pq[:], func=AF.Copy), reads=[pqk], writes=[kk["fA"]])
                        qperm = bA[:].rearrange("p (g k e) -> p k g e", g=4, k=2, e=64)
                        if path == 0:
                            rope(fA[:], kk["fA"], 8, cg, fB[:], kk["fB"])
                            S.op("dve", lambda e, qperm=qperm: e.tensor_copy(out=qperm, in_=fB[:].rearrange("p (k g e) -> p k g e", k=2, g=4, e=64)), reads=[kk["fB"]], writes=[kk["bA"]])
                        else:
                            S.op("dve", lambda e, qperm=qperm: e.tensor_copy(out=qperm, in_=fA[:].rearrange("p (k g e) -> p k g e", k=2, g=4, e=64)), reads=[kk["fA"]], writes=[kk["bA"]])

                    def qprep_b(c):
                        ptT, pkT = next_psT()
                        for j in range(4):
                            S.op("pe", lambda e, j=j, ptT=ptT: e.transpose(out=ptT[:, j, :], in_=bA[:, j * 128:(j + 1) * 128], identity=ident[:]), reads=[kk["bA"], "ident"], writes=[pkT])
                        qd = qTs[c % 2]
                        S.op("act", lambda e, ptT=ptT, qd=qd: e.activation(out=qd, in_=ptT[:, 0:4, :], func=AF.Copy), reads=[pkT], writes=[qTk[c % 2]])
                    qprep_a(0); qprep_b(0)
                    for c in range(4):
                        cg = blk * 4 + c
                        if c + 1 < 4:
                            qprep_a(c + 1)
                        qcur = qTs[c % 2]; qcurk = qTk[c % 2]
                        tiles = []
                        if path == 0:
                            if cg > 0:
                                tiles.append(("a", cg - 1, "p"))
                            tiles.append(("a", cg, None))
                            if cg < 31:
                                tiles.append(("a", cg + 1, "n"))
                            tiles.append(("c", 0, None)); tiles.append(("c", 1, None))
                        else:
                            base = (cg // 2) * 2
                            tiles.append(("a", base, None)); tiles.append(("a", base + 1, None))
                        for kvh in range(2):
                            po = psF[kvh]; pok = "psF%d" % kvh
                            def emit_scores(ti, kvh=kvh, tiles=tiles, qcur=qcur, qcurk=qcurk):
                                kind, idx, msk = tiles[ti]
                                psc, psck = next_psF()
                                if kind == "a":
                                    kt_ap = KT_all[64 * kvh:64 * kvh + 64, idx * 128:(idx + 1) * 128]; ktk = "KT_all"
                                else:
                                    kt_ap = cKT[64 * kvh:64 * kvh + 64, idx * 128:(idx + 1) * 128]; ktk = "cKT"
                                S.op("pe", lambda e, psc=psc, kt_ap=kt_ap, kvh=kvh, qcur=qcur: e.matmul(psc[:], lhsT=kt_ap, rhs=qcur[64 * kvh:64 * kvh + 64, :, :], start=True, stop=True),
                                     reads=[ktk, qcurk], writes=[psck])
                                return psc, psck
                            nxt = emit_scores(0)
                            for ti, (kind, idx, msk) in enumerate(tiles):
                                psc, psck = nxt
                                if ti + 1 < len(tiles):
                                    nxt = emit_scores(ti + 1)
                                if kind == "a":
                                    v_ap = V_all[:, idx, kvh * 65:(kvh + 1) * 65]; vk = "V_all"
                                else:
                                    v_ap = cV[:, idx, kvh * 65:(kvh + 1) * 65]; vk = "cV"
                                pi = ti % 2
                                pt_ = pT[pi]; ptk = kk["pT%d" % pi]
                                S.op("act", lambda e, psc=psc, pt_=pt_: e.activation(out=pt_[:], in_=psc[:], func=AF.Exp, scale=0.125), reads=[psck], writes=[ptk])
                                if msk is not None:
                                    mk = maskp if msk == "p" else maskn
                                    S.op("pool", lambda e, pt_=pt_, mk=mk: e.tensor_tensor(out=pt_[:].rearrange("p (g q) -> p g q", g=4), in0=pt_[:].rearrange("p (g q) -> p g q", g=4),
                                                                                            in1=mk[:, :].unsqueeze(1).broadcast_to([128, 4, 128]), op=ALU.mult),
                                         reads=[ptk, "maskp", "maskn"], writes=[ptk])
                                for g in range(4):
                                    S.op("pe", lambda e, g=g, po=po, pt_=pt_, v_ap=v_ap, ti=ti, nt=len(tiles): e.matmul(po[:, g * 65:(g + 1) * 65], lhsT=pt_[:, g * 128:(g + 1) * 128], rhs=v_ap,
                                                                                                     start=(ti == 0 and g == 0), stop=(ti == nt - 1 and g == 3), skip_group_check=True),
                                         reads=[ptk, vk], writes=[pok])
                            po3 = po[:, 0:260].rearrange("p (g e) -> p g e", e=65)
                            S.op("dve", lambda e, po3=po3, kvh=kvh: e.tensor_tensor(out=st[:, 8:12], in0=po3[:, :, 64], in1=snk[:, kvh * 4:(kvh + 1) * 4], op=ALU.add),
                                 reads=[pok, kk["snk"]], writes=[kk["st"]])
                            S.op("dve", lambda e: e.reciprocal(out=st[:, 12:16], in_=st[:, 8:12]), reads=[kk["st"]], writes=[kk["st"]])
                            S.op("dve", lambda e, po3=po3, kvh=kvh: e.tensor_tensor(out=bB[:, kvh * 256:(kvh + 1) * 256].rearrange("p (g e) -> p g e", e=64), in0=po3[:, :, 0:64],
                                                                                   in1=st[:, 12:16].unsqueeze(2).broadcast_to([128, 4, 64]), op=ALU.mult),
                                 reads=[pok, kk["st"]], writes=[kk["bB"]])
                        ptT, pkT = next_psT()
                        for j in range(4):
                            S.op("pe", lambda e, j=j, ptT=ptT: e.transpose(out=ptT[:, j, :], in_=bB[:, j * 128:(j + 1) * 128], identity=ident[:]), reads=[kk["bB"], "ident"], writes=[pkT])
                        S.op("act", lambda e, ptT=ptT, c=c: e.activation(out=oT[:, :, c * 128:(c + 1) * 128], in_=ptT[:, 0:4, :], func=AF.Copy), reads=[pkT], writes=[kk["oT"]])
                        if c + 1 < 4:
                            qprep_b(c + 1)
                    if l == 0 and blk == 0:
                        dump("ybT", ybT[:], kk["ybT"]); dump("oT", oT[:], kk["oT"])
                    if l == 0 and blk == 8:
                        dump("ybTp", ybT[:], kk["ybT"]); dump("oTp", oT[:], kk["oT"])
                    for jj in range(2):
                        wz2, wz2k = load_w(wb_glu[l, :, 1024 + jj * 512:1024 + (jj + 1) * 512], 4, 512, SK(wb_glu, l))
                        wz1, wz1k = load_w(wb_glu[l, :, jj * 512:(jj + 1) * 512], 4, 512, SK(wb_glu, l))
                        wg, wgk = load_w(wb_in[l, :, OFF_G + jj * 512:OFF_G + (jj + 1) * 512], 8, 512, scr_in)
                        for j in range(4):
                            jt = jj * 4 + j
                            p2, p2k = next_psF(True); p1, p1k = next_psF(True); pg, pgk = next_psF(True)
                            for k in range(4):
                                S.op("pe", lambda e, j=j, k=k, p2=p2, wz2=wz2, tok0=tok0: e.matmul(p2[:], lhsT=wz2[:, k, j * 128:(j + 1) * 128], rhs=yaT[:, k, :], start=(k == 0), stop=(k == 3)), reads=[wz2k, kk["yaT"]], writes=[p2k])
                            for k in range(4):
                                S.op("pe", lambda e, j=j, k=k, p1=p1, wz1=wz1, tok0=tok0: e.matmul(p1[:], lhsT=wz1[:, k, j * 128:(j + 1) * 128], rhs=yaT[:, k, :], start=(k == 0), stop=(k == 3)), reads=[wz1k, kk["yaT"]], writes=[p1k])
                            for k in range(8):
                                S.op("pe", lambda e, j=j, k=k, pg=pg, wg=wg: e.matmul(pg[:], lhsT=wg[:, k, j * 128:(j + 1) * 128], rhs=hT[:, k, :], start=(k == 0), stop=(k == 7)), reads=[wgk, "hT"], writes=[pgk])
                            S.op("act", lambda e, p2=p2: e.activation(out=fA[:], in_=p2[:], func=AF.Sigmoid), reads=[p2k], writes=[kk["fA"]])
                            S.op("act", lambda e, pg=pg: e.activation(out=fB[:], in_=pg[:], func=AF.Sigmoid), reads=[pgk], writes=[kk["fB"]])
                            S.op("dve", lambda e, p1=p1: e.tensor_tensor(out=fA[:], in0=p1[:], in1=fA[:], op=ALU.mult), reads=[p1k, kk["fA"]], writes=[kk["fA"]])
                            S.op("dve", lambda e, jt=jt: e.tensor_tensor(out=macc[:, jt, :], in0=fA[:], in1=fB[:], op=ALU.mult), reads=[kk["fA"], kk["fB"]], writes=[kk["macc"]])
                        for bi, (wsrc, actT, akey) in enumerate(((wb_b, ybT, kk["ybT"]), (wb_c, oT, kk["oT"]))):
                            wy, wyk = load_w(wsrc[l, :, jj * 512:(jj + 1) * 512], 4, 512, SK(wsrc, l))
                            g0 = OFF_G + (bi + 1) * 1024 + jj * 512
                            wg, wgk = load_w(wb_in[l, :, g0:g0 + 512], 8, 512, scr_in)
                            for j in range(4):
                                jt = jj * 4 + j
                                py, pyk = next_psF(True); pg, pgk = next_psF(True)
                                for k in range(4):
                                    S.op("pe", lambda e, j=j, k=k, py=py, wy=wy, actT=actT: e.matmul(py[:], lhsT=wy[:, k, j * 128:(j + 1) * 128], rhs=actT[:, k, :], start=(k == 0), stop=(k == 3)), reads=[wyk, akey], writes=[pyk])
                                for k in range(8):
                                    S.op("pe", lambda e, j=j, k=k, pg=pg, wg=wg: e.matmul(pg[:], lhsT=wg[:, k, j * 128:(j + 1) * 128], rhs=hT[:, k, :], start=(k == 0), stop=(k == 7)), reads=[wgk, "hT"], writes=[pgk])
                                S.op("act", lambda e, pg=pg: e.activation(out=fB[:], in_=pg[:], func=AF.Sigmoid), reads=[pgk], writes=[kk["fB"]])
                                S.op("dve", lambda e, py=py: e.tensor_tensor(out=fA[:], in0=py[:], in1=fB[:], op=ALU.mult), reads=[pyk, kk["fB"]], writes=[kk["fA"]])
                                if bi == 0:
                                    S.op("dve", lambda e, jt=jt: e.tensor_tensor(out=macc[:, jt, :], in0=macc[:, jt, :], in1=fA[:], op=ALU.add), reads=[kk["fA"], kk["macc"]], writes=[kk["macc"]])
                                else:
                                    S.op("dve", lambda e, jt=jt: e.tensor_tensor(out=mgT[:, jt, :], in0=macc[:, jt, :], in1=fA[:], op=ALU.add), reads=[kk["fA"], kk["macc"]], writes=[kk["mgT"]])
                    if l == 0 and blk == 0:
                        dump("mgT", mgT[:], kk["mgT"])
                    wo0, wo0k = load_w(wb_o[l, :, 0:512], 8, 512, SK(wb_o, l))
                    wo1, wo1k = load_w(wb_o[l, :, 512:1024], 8, 512, SK(wb_o, l))
                    wo_ps = {}

                    def emit_wo(c):
                        lst = []
                        for hh, (wo, wok) in enumerate(((wo0, wo0k), (wo1, wo1k))):
                            pm_, pmk_ = next_psF(True)
                            for k in range(8):
                                S.op("pe", lambda e, c=c, k=k, pm_=pm_, wo=wo: e.matmul(pm_[:], lhsT=mgT[:, k, c * 128:(c + 1) * 128], rhs=wo[:, k, :], start=(k == 0), stop=(k == 7)), reads=[wok, kk["mgT"]], writes=[pmk_])
                            lst.append((pm_, pmk_))
                        wo_ps[c] = lst
                    emit_wo(0)
                    xpre = {}

                    def load_x(c, tok0=tok0):
                        i = rr["xt"]; rr["xt"] = 1 - i
                        xt = xt_ring[i]; xk = "xt%d" % i
                        t0 = tok0 + c * 128
                        S.dma("sp", lambda e, xt=xt, t0=t0: e.dma_start(out=xt[:], in_=src[t0:t0 + 128, :]), reads=[("dram", id(src))], writes=[xk])
                        xpre[c] = (xt, xk)
                    load_x(0)
                    for c in range(4):
                        t0 = tok0 + c * 128
                        if c + 1 < 4:
                            emit_wo(c + 1)
                            load_x(c + 1)
                        xt, xk = xpre[c]
                        (pm0, pmk0), (pm1, pmk1) = wo_ps[c]
                        S.op("act", lambda e, pm0=pm0: e.activation(out=junk[:, 0:512], in_=pm0[:], func=AF.Square, accum_out=ss_t[:, 1:2]), reads=[pmk0], writes=["junk", ("ss", 1)])
                        S.op("act", lambda e, pm1=pm1: e.activation(out=junk[:, 512:1024], in_=pm1[:], func=AF.Square, accum_out=ss_t[:, 6:7]), reads=[pmk1], writes=["junk", ("ss", 6)])
                        S.op("dve", lambda e: e.tensor_tensor(out=ss_t[:, 1:2], in0=ss_t[:, 1:2], in1=ss_t[:, 6:7], op=ALU.add), reads=[("ss", 1), ("ss", 6)], writes=[("ss", 1)])
                        S.op("dve", lambda e: e.tensor_scalar(out=ss_t[:, 1:2], in0=ss_t[:, 1:2], scalar1=1.0 / D, scalar2=EPS, op0=ALU.mult, op1=ALU.add), reads=[("ss", 1)], writes=[("ss", 1)])
                        S.op("act", lambda e: e.activation(out=ss_t[:, 1:2], in_=ss_t[:, 1:2], func=AF.Sqrt), reads=[("ss", 1)], writes=[("ss", 1)])
                        S.op("dve", lambda e: e.reciprocal(out=ss_t[:, 1:2], in_=ss_t[:, 1:2]), reads=[("ss", 1)], writes=[("ss", 1)])
                        for hh, (pm_, pmk_) in enumerate(((pm0, pmk0), (pm1, pmk1))):
                            S.op("dve", lambda e, pm_=pm_, hh=hh: e.scalar_tensor_tensor(out=tmpf[:, hh * 512:(hh + 1) * 512], in0=pm_[:], scalar=ss_t[:, 1:2], in1=rows[:, 2, hh * 512:(hh + 1) * 512], op0=ALU.mult, op1=ALU.mult),
                                 reads=[pmk_, ("ss", 1), "rows"], writes=["tmpf"])
                        S.op("dve", lambda e, xt=xt: e.tensor_tensor(out=xt[:], in0=xt[:], in1=tmpf[:], op=ALU.add), reads=["tmpf", xk], writes=[xk])
                        S.dma("sp", lambda e, xt=xt, t0=t0: e.dma_start(out=xmid[t0:t0 + 128, :], in_=xt[:]), reads=[xk], writes=[("dram", id(xmid))])
                        rms_rstd(xt[:], xk, 2)
                        S.op("dve", lambda e, xt=xt: e.scalar_tensor_tensor(out=tmpf[:], in0=xt[:], scalar=ss_t[:, 2:3], in1=rows2[:, 0, :], op0=ALU.mult, op1=ALU.mult),
                             reads=[xk, ("ss", 2), kk["rows2"]], writes=["tmpf"])
                        j2 = rr["hb"]; rr["hb"] = 1 - j2
                        hb2 = hb_ring[j2]; hk2 = "hb%d" % j2
                        S.op("dve", lambda e, hb2=hb2: e.tensor_tensor(out=hb2[:], in0=tmpf[:], in1=rows2[:, 1, :], op=ALU.add), reads=["tmpf", kk["rows2"]], writes=[hk2])
                        ptT, pkT = next_psT()
                        for k in range(8):
                            S.op("pe", lambda e, k=k, hb2=hb2, ptT=ptT: e.transpose(out=ptT[:, k, :], in_=hb2[:, k * 128:(k + 1) * 128], identity=ident[:]), reads=[hk2, "ident"], writes=[pkT])
                        S.op("act", lambda e, ptT=ptT, c=c: e.activation(out=hT[:, :, c * 128:(c + 1) * 128], in_=ptT[:], func=AF.Copy), reads=[pkT], writes=["hT"])
                        col = seq_col(blk * 4 + c)
                        S.dma("sp", lambda e, c=c, col=col: e.dma_start(out=h2s[:, :, col:col + 128].rearrange("k p t -> p k t"), in_=hT[:, :, c * 128:(c + 1) * 128]),
                              reads=["hT"], writes=["h2s"])

        def phase_D(l):
            dst = x1 if l == 0 else y
            with contextlib.ExitStack() as ph:
                alloc_wring(ph)
                wd = sb(P.name("wd"), [128, 22, 1024], BF16, ph)
                h2w = sb(P.name("h2w"), [128, 8, 4, 130], BF16, ph)
                cpar = sb(P.name("cpar"), [128, 4, 44], F32, ph)
                zt = sb(P.name("zt"), [128, 130], BF16, ph)
                g1 = sb(P.name("g1"), [128, 4, 128], F32, ph)
                g2 = sb(P.name("g2"), [128, 4, 128], F32, ph)
                gvT = uT_all[:, 0:3, :].rearrange("p a t -> p (a t)")[:, 0:22 * 512].rearrange("p (f t) -> p f t", f=22)
                kk = {n: P.name(n) for n in ("wd", "h2w", "cpar", "zt", "g1", "g2")}
                S.dma("sp", lambda e: e.dma_start(out=wd[:], in_=wb_down[l].rearrange("(k p) n -> p k n", p=128)), reads=SK(wb_down, l), writes=[kk["wd"]])
                for i in range(3):
                    S.dma("sp", lambda e, i=i: e.dma_start(out=cpar[:, i, :], in_=conv_w[l, i, :].rearrange("(f p) -> p f", p=128), allow_slow_non_contiguous=True), writes=[kk["cpar"]])
                S.dma("sp", lambda e: e.dma_start(out=cpar[:, 3, :], in_=conv_b[l, :].rearrange("(f p) -> p f", p=128), allow_slow_non_contiguous=True), writes=[kk["cpar"]])
                S.op("pool", lambda e: e.memset(zt[:], 0.0), writes=[kk["zt"]])
                pads = [0, TS + 1] + [TS + 2 + i * 258 for i in range(4)] + [TS + 2 + i * 258 + 257 for i in range(4)]
                for pc in pads:
                    S.dma("sp", lambda e, pc=pc: e.dma_start(out=h2s[:, :, pc:pc + 1].rearrange("k p o -> p k o"), in_=zt[:, 0:8].unsqueeze(2), allow_slow_non_contiguous=True),
                          reads=[kk["zt"]], writes=["h2s"])
                for blk in range(NBLK):
                    path = 0 if blk < 8 else 1
                    load_rows(l, path, 1)
                    for c in range(4):
                        col = seq_col(blk * 4 + c) - 1
                        S.dma("sp", lambda e, c=c, col=col: e.dma_start(out=h2w[:, :, c, :], in_=h2s[:, :, col:col + 130].rearrange("k p t -> p k t")),
                              reads=["h2s"], writes=[kk["h2w"]])
                    for ft in range(22):
                        wgt, wgk = load_w(wb_up[l, :, ft * 128:(ft + 1) * 128], 8, 128, SK(wb_up, l))
                        wvt, wvk = load_w(wb_up[l, :, D_FF + ft * 128:D_FF + (ft + 1) * 128], 8, 128, SK(wb_up, l))
                        for (wt, wk, fi, gt) in ((wgt, wgk, ft, g1), (wvt, wvk, 22 + ft, g2)):
                            pz, pzk = next_psF(True)
                            for c in range(4):
                                for k in range(8):
                                    S.op("pe", lambda e, c=c, k=k, pz=pz, wt=wt: e.matmul(pz[:, c * 128:c * 128 + 128], lhsT=wt[:, k, :], rhs=h2w[:, k, c, 1:129], start=(k == 0 and c == 0), stop=(k == 7 and c == 3), skip_group_check=True),
                                         reads=[wk, kk["h2w"]], writes=[pzk])
                            ph_, phk = next_psF(True)
                            for k in range(8):
                                S.op("pe", lambda e, k=k, ph_=ph_, wt=wt: e.matmul(ph_[:, 0:8].rearrange("p (c o) -> p c o", o=2), lhsT=wt[:, k, :], rhs=h2w[:, k, :, 0:130:129], start=(k == 0), stop=(k == 7)),
                                     reads=[wk, kk["h2w"]], writes=[phk])
                            gt_k = kk["g1"] if gt is g1 else kk["g2"]
                            S.op("act", lambda e, pz=pz, fi=fi, gt=gt: e.activation(out=gt[:].rearrange("p c t -> p (c t)"), in_=pz[:], func=AF.Identity, bias=cpar[:, 3, fi:fi + 1], scale=cpar[:, 1, fi:fi + 1]),
                                 reads=[pzk, kk["cpar"]], writes=[gt_k])
                            pz3 = pz[:].rearrange("p (c t) -> p c t", t=128)
                            ph3 = ph_[:, 0:8].rearrange("p (c o) -> p c o", o=2)
                            S.op("dve", lambda e, pz3=pz3, fi=fi, gt=gt: e.scalar_tensor_tensor(out=gt[:, :, 1:128], in0=pz3[:, :, 0:127], scalar=cpar[:, 0, fi:fi + 1], in1=gt[:, :, 1:128], op0=ALU.mult, op1=ALU.add),
                                 reads=[pzk, kk["cpar"], gt_k], writes=[gt_k])
                            S.op("dve", lambda e, pz3=pz3, fi=fi, gt=gt: e.scalar_tensor_tensor(out=gt[:, :, 0:127], in0=pz3[:, :, 1:128], scalar=cpar[:, 2, fi:fi + 1], in1=gt[:, :, 0:127], op0=ALU.mult, op1=ALU.add),
                                 reads=[pzk, kk["cpar"], gt_k], writes=[gt_k])
                            S.op("dve", lambda e, ph3=ph3, fi=fi, gt=gt: e.scalar_tensor_tensor(out=gt[:, :, 0:1], in0=ph3[:, :, 0:1], scalar=cpar[:, 0, fi:fi + 1], in1=gt[:, :, 0:1], op0=ALU.mult, op1=ALU.add),
                                 reads=[phk, kk["cpar"], gt_k], writes=[gt_k])
                            S.op("dve", lambda e, ph3=ph3, fi=fi, gt=gt: e.scalar_tensor_tensor(out=gt[:, :, 127:128], in0=ph3[:, :, 1:2], scalar=cpar[:, 2, fi:fi + 1], in1=gt[:, :, 127:128], op0=ALU.mult, op1=ALU.add),
                                 reads=[phk, kk["cpar"], gt_k], writes=[gt_k])
                        S.op("act", lambda e: e.activation(out=g1[:], in_=g1[:], func=AF.Silu), reads=[kk["g1"]], writes=[kk["g1"]])
                        S.op("dve", lambda e, ft=ft: e.tensor_tensor(out=gvT[:, ft, :], in0=g1[:].rearrange("p c t -> p (c t)"), in1=g2[:].rearrange("p c t -> p (c t)"), op=ALU.mult),
                             reads=[kk["g1"], kk["g2"]], writes=["uT_all"])
                    for c in range(4):
                        t0 = blk * 512 + c * 128
                        for hh in range(2):
                            pm_, pmk_ = next_psF(True)
                            for k in range(22):
                                S.op("pe", lambda e, c=c, k=k, pm_=pm_, hh=hh: e.matmul(pm_[:], lhsT=gvT[:, k, c * 128:(c + 1) * 128], rhs=wd[:, k, hh * 512:(hh + 1) * 512], start=(k == 0), stop=(k == 21)),
                                     reads=[kk["wd"], "uT_all"], writes=[pmk_])
                            S.op("act", lambda e, pm_=pm_, hh=hh: e.activation(out=tmpf[:, hh * 512:(hh + 1) * 512], in_=pm_[:], func=AF.Copy), reads=[pmk_], writes=["tmpf"])
                        rms_rstd(tmpf[:], "tmpf", 1)
                        i = rr["xt"]; rr["xt"] = 1 - i
                        xt = xt_ring[i]; xk = "xt%d" % i
                        S.dma("sp", lambda e, xt=xt, t0=t0: e.dma_start(out=xt[:], in_=xmid[t0:t0 + 128, :]), reads=[("dram", id(xmid))], writes=[xk])
                        S.op("dve", lambda e: e.scalar_tensor_tensor(out=tmpf[:], in0=tmpf[:], scalar=ss_t[:, 1:2], in1=rows[:, 2, :], op0=ALU.mult, op1=ALU.mult),
                             reads=["tmpf", ("ss", 1), "rows"], writes=["tmpf"])
                        S.op("dve", lambda e, xt=xt: e.tensor_tensor(out=xt[:], in0=xt[:], in1=tmpf[:], op=ALU.add), reads=["tmpf", xk], writes=[xk])
                        S.dma("sp", lambda e, xt=xt, t0=t0: e.dma_start(out=dst[t0:t0 + 128, :], in_=xt[:]), reads=[xk], writes=[("dram", id(dst))])

        PI = math.pi

        SC = 64
        NSC = NTOK // SC

        def phase_B(l):
            with contextlib.ExitStack() as ph:
                yfb = sb(P.name("yfb"), [128, 64], F32, ph)
                yab = sb(P.name("yab"), [128, 64], BF16, ph)
                dsk = sb(P.name("dsk"), [128, 4], F32, ph)
                S.dma("sp", lambda e: e.dma_start(out=dsk[:], in_=dskT[l]), writes=["dsk"])
                ctx = []
                for d in range(2):
                    eng = "dve" if d == 0 else "pool"
                    pp = sb(P.name("pp"), [128, 1104], F32, ph)
                    wk = sb(P.name("wk"), [128, 12, 16], F32, ph)
                    Bx = sb(P.name("Bx"), [128, 2, 16, 16], F32, ph)
                    tb = sb(P.name("tb"), [128, 16, 16], F32, ph)
                    AB = sb(P.name("AB"), [128, 2, 32], F32, ph)
                    Mt = sb(P.name("Mt"), [128, 2, 4, 128], BF16, ph)
                    Bcm = sb(P.name("Bcm"), [128, 2, 4, 128], BF16, ph)
                    Cbd = sb(P.name("Cbd"), [128, 2, 16, 128], BF16, ph)
                    Hc = sb(P.name("Hc"), [128, SC + 1, 48], F32, ph)
                    xs = sb(P.name("xs"), [128, SC, 32], F32, ph)
                    Hb = sb(P.name("Hb"), [128, 32, SC], BF16, ph)
                    t1 = sb(P.name("t1"), [128, 32], F32, ph)
                    t2 = sb(P.name("t2"), [128, 32], F32, ph)
                    ya = sb(P.name("ya"), [128, SC], F32, ph)
                    K = {n: P.name(n) for n in ("pp", "wk", "Bx", "tb", "AB", "Mt", "Bcm", "Cbd", "Hc", "xs", "Hb", "t1", "t2", "ya")}
                    S.dma("sp", lambda e, pp=pp, d=d: e.dma_start(out=pp[:], in_=s5p[l, d]), writes=[K["pp"]])
                    lre, lim, lst = pp[:, 0:16], pp[:, 16:32], pp[:, 32:48]
                    bre = pp[:, 48:304].rearrange("p (g e) -> p g e", e=16); bim = pp[:, 304:560].rearrange("p (g e) -> p g e", e=16)
                    cre = pp[:, 560:816].rearrange("p (g e) -> p g e", e=16); cim = pp[:, 816:1072].rearrange("p (g e) -> p g e", e=16)
                    W = lambda i, wk=wk: wk[:, i, :]

                    def dv(fn, pp=pp, K=K):
                        S.op("dve", fn, reads=[K["pp"], K["wk"]], writes=[K["wk"]])

                    def ac(fn, K=K):
                        S.op("act", fn, reads=[K["wk"], K["pp"]], writes=[K["wk"]])
                    ac(lambda e, W=W, lst=lst: e.activation(out=W(0), in_=lst, func=AF.Exp))
                    dv(lambda e, W=W, lre=lre: e.tensor_tensor(out=W(1), in0=lre, in1=W(0), op=ALU.mult))
                    dv(lambda e, W=W, lim=lim: e.tensor_tensor(out=W(2), in0=lim, in1=W(0), op=ALU.mult))
                    ac(lambda e, W=W: e.activation(out=W(3), in_=W(1), func=AF.Exp))
                    ac(lambda e, W=W: e.activation(out=W(4), in_=W(2), func=AF.Sin, scale=1.0 / 16))
                    ac(lambda e, W=W: e.activation(out=W(5), in_=W(2), func=AF.Sin, scale=1.0 / 32))
                    dv(lambda e, W=W: e.tensor_tensor(out=W(5), in0=W(5), in1=W(5), op=ALU.mult))
                    dv(lambda e, W=W: e.tensor_scalar(out=W(5), in0=W(5), scalar1=-2.0, scalar2=1.0, op0=ALU.mult, op1=ALU.add))
                    for _ in range(4):
                        dv(lambda e, W=W: e.tensor_tensor(out=W(6), in0=W(4), in1=W(5), op=ALU.mult))
                        dv(lambda e, W=W: e.tensor_tensor(out=W(7), in0=W(4), in1=W(4), op=ALU.mult))
                        dv(lambda e, W=W: e.tensor_tensor(out=W(5), in0=W(5), in1=W(5), op=ALU.mult))
                        dv(lambda e, W=W: e.tensor_tensor(out=W(5), in0=W(5), in1=W(7), op=ALU.subtract))
                        dv(lambda e, W=W: e.tensor_scalar(out=W(4), in0=W(6), scalar1=2.0, scalar2=None, op0=ALU.mult))
                    dv(lambda e, W=W: e.tensor_tensor(out=W(6), in0=W(3), in1=W(5), op=ALU.mult))
                    dv(lambda e, W=W: e.tensor_tensor(out=W(7), in0=W(3), in1=W(4), op=ALU.mult))
                    dv(lambda e, W=W, lre=lre: e.tensor_tensor(out=W(8), in0=lre, in1=lre, op=ALU.mult))
                    dv(lambda e, W=W, lim=lim: e.tensor_tensor(out=W(9), in0=lim, in1=lim, op=ALU.mult))
                    dv(lambda e, W=W: e.tensor_tensor(out=W(8), in0=W(8), in1=W(9), op=ALU.add))
                    dv(lambda e, W=W: e.reciprocal(out=W(8), in_=W(8)))
                    dv(lambda e, W=W: e.tensor_scalar(out=W(9), in0=W(6), scalar1=-1.0, scalar2=None, op0=ALU.add))
                    dv(lambda e, W=W, lre=lre: e.tensor_tensor(out=W(10), in0=W(9), in1=lre, op=ALU.mult))
                    dv(lambda e, W=W, lim=lim: e.tensor_tensor(out=W(11), in0=W(7), in1=lim, op=ALU.mult))
                    dv(lambda e, W=W: e.tensor_tensor(out=W(10), in0=W(10), in1=W(11), op=ALU.add))
                    dv(lambda e, W=W: e.tensor_tensor(out=W(10), in0=W(10), in1=W(8), op=ALU.mult))
                    dv(lambda e, W=W, lre=lre: e.tensor_tensor(out=W(11), in0=W(7), in1=lre, op=ALU.mult))
                    dv(lambda e, W=W, lim=lim: e.tensor_tensor(out=W(9), in0=W(9), in1=lim, op=ALU.mult))
                    dv(lambda e, W=W: e.tensor_tensor(out=W(11), in0=W(11), in1=W(9), op=ALU.subtract))
                    dv(lambda e, W=W: e.tensor_tensor(out=W(11), in0=W(11), in1=W(8), op=ALU.mult))
                    fre_b = wk[:, 10, :].unsqueeze(2).broadcast_to([128, 16, 16]); fim_b = wk[:, 11, :].unsqueeze(2).broadcast_to([128, 16, 16])
                    S.op("dve", lambda e, Bx=Bx, bre=bre, fre_b=fre_b: e.tensor_tensor(out=Bx[:, 0], in0=bre, in1=fre_b, op=ALU.mult), reads=[K["pp"], K["wk"]], writes=[K["Bx"]])
                    S.op("dve", lambda e, tb=tb, bim=bim, fim_b=fim_b: e.tensor_tensor(out=tb[:], in0=bim, in1=fim_b, op=ALU.mult), reads=[K["pp"], K["wk"]], writes=[K["tb"]])
                    S.op("dve", lambda e, Bx=Bx, tb=tb: e.tensor_tensor(out=Bx[:, 0], in0=Bx[:, 0], in1=tb[:], op=ALU.subtract), reads=[K["Bx"], K["tb"]], writes=[K["Bx"]])
                    S.op("dve", lambda e, Bx=Bx, bim=bim, fre_b=fre_b: e.tensor_tensor(out=Bx[:, 1], in0=bim, in1=fre_b, op=ALU.mult), reads=[K["pp"], K["wk"]], writes=[K["Bx"]])
                    S.op("dve", lambda e, tb=tb, bre=bre, fim_b=fim_b: e.tensor_tensor(out=tb[:], in0=bre, in1=fim_b, op=ALU.mult), reads=[K["pp"], K["wk"], K["Bx"]], writes=[K["tb"]])
                    S.op("dve", lambda e, Bx=Bx, tb=tb: e.tensor_tensor(out=Bx[:, 1], in0=Bx[:, 1], in1=tb[:], op=ALU.add), reads=[K["Bx"], K["tb"]], writes=[K["Bx"]])
                    S.op("dve", lambda e, AB=AB, W=W: e.tensor_copy(out=AB[:, 0, 0:16], in_=W(6)), reads=[K["wk"]], writes=[K["AB"]])
                    S.op("dve", lambda e, AB=AB, W=W: e.tensor_copy(out=AB[:, 0, 16:32], in_=W(6)), reads=[K["wk"]], writes=[K["AB"]])
                    S.op("dve", lambda e, AB=AB, W=W: e.tensor_scalar(out=AB[:, 1, 0:16], in0=W(7), scalar1=-1.0, scalar2=None, op0=ALU.mult), reads=[K["wk"]], writes=[K["AB"]])
                    S.op("dve", lambda e, AB=AB, W=W: e.tensor_copy(out=AB[:, 1, 16:32], in_=W(7)), reads=[K["wk"]], writes=[K["AB"]])
                    S.op("dve", lambda e, Mt=Mt: e.memset(Mt[:], 0.0), writes=[K["Mt"]])
                    S.op("dve", lambda e, Cbd=Cbd: e.memset(Cbd[:], 0.0), writes=[K["Cbd"]])
                    for part in range(2):
                        for h in range(2):
                            hs = slice(64 * h, 64 * h + 64)
                            S.op("dve", lambda e, Mt=Mt, Bx=Bx, part=part, h=h, hs=hs: e.tensor_copy(
                                out=Mt[hs, part].rearrange("p c (q h e) -> p c q h e", q=4, h=2, e=16)[:, :, :, h, :],
                                in_=Bx[hs, part].rearrange("p (c q) e -> p c q e", c=4)), reads=[K["Bx"], K["Mt"]], writes=[K["Mt"]])
                            for q in range(4):
                                csrc = (cre if part == 0 else cim)
                                S.op("dve", lambda e, Cbd=Cbd, csrc=csrc, part=part, h=h, hs=hs, q=q: e.tensor_scalar(
                                    out=Cbd[hs, part].rearrange("p (c q) n -> p c q n", q=4)[:, :, q, 32 * q + 16 * h:32 * q + 16 * h + 16],
                                    in0=csrc[hs].rearrange("p (c q) e -> p c q e", q=4)[:, :, q, :], scalar1=(1.0 if part == 0 else -1.0), scalar2=None, op0=ALU.mult),
                                     reads=[K["pp"], K["Cbd"]], writes=[K["Cbd"]])
                        ptT, pkT = next_psT()
                        for ct in range(4):
                            S.op("pe", lambda e, Mt=Mt, part=part, ct=ct, ptT=ptT: e.transpose(out=ptT[:, ct, :], in_=Mt[:, part, ct, :], identity=ident[:]), reads=[K["Mt"], "ident"], writes=[pkT])
                        S.op("act", lambda e, Bcm=Bcm, part=part, ptT=ptT: e.activation(out=Bcm[:, part], in_=ptT[:, 0:4, :], func=AF.Copy), reads=[pkT], writes=[K["Bcm"]])
                    ctx.append((eng, pp, AB, Bcm, Cbd, Hc, xs, Hb, t1, t2, ya, K))

                def seq_of(cg):
                    if cg < TS // SC:
                        return (0, 0, TS // SC - 1)
                    p_ = (cg - TS // SC) // (256 // SC)
                    f_ = TS // SC + p_ * (256 // SC)
                    return (1 + p_, f_, f_ + 256 // SC - 1)

                npc = 256 // SC
                bw = list(range(TS // SC - 1, -1, -1))
                for p_ in range(4):
                    f_ = TS // SC + p_ * npc
                    bw += list(range(f_ + npc - 1, f_ - 1, -1))
                order = [list(range(NSC)), bw]
                for d in range(2):
                    eng, pp, AB, Bcm, Cbd, Hc, xs, Hb, t1, t2, ya, K = ctx[d]
                    for cg in order[d]:
                        sq, first, last = seq_of(cg)
                        t0 = cg * SC
                        init_idx = 0 if d == 0 else SC
                        is_start = (cg == first) if d == 0 else (cg == last)
                        if is_start:
                            if sq == 0:
                                S.op(eng, lambda e, Hc=Hc, pp=pp, init_idx=init_idx: e.tensor_copy(out=Hc[:, init_idx, 0:32], in_=pp[:, 1072:1104]), reads=[K["pp"], K["Hc"]], writes=[K["Hc"]])
                                S.op(eng, lambda e, Hc=Hc, pp=pp, init_idx=init_idx: e.tensor_copy(out=Hc[:, init_idx, 32:48], in_=pp[:, 1072:1088]), reads=[K["pp"], K["Hc"]], writes=[K["Hc"]])
                            else:
                                S.op(eng, lambda e, Hc=Hc, init_idx=init_idx: e.memset(Hc[:, init_idx, :], 0.0), reads=[K["Hc"]], writes=[K["Hc"]])
                        else:
                            S.op(eng, lambda e, Hc=Hc, init_idx=init_idx: e.tensor_copy(out=Hc[:, init_idx, :], in_=Hc[:, SC - init_idx, :]), reads=[K["Hc"]], writes=[K["Hc"]])
                        for part in range(2):
                            for ct in range(4):
                                pc = part * 4 + ct
                                for q in range(4):
                                    S.op("pe", lambda e, Bcm=Bcm, part=part, ct=ct, q=q, t0=t0, pc=pc: e.matmul(psF[2 + q][:, pc * SC:(pc + 1) * SC], lhsT=Bcm[32 * q:32 * q + 32, part, ct, :],
                                                                                                             rhs=uT_all[32 * q:32 * q + 32, ct, t0:t0 + SC], start=True, stop=True, skip_group_check=True, tile_position=(32 * q, 0)),
                                         reads=[K["Bcm"], "uT_all"], writes=["psF%d" % (2 + q)])
                        for q in range(4):
                            S.op("act", lambda e, xs=xs, q=q: e.activation(out=xs[:].rearrange("p t (pc q) -> p q pc t", q=4)[:, q], in_=psF[2 + q][:, 0:8 * SC].rearrange("p (pc t) -> p pc t", pc=8), func=AF.Copy),
                                 reads=["psF%d" % (2 + q)], writes=[K["xs"]])
                        for i in range(SC):
                            t = i if d == 0 else SC - 1 - i
                            si = t if d == 0 else t + 1
                            di = t + 1 if d == 0 else t
                            S.op(eng, lambda e, t1=t1, AB=AB, Hc=Hc, si=si: e.tensor_tensor(out=t1[:], in0=AB[:, 0, :], in1=Hc[:, si, 0:32], op=ALU.mult), reads=[K["Hc"], K["AB"]], writes=[K["t1"]])
                            S.op(eng, lambda e, t2=t2, AB=AB, Hc=Hc, si=si: e.tensor_tensor(out=t2[:], in0=AB[:, 1, :], in1=Hc[:, si, 16:48], op=ALU.mult), reads=[K["Hc"], K["AB"]], writes=[K["t2"]])
                            S.op(eng, lambda e, t1=t1, xs=xs, t=t: e.tensor_tensor(out=t1[:], in0=t1[:], in1=xs[:, t, :], op=ALU.add), reads=[K["t1"], K["xs"]], writes=[K["t1"]])
                            S.op(eng, lambda e, t1=t1, t2=t2, Hc=Hc, di=di: e.tensor_tensor(out=Hc[:, di, 0:32], in0=t1[:], in1=t2[:], op=ALU.add), reads=[K["t1"], K["t2"]], writes=[K["Hc"]])
                            S.op(eng, lambda e, Hc=Hc, di=di: e.tensor_copy(out=Hc[:, di, 32:48], in_=Hc[:, di, 0:16]), reads=[K["Hc"]], writes=[K["Hc"]])
                        off = 1 if d == 0 else 0
                        S.op("act", lambda e, Hb=Hb, Hc=Hc, off=off: e.activation(out=Hb[:], in_=Hc[:, off:off + SC, 0:32].rearrange("p t j -> p j t"), func=AF.Copy), reads=[K["Hc"]], writes=[K["Hb"]])
                        if sq > 0 and ((d == 0 and cg == last) or (d == 1 and cg == first)):
                            fi = SC if d == 0 else 0
                            S.dma("sp", lambda e, Hc=Hc, fi=fi, sq=sq, d=d: e.dma_start(out=nst[l, d, sq - 1], in_=Hc[:, fi, 0:32]), reads=[K["Hc"]], writes=["nst"])
                        for ct in range(4):
                            py, pyk = psF[ct % 2], "psF%d" % (ct % 2)
                            n = 0
                            for part in range(2):
                                for q in range(4):
                                    S.op("pe", lambda e, py=py, Cbd=Cbd, Hb=Hb, part=part, ct=ct, q=q, n=n: e.matmul(py[:, 0:SC], lhsT=Cbd[:, part, 4 * ct + q, :], rhs=Hb[:, part * 16 + 4 * ct + q, :], start=(n == 0), stop=(n == 7)),
                                         reads=[K["Cbd"], K["Hb"]], writes=[pyk])
                                    n += 1
                            if d == 0:
                                S.op("act", lambda e, py=py, ya=ya: e.activation(out=ya[:], in_=py[:, 0:SC], func=AF.Copy), reads=[pyk], writes=[K["ya"]])
                                S.dma("sp", lambda e, ya=ya, ct=ct, t0=t0: e.dma_start(out=yfs[ct, :, t0:t0 + SC], in_=ya[:]), reads=[K["ya"]], writes=[("yfs", ct, t0)])
                            else:
                                S.dma("sp", lambda e, ct=ct, t0=t0: e.dma_start(out=yfb[:], in_=yfs[ct, :, t0:t0 + SC]), reads=[("yfs", ct, t0)], writes=["yfb"])
                                S.op("dve", lambda e, py=py, ya=ya: e.tensor_tensor(out=ya[:], in0=py[:, 0:SC], in1=yfb[:], op=ALU.add), reads=[pyk, "yfb"], writes=[K["ya"]])
                                S.op("dve", lambda e, ya=ya, ct=ct, t0=t0: e.scalar_tensor_tensor(out=ya[:], in0=uT_all[:, ct, t0:t0 + SC], scalar=dsk[:, ct:ct + 1], in1=ya[:], op0=ALU.mult, op1=ALU.add),
                                     reads=["uT_all", "dsk", K["ya"]], writes=[K["ya"]])
                                S.op("act", lambda e, ya=ya: e.activation(out=yab[:], in_=ya[:], func=AF.Gelu), reads=[K["ya"]], writes=["yab"])
                                S.dma("sp", lambda e, ct=ct, t0=t0: e.dma_start(out=yas[ct, :, t0:t0 + SC], in_=yab[:]), reads=["yab"], writes=["yas"])

        def phase_B2(l):
            with contextlib.ExitStack() as ph:
                dsk = sb(P.name("dsk"), [128, 4], F32, ph)
                S.dma("sp", lambda e: e.dma_start(out=dsk[:], in_=dskT[l]), writes=["dsk"])
                Bcm_b = sb(P.name("Bcm_b"), [128, 16, 2, 4, 128], BF16, ph)
                Wo = sb(P.name("Wo"), [128, 2, 16, 512], BF16, ph)
                Kbd = sb(P.name("Kbd"), [128, 16, 4, 128], BF16, ph)
                Cbd = sb(P.name("Cbd"), [128, 2, 16, 128], BF16, ph)
                ytb = [sb(P.name("ytb"), [32, 16, 128], BF16, ph) for _ in range(2)]
                rowsf = rows[:].rearrange("p a n -> p (a n)")
                pp = rowsf[:, 0:1104]
                Pw = rowsf[:, 1104:1648].rearrange("p (r k g) -> p r k g", r=2, k=17)
                Bx = rowsf[:, 1648:2160].rearrange("p (r g e) -> p r g e", r=2, g=16)
                wk = rowsf[:, 2160:2352].rearrange("p (i g) -> p i g", i=12)
                tb = rowsf[:, 2352:2608].rearrange("p (g e) -> p g e", g=16)
                V = rowsf[:, 2608:2864].rearrange("p (g e) -> p g e", g=16)
                pa = rowsf[:, 2864:2896].rearrange("p (r g) -> p r g", r=2)
                cur = rowsf[:, 2896:2944]
                t1 = rowsf[:, 2944:2976]
                t2 = rowsf[:, 2976:3008]
                AB = rowsf[:, 3008:3072].rearrange("p (r g) -> p r g", r=2)
                wsb = xt_ring[0][:].rearrange("p (j a) -> p j a", j=32)
                yfb = xt_ring[1][:, 0:512]
                V2 = xt_ring[1][:, 512:768].rearrange("p (g e) -> p g e", g=16)
                Xbd = hT[:].rearrange("p k (c n) -> p (k c) n", n=128).rearrange("p (r g) n -> p r g n", r=2)
                Mt = junk[:].rearrange("p (r c n) -> p r c n", r=2, c=4)
                Est = hb_ring[0][:].rearrange("p (j a) -> p j a", j=32)
                yab = hb_ring[1][:, 0:512]
                KB = {n: P.name(n) for n in ("Bcm_b", "Wo", "Kbd", "Xbd", "Cbd", "Mt", "Pw", "V", "V2", "pa", "wsb", "Est", "cur", "t1", "t2", "AB", "ytb0", "ytb1", "yfb", "yab")}
                KP = {n: P.name(n) for n in ("pp", "wk", "Bx", "tb")}
                S.op("dve", lambda e: e.memset(Xbd, 0.0), writes=[KB["Xbd"]])
                S.op("dve", lambda e: e.memset(Mt, 0.0), writes=[KB["Mt"]])
                for d in range(2):
                    eng = "dve" if d == 0 else "pool"
                    K = KP
                    S.dma("sp", lambda e, pp=pp, d=d: e.dma_start(out=pp, in_=s5p[l, d]), writes=[K["pp"]])
                    lre, lim, lst = pp[:, 0:16], pp[:, 16:32], pp[:, 32:48]
                    bre = pp[:, 48:304].rearrange("p (g e) -> p g e", e=16); bim = pp[:, 304:560].rearrange("p (g e) -> p g e", e=16)
                    cre = pp[:, 560:816].rearrange("p (g e) -> p g e", e=16); cim = pp[:, 816:1072].rearrange("p (g e) -> p g e", e=16)
                    W = lambda i, wk=wk: wk[:, i, :]

                    def dv(fn, pp=pp, K=K):
                        S.op("dve", fn, reads=[K["pp"], K["wk"]], writes=[K["wk"]])

                    def ac(fn, K=K):
                        S.op("act", fn, reads=[K["wk"], K["pp"]], writes=[K["wk"]])
                    ac(lambda e, W=W, lst=lst: e.activation(out=W(0), in_=lst, func=AF.Exp))
                    dv(lambda e, W=W, lre=lre: e.tensor_tensor(out=W(1), in0=lre, in1=W(0), op=ALU.mult))
                    dv(lambda e, W=W, lim=lim: e.tensor_tensor(out=W(2), in0=lim, in1=W(0), op=ALU.mult))
                    ac(lambda e, W=W: e.activation(out=W(3), in_=W(1), func=AF.Exp))
                    ac(lambda e, W=W: e.activation(out=W(4), in_=W(2), func=AF.Sin, scale=1.0 / 16))
                    ac(lambda e, W=W: e.activation(out=W(5), in_=W(2), func=AF.Sin, scale=1.0 / 32))
                    dv(lambda e, W=W: e.tensor_tensor(out=W(5), in0=W(5), in1=W(5), op=ALU.mult))
                    dv(lambda e, W=W: e.tensor_scalar(out=W(5), in0=W(5), scalar1=-2.0, scalar2=1.0, op0=ALU.mult, op1=ALU.add))
                    for _ in range(4):
                        dv(lambda e, W=W: e.tensor_tensor(out=W(6), in0=W(4), in1=W(5), op=ALU.mult))
                        dv(lambda e, W=W: e.tensor_tensor(out=W(7), in0=W(4), in1=W(4), op=ALU.mult))
                        dv(lambda e, W=W: e.tensor_tensor(out=W(5), in0=W(5), in1=W(5), op=ALU.mult))
                        dv(lambda e, W=W: e.tensor_tensor(out=W(5), in0=W(5), in1=W(7), op=ALU.subtract))
                        dv(lambda e, W=W: e.tensor_scalar(out=W(4), in0=W(6), scalar1=2.0, scalar2=None, op0=ALU.mult))
                    dv(lambda e, W=W: e.tensor_tensor(out=W(6), in0=W(3), in1=W(5), op=ALU.mult))
                    dv(lambda e, W=W: e.tensor_tensor(out=W(7), in0=W(3), in1=W(4), op=ALU.mult))
                    dv(lambda e, W=W, lre=lre: e.tensor_tensor(out=W(8), in0=lre, in1=lre, op=ALU.mult))
                    dv(lambda e, W=W, lim=lim: e.tensor_tensor(out=W(9), in0=lim, in1=lim, op=ALU.mult))
                    dv(lambda e, W=W: e.tensor_tensor(out=W(8), in0=W(8), in1=W(9), op=ALU.add))
                    dv(lambda e, W=W: e.reciprocal(out=W(8), in_=W(8)))
                    dv(lambda e, W=W: e.tensor_scalar(out=W(9), in0=W(6), scalar1=-1.0, scalar2=None, op0=ALU.add))
                    dv(lambda e, W=W, lre=lre: e.tensor_tensor(out=W(10), in0=W(9), in1=lre, op=ALU.mult))
                    dv(lambda e, W=W, lim=lim: e.tensor_tensor(out=W(11), in0=W(7), in1=lim, op=ALU.mult))
                    dv(lambda e, W=W: e.tensor_tensor(out=W(10), in0=W(10), in1=W(11), op=ALU.add))
                    dv(lambda e, W=W: e.tensor_tensor(out=W(10), in0=W(10), in1=W(8), op=ALU.mult))
                    dv(lambda e, W=W, lre=lre: e.tensor_tensor(out=W(11), in0=W(7), in1=lre, op=ALU.mult))
                    dv(lambda e, W=W, lim=lim: e.tensor_tensor(out=W(9), in0=W(9), in1=lim, op=ALU.mult))
                    dv(lambda e, W=W: e.tensor_tensor(out=W(11), in0=W(11), in1=W(9), op=ALU.subtract))
                    dv(lambda e, W=W: e.tensor_tensor(out=W(11), in0=W(11), in1=W(8), op=ALU.mult))
                    fre_b = wk[:, 10, :].unsqueeze(2).broadcast_to([128, 16, 16]); fim_b = wk[:, 11, :].unsqueeze(2).broadcast_to([128, 16, 16])
                    S.op("dve", lambda e, Bx=Bx, bre=bre, fre_b=fre_b: e.tensor_tensor(out=Bx[:, 0], in0=bre, in1=fre_b, op=ALU.mult), reads=[K["pp"], K["wk"]], writes=[K["Bx"]])
                    S.op("dve", lambda e, tb=tb, bim=bim, fim_b=fim_b: e.tensor_tensor(out=tb[:], in0=bim, in1=fim_b, op=ALU.mult), reads=[K["pp"], K["wk"]], writes=[K["tb"]])
                    S.op("dve", lambda e, Bx=Bx, tb=tb: e.tensor_tensor(out=Bx[:, 0], in0=Bx[:, 0], in1=tb[:], op=ALU.subtract), reads=[K["Bx"], K["tb"]], writes=[K["Bx"]])
                    S.op("dve", lambda e, Bx=Bx, bim=bim, fre_b=fre_b: e.tensor_tensor(out=Bx[:, 1], in0=bim, in1=fre_b, op=ALU.mult), reads=[K["pp"], K["wk"]], writes=[K["Bx"]])
                    S.op("dve", lambda e, tb=tb, bre=bre, fim_b=fim_b: e.tensor_tensor(out=tb[:], in0=bre, in1=fim_b, op=ALU.mult), reads=[K["pp"], K["wk"], K["Bx"]], writes=[K["tb"]])
                    S.op("dve", lambda e, Bx=Bx, tb=tb: e.tensor_tensor(out=Bx[:, 1], in0=Bx[:, 1], in1=tb[:], op=ALU.add), reads=[K["Bx"], K["tb"]], writes=[K["Bx"]])

                    lr, li = wk[:, 6, :], wk[:, 7, :]
                    S.op("dve", lambda e: e.memset(Pw[:, 0, 0, :], 1.0), reads=[KB["Pw"]], writes=[KB["Pw"]])
                    S.op("dve", lambda e: e.memset(Pw[:, 1, 0, :], 0.0), reads=[KB["Pw"]], writes=[KB["Pw"]])
                    for k in range(16):
                        S.op("dve", lambda e, k=k, lr=lr: e.tensor_tensor(out=pa[:, 0, :], in0=Pw[:, 0, k, :], in1=lr, op=ALU.mult), reads=[KB["Pw"], K["wk"]], writes=[KB["pa"]])
                        S.op("dve", lambda e, k=k, li=li: e.tensor_tensor(out=pa[:, 1, :], in0=Pw[:, 1, k, :], in1=li, op=ALU.mult), reads=[KB["Pw"], K["wk"]], writes=[KB["pa"]])
                        S.op("dve", lambda e, k=k: e.tensor_tensor(out=Pw[:, 0, k + 1, :], in0=pa[:, 0, :], in1=pa[:, 1, :], op=ALU.subtract), reads=[KB["pa"]], writes=[KB["Pw"]])
                        S.op("dve", lambda e, k=k, li=li: e.tensor_tensor(out=pa[:, 0, :], in0=Pw[:, 0, k, :], in1=li, op=ALU.mult), reads=[KB["Pw"], K["wk"]], writes=[KB["pa"]])
                        S.op("dve", lambda e, k=k, lr=lr: e.tensor_tensor(out=pa[:, 1, :], in0=Pw[:, 1, k, :], in1=lr, op=ALU.mult), reads=[KB["Pw"], K["wk"]], writes=[KB["pa"]])
                        S.op("dve", lambda e, k=k: e.tensor_tensor(out=Pw[:, 1, k + 1, :], in0=pa[:, 0, :], in1=pa[:, 1, :], op=ALU.add), reads=[KB["pa"]], writes=[KB["Pw"]])
                    S.op("dve", lambda e: e.tensor_copy(out=AB[:, 0, 0:16], in_=Pw[:, 0, 16, :]), reads=[KB["Pw"], KB["AB"]], writes=[KB["AB"]])
                    S.op("dve", lambda e: e.tensor_copy(out=AB[:, 0, 16:32], in_=Pw[:, 0, 16, :]), reads=[KB["Pw"], KB["AB"]], writes=[KB["AB"]])
                    S.op("dve", lambda e: e.tensor_scalar(out=AB[:, 1, 0:16], in0=Pw[:, 1, 16, :], scalar1=-1.0, scalar2=None, op0=ALU.mult), reads=[KB["Pw"], KB["AB"]], writes=[KB["AB"]])
                    S.op("dve", lambda e: e.tensor_copy(out=AB[:, 1, 16:32], in_=Pw[:, 1, 16, :]), reads=[KB["Pw"], KB["AB"]], writes=[KB["AB"]])
                    S.op("dve", lambda e: e.memset(Cbd[:], 0.0), reads=[KB["Cbd"]], writes=[KB["Cbd"]])
                    S.op("dve", lambda e: e.memset(Wo[:], 0.0), reads=[KB["Wo"]], writes=[KB["Wo"]])
                    for part in range(2):
                        csrc = (cre if part == 0 else cim)
                        for h in range(2):
                            hs = slice(64 * h, 64 * h + 64)
                            for q in range(4):
                                S.op("dve", lambda e, csrc=csrc, part=part, h=h, hs=hs, q=q: e.tensor_scalar(
                                    out=Cbd[hs, part].rearrange("p (c q) n -> p c q n", q=4)[:, :, q, 32 * q + 16 * h:32 * q + 16 * h + 16],
                                    in0=csrc[hs].rearrange("p (c q) e -> p c q e", q=4)[:, :, q, :], scalar1=(1.0 if part == 0 else -1.0), scalar2=None, op0=ALU.mult),
                                     reads=[K["pp"], KB["Cbd"]], writes=[KB["Cbd"]])

                    def pwb(ri, k):
                        return Pw[:, ri, k, :].unsqueeze(2).broadcast_to([128, 16, 16])
                    for e_ in range(16):
                        b_ = (15 - e_) if d == 0 else e_
                        for part in range(2):
                            i0, i1 = (0, 1) if part == 0 else (1, 0)
                            S.op("dve", lambda e, e_=e_, i0=i0, Bx=Bx: e.tensor_tensor(out=V[:], in0=Bx[:, i0], in1=pwb(0, e_), op=ALU.mult), reads=[K["Bx"], KB["Pw"]], writes=[KB["V"]])
                            S.op("dve", lambda e, e_=e_, i1=i1, Bx=Bx: e.tensor_tensor(out=V2[:], in0=Bx[:, i1], in1=pwb(1, e_), op=ALU.mult), reads=[K["Bx"], KB["Pw"]], writes=[KB["V2"]])
                            S.op("dve", lambda e, part=part: e.tensor_tensor(out=V[:], in0=V[:], in1=V2[:], op=(ALU.subtract if part == 0 else ALU.add)), reads=[KB["V"], KB["V2"]], writes=[KB["V"]])
                            for h in range(2):
                                hs = slice(64 * h, 64 * h + 64)
                                S.op("dve", lambda e, part=part, h=h, hs=hs: e.tensor_copy(
                                    out=Mt[hs, part].rearrange("p c (q h e) -> p c q h e", q=4, h=2, e=16)[:, :, :, h, :],
                                    in_=V[hs].rearrange("p (c q) e -> p c q e", c=4)), reads=[KB["V"], KB["Mt"]], writes=[KB["Mt"]])
                                for q in range(4):
                                    S.op("dve", lambda e, part=part, h=h, hs=hs, q=q: e.tensor_copy(
                                        out=Xbd[hs, part].rearrange("p (c q) n -> p c q n", q=4)[:, :, q, 32 * q + 16 * h:32 * q + 16 * h + 16],
                                        in_=V[hs].rearrange("p (c q) e -> p c q e", q=4)[:, :, q, :]), reads=[KB["V"], KB["Xbd"]], writes=[KB["Xbd"]])
                            ptT, pkT = next_psT()
                            for ct in range(4):
                                S.op("pe", lambda e, part=part, ct=ct, ptT=ptT: e.transpose(out=ptT[:, ct, :], in_=Mt[:, part, ct, :], identity=ident[:]), reads=[KB["Mt"], "ident"], writes=[pkT])
                            S.op("act", lambda e, part=part, ptT=ptT, b_=b_: e.activation(out=Bcm_b[:, b_, part], in_=ptT[:, 0:4, :], func=AF.Copy), reads=[pkT], writes=[KB["Bcm_b"]])
                        pk_, pkk = next_psF()
                        for ct in range(4):
                            n = 0
                            for part in range(2):
                                for q in range(4):
                                    S.op("pe", lambda e, pk_=pk_, part=part, ct=ct, q=q, n=n: e.matmul(pk_[:, ct * 128:(ct + 1) * 128], lhsT=Xbd[:, part, 4 * ct + q, :], rhs=Cbd[:, part, 4 * ct + q, :],
                                                                                              start=(n == 0 and ct == 0), stop=(n == 7 and ct == 3), skip_group_check=True),
                                         reads=[KB["Xbd"], KB["Cbd"]], writes=[pkk])
                                    n += 1
                        S.op("act", lambda e, pk_=pk_, e_=e_: e.activation(out=Kbd[:, e_].rearrange("p c n -> p (c n)"), in_=pk_[:], func=AF.Copy), reads=[pkk], writes=[KB["Kbd"]])
                    for k in range(1, 17):
                        b_ = (k - 1) if d == 0 else (16 - k)
                        for part in range(2):
                            if part == 0:
                                S.op("dve", lambda e, k=k: e.tensor_tensor(out=V[:], in0=cre, in1=pwb(0, k), op=ALU.mult), reads=[K["pp"], KB["Pw"]], writes=[KB["V"]])
                                S.op("dve", lambda e, k=k: e.tensor_tensor(out=V2[:], in0=cim, in1=pwb(1, k), op=ALU.mult), reads=[K["pp"], KB["Pw"]], writes=[KB["V2"]])
                                S.op("dve", lambda e: e.tensor_tensor(out=V[:], in0=V[:], in1=V2[:], op=ALU.subtract), reads=[KB["V"], KB["V2"]], writes=[KB["V"]])
                            else:
                                S.op("dve", lambda e, k=k: e.tensor_tensor(out=V[:], in0=cre, in1=pwb(1, k), op=ALU.mult), reads=[K["pp"], KB["Pw"]], writes=[KB["V"]])
                                S.op("dve", lambda e, k=k: e.tensor_tensor(out=V2[:], in0=cim, in1=pwb(0, k), op=ALU.mult), reads=[K["pp"], KB["Pw"]], writes=[KB["V2"]])
                                S.op("dve", lambda e: e.scalar_tensor_tensor(out=V[:], in0=V[:], scalar=-1.0, in1=V2[:], op0=ALU.mult, op1=ALU.subtract), reads=[KB["V"], KB["V2"]], writes=[KB["V"]])
                            for h in range(2):
                                hs = slice(64 * h, 64 * h + 64)
                                S.op("dve", lambda e, part=part, h=h, hs=hs, b_=b_: e.tensor_copy(
                                    out=Wo[hs, part].rearrange("p g (b c) -> p g b c", b=16)[:, :, b_, 16 * h:16 * h + 16], in_=V[hs]),
                                     reads=[KB["V"], KB["Wo"]], writes=[KB["Wo"]])
                    blocks = list(range(NBLK)) if d == 0 else [7, 6, 5, 4, 3, 2, 1, 0, 9, 8]
                    def emit_in(blk, slot, d=d):
                        tok0 = blk * 512
                        wsb_ = wsbs[slot]
                        u3 = [uT_all[:, ct, tok0:tok0 + 512].rearrange("p (a b) -> p a b", b=16) for ct in range(4)]
                        for pc in range(8):
                            part, ct = pc // 4, pc % 4
                            for b_ in range(16):
                                for q in range(4):
                                    S.op("pe", lambda e, part=part, ct=ct, b_=b_, q=q, pc=pc, tok0=tok0: e.matmul(
                                        psF[2 + q][:, pc * 32:(pc + 1) * 32], lhsT=Bcm_b[32 * q:32 * q + 32, b_, part, ct, :],
                                        rhs=uT_all[32 * q:32 * q + 32, ct, tok0:tok0 + 512].rearrange("p (a b) -> p a b", b=16)[:, :, b_],
                                        start=(b_ == 0), stop=(b_ == 15), skip_group_check=True, tile_position=(32 * q, 0)),
                                         reads=[KB["Bcm_b"], "uT_all"], writes=["psF%d" % (2 + q)])
                        for q in range(4):
                            S.op("act", lambda e, q=q, wsb_=wsb_: e.activation(out=wsb_.rearrange("p (pc q) a -> p q pc a", q=4)[:, q], in_=psF[2 + q][:, 0:256].rearrange("p (pc a) -> p pc a", pc=8), func=AF.Copy),
                                 reads=["psF%d" % (2 + q)], writes=[KB["wsb"] + str(slot)])

                    def emit_scan(blk, slot, d=d, eng=eng, pp=pp, K=K):
                        wsb_ = wsbs[slot]
                        alist = list(range(32)) if d == 0 else list(range(31, -1, -1))
                        for a in alist:
                            if blk < 8:
                                is_init = (blk == 0 and a == 0) if d == 0 else (blk == 7 and a == 31)
                                is_fin = False; sq = 0
                            else:
                                is_init = (a % 16 == 0) if d == 0 else (a % 16 == 15)
                                is_fin = (a % 16 == 15) if d == 0 else (a % 16 == 0)
                                sq = 1 + (blk - 8) * 2 + a // 16
                            if is_init:
                                if sq == 0:
                                    S.op(eng, lambda e, pp=pp: e.tensor_copy(out=cur[:, 0:32], in_=pp[:, 1072:1104]), reads=[K["pp"], KB["cur"]], writes=[KB["cur"]])
                                    S.op(eng, lambda e, pp=pp: e.tensor_copy(out=cur[:, 32:48], in_=pp[:, 1072:1088]), reads=[K["pp"], KB["cur"]], writes=[KB["cur"]])
                                else:
                                    S.op(eng, lambda e: e.memset(cur[:], 0.0), reads=[KB["cur"]], writes=[KB["cur"]])
                            S.op(eng, lambda e, a=a: e.tensor_copy(out=Est[:, :, a], in_=cur[:, 0:32]), reads=[KB["cur"], KB["Est"]], writes=[KB["Est"]])
                            S.op(eng, lambda e: e.tensor_tensor(out=t1[:], in0=AB[:, 0, :], in1=cur[:, 0:32], op=ALU.mult), reads=[KB["cur"], KB["AB"]], writes=[KB["t1"]])
                            S.op(eng, lambda e: e.tensor_tensor(out=t2[:], in0=AB[:, 1, :], in1=cur[:, 16:48], op=ALU.mult), reads=[KB["cur"], KB["AB"]], writes=[KB["t2"]])
                            S.op(eng, lambda e, a=a, wsb_=wsb_: e.tensor_tensor(out=t1[:], in0=t1[:], in1=wsb_[:, :, a], op=ALU.add), reads=[KB["t1"], KB["wsb"] + str(slot)], writes=[KB["t1"]])
                            S.op(eng, lambda e: e.tensor_tensor(out=cur[:, 0:32], in0=t1[:], in1=t2[:], op=ALU.add), reads=[KB["t1"], KB["t2"]], writes=[KB["cur"]])
                            S.op(eng, lambda e: e.tensor_copy(out=cur[:, 32:48], in_=cur[:, 0:16]), reads=[KB["cur"]], writes=[KB["cur"]])
                            if is_fin:
                                S.dma("sp", lambda e, sq=sq, d=d: e.dma_start(out=nst[l, d, sq - 1], in_=cur[:, 0:32]), reads=[KB["cur"]], writes=["nst"])

                    def emit_out(blk, d=d):
                        tok0 = blk * 512
                        for ct in range(4):
                            yb_ = psF[ct % 2]; ybk = "psF%d" % (ct % 2)
                            y3 = yb_[:, 0:512].rearrange("p (a b) -> p a b", b=16)
                            for k in range(16):
                                if d == 0:
                                    osl, isl = slice(k, 16), slice(0, 16 - k)
                                else:
                                    osl, isl = slice(0, 16 - k), slice(k, 16)
                                S.op("pe", lambda e, ct=ct, k=k, y3=y3, osl=osl, isl=isl, tok0=tok0: e.matmul(
                                    y3[:, :, osl], lhsT=Kbd[:, k, ct, :], rhs=uT_all[:, ct, tok0:tok0 + 512].rearrange("p (a b) -> p a b", b=16)[:, :, isl],
                                    start=(k == 0), stop=False, skip_group_check=True), reads=[KB["Kbd"], "uT_all"], writes=[ybk])
                            yt = ytb[ct % 2]; ytk = KB["ytb%d" % (ct % 2)]
                            for q in range(4):
                                gp = 4 * ct + q
                                ps_ = psTf[q % 2]; psk = "psT%d" % (q % 2)
                                for part in range(2):
                                    S.op("pe", lambda e, ps_=ps_, part=part, gp=gp: e.matmul(ps_[0:32, 0:512], lhsT=Est[:, part * 16 + gp, :], rhs=Wo[:, part, gp, :],
                                                                                           start=(part == 0), stop=(part == 1)), reads=[KB["Est"], KB["Wo"]], writes=[psk])
                                S.op("act", lambda e, ps_=ps_, yt=yt, q=q: e.activation(out=yt[:, :, 32 * q:32 * q + 32], in_=ps_[0:32, 0:512].rearrange("p (b c) -> p b c", c=32), func=AF.Copy),
                                     reads=[psk], writes=[ytk])
                            for b_ in range(16):
                                S.op("pe", lambda e, y3=y3, yt=yt, b_=b_: e.matmul(y3[:, :, b_], lhsT=yt[:, b_, :], rhs=ident[0:32, 0:32], start=False, stop=(b_ == 15), skip_group_check=True),
                                     reads=[ytk, "ident"], writes=[ybk])
                            if d == 0:
                                S.op("act", lambda e, yb_=yb_: e.activation(out=yfb[:], in_=yb_[:], func=AF.Copy), reads=[ybk], writes=[KB["yfb"]])
                                S.dma("sp", lambda e, ct=ct, tok0=tok0: e.dma_start(out=yfs[ct, :, tok0:tok0 + 512], in_=yfb[:]), reads=[KB["yfb"]], writes=[("yfs", ct, tok0)])
                            else:
                                S.dma("sp", lambda e, ct=ct, tok0=tok0: e.dma_start(out=yfb[:], in_=yfs[ct, :, tok0:tok0 + 512]), reads=[("yfs", ct, tok0)], writes=[KB["yfb"]])
                                S.op("dve", lambda e, yb_=yb_: e.tensor_tensor(out=yfb[:], in0=yb_[:], in1=yfb[:], op=ALU.add), reads=[ybk, KB["yfb"]], writes=[KB["yfb"]])
                                S.op("dve", lambda e, ct=ct, tok0=tok0: e.scalar_tensor_tensor(out=yfb[:], in0=uT_all[:, ct, tok0:tok0 + 512], scalar=dsk[:, ct:ct + 1], in1=yfb[:], op0=ALU.mult, op1=ALU.add),
                                     reads=["uT_all", "dsk", KB["yfb"]], writes=[KB["yfb"]])
                                S.op("act", lambda e: e.activation(out=yab[:], in_=yfb[:], func=AF.Gelu), reads=[KB["yfb"]], writes=[KB["yab"]])
                                S.dma("sp", lambda e, ct=ct, tok0=tok0: e.dma_start(out=yas[ct, :, tok0:tok0 + 512], in_=yab[:]), reads=[KB["yab"]], writes=["yas"])


                    wsbs = [wsb, tmpf[:].rearrange("p (j a) -> p j a", j=32)]
                    psTf = [psT[i][:].rearrange("p k n -> p (k n)").bitcast(F32) for i in range(2)]
                    for bi, blk in enumerate(blocks):
                        emit_in(blk, 0)
                        emit_scan(blk, 0)
                        emit_out(blk)

        def dump(name, ap, key):
            if name in dbg:
                S.dma("sp", lambda e: e.dma_start(out=dbg[name], in_=ap), reads=[key], writes=["dbgout_" + name])

        for l in range(NL):
            phase_mod(l)
            S.barrier()
            if stage >= 2:
                phase_A(l)
                S.barrier()
                if l == 0:
                    dump("uT", uT_all[:], "uT_all")
                    dump("KT", KT_all[:], "KT_all")
                    dump("V", V_all[:], "V_all")
            if stage >= 3:
                if stage >= 5:
                    if stage == 6:
                        phase_B(l)
                    else:
                        phase_B2(l)
                    S.barrier()
                phase_C(l)
                S.barrier()
                if l == 0:
                    dump("xmid", xmid, ("dram", id(xmid)))
            if stage >= 4:
                phase_D(l)
                S.barrier()
                if l == 0:
                    dump("x1", x1, ("dram", id(x1)))
            if stage < 4:
                break

        if "modrows" in dbg:
            S.dma("sp", lambda e: e.dma_start(out=dbg["modrows"], in_=modrows), reads=["modrows"], writes=["dbgout"])

        S.wait_all("sp")
        S.emit(nc, top)
    return nc


def host_prep(inputs, core):
    b = core
    f = lambda a: np.ascontiguousarray(np.asarray(a, dtype=np.float32))
    m = {}
    m["x0"] = f(np.concatenate([inputs["x_sample"][b], inputs["x_prompt"][4 * b:4 * b + 4].reshape(1024, D)], axis=0))
    m["ckv"] = f(np.stack([inputs["cache_k"][b].reshape(NL, 256, 128), inputs["cache_v"][b].reshape(NL, 256, 128)], axis=1))
    m["cvec"] = f(np.stack([inputs["c"][b], inputs["c_ctx"]], axis=0))
    m["w_mod"] = f(inputs["w_mod"]); m["b_mod"] = f(inputs["b_mod"])
    m["gvec"] = f(np.stack([inputs["g_pre_mix"], inputs["g_post_mix"], inputs["g_pre_ffn"], inputs["g_post_ffn"]], axis=1))
    for k in ("w_in", "w_glu", "w_b_out", "w_c_out", "w_o", "w_up", "w_down", "conv_w", "conv_b", "g_sgu", "sink"):
        m[k] = f(inputs[k])
    m["w_spT"] = f(np.transpose(inputs["w_spatial"], (0, 1, 3, 2)))
    m["b_sp"] = f(inputs["b_spatial"])
    def sm(a):
        a = np.asarray(a, np.float32)
        a = a.reshape((16, 2, 64) + a.shape[2:])
        return np.ascontiguousarray(np.moveaxis(a, 0, 2)).reshape((128, 16) + a.shape[3:])
    s5 = np.zeros((NL, 2, 128, 1104), np.float32)
    for l in range(NL):
        for d in range(2):
            s5[l, d, :, 0:16] = sm(inputs["lam_re"][l, d]); s5[l, d, :, 16:32] = sm(inputs["lam_im"][l, d])
            s5[l, d, :, 32:48] = sm(np.repeat(inputs["log_step"][l, d][:, None], 64, axis=1))
            s5[l, d, :, 48:304] = sm(inputs["b_re"][l, d]).reshape(128, 256); s5[l, d, :, 304:560] = sm(inputs["b_im"][l, d]).reshape(128, 256)
            s5[l, d, :, 560:816] = sm(np.transpose(inputs["c_re"][l, d], (0, 2, 1))).reshape(128, 256)
            s5[l, d, :, 816:1072] = sm(np.transpose(inputs["c_im"][l, d], (0, 2, 1))).reshape(128, 256)
            s5[l, d, :, 1072:1088] = sm(inputs["state_ssm_re"][b, l, d]); s5[l, d, :, 1088:1104] = sm(inputs["state_ssm_im"][b, l, d])
    m["s5p"] = s5
    m["dskT"] = f(np.transpose(inputs["d_skip"].reshape(NL, 4, 128), (0, 2, 1)))
    p = np.arange(128)[:, None]; c = np.arange(32)[None, :]
    t = 128 * c + p
    row = (t // 64).astype(np.float32); col = (t % 64).astype(np.float32)
    inv = (10000.0 ** (-np.arange(0, 32, 2, dtype=np.float32) / 32)).astype(np.float32)
    ar = row[:, :, None] * inv[None, None, :]; ac = col[:, :, None] * inv[None, None, :]
    cosr, sinr, cosc, sinc = np.cos(ar), np.sin(ar), np.cos(ac), np.sin(ac)
    m["ropec"] = f(np.concatenate([cosr, cosr, cosc, cosc], axis=-1))
    m["ropes"] = f(np.concatenate([-sinr, sinr, -sinc, sinc], axis=-1))
    return m


_NC_CACHE = {}


def kernel(**inputs):
    inputs = {k: np.asarray(v) for k, v in inputs.items()}
    if "nc" not in _NC_CACHE:
        _NC_CACHE["nc"] = build(stage=99)
    nc = _NC_CACHE["nc"]
    in_maps = [host_prep(inputs, c) for c in range(8)]
    res = run_bass_kernel_spmd(nc, in_maps, core_ids=list(range(8)))
    ys = np.zeros((8, 4096, D), np.float32)
    yp = np.zeros((32, 256, D), np.float32)
    nk = np.zeros((32, NL, 256, 2, 64), np.float32)
    nv = np.zeros((32, NL, 256, 2, 64), np.float32)
    nsr = np.zeros((32, NL, 2, 32, 64), np.float32)
    nsi = np.zeros((32, NL, 2, 32, 64), np.float32)
    for c in range(8):
        r = res.results[c]
        yy = np.asarray(r["y"], np.float32)
        ys[c] = yy[:4096]
        yp[4 * c:4 * c + 4] = yy[4096:].reshape(4, 256, D)
        st_ = np.asarray(r["nst"], np.float32)
        for l in range(NL):
            for d in range(2):
                for p_ in range(4):
                    a = st_[l, d, p_]
                    nsr[4 * c + p_, l, d] = a[:, 0:16].reshape(2, 64, 16).transpose(2, 0, 1).reshape(32, 64)
                    nsi[4 * c + p_, l, d] = a[:, 16:32].reshape(2, 64, 16).transpose(2, 0, 1).reshape(32, 64)
        kv = np.asarray(r["nkv"], np.float32)
        for l in range(NL):
            nk[4 * c:4 * c + 4, l] = kv[l, 0].reshape(4, 256, 2, 64)
            nv[4 * c:4 * c + 4, l] = kv[l, 1].reshape(4, 256, 2, 64)
    return (yp, ys, nk, nv, nsr, nsi)
```

```python
import contextlib
import math
import numpy as np
import concourse.bass as bass
import concourse.mybir as mybir
from concourse.bass_utils import run_bass_kernel_spmd

F32 = mybir.dt.float32
BF16 = mybir.dt.bfloat16
AF = mybir.ActivationFunctionType
ALU = mybir.AluOpType
AX = mybir.AxisListType

COMPUTE = ("pe", "act", "dve", "pool")
N_DMA_SEMS = 12

D = 1024
NTOK = 5120
TS = 4096
NBLK = 10
D_IN = 5376
OFF_B, OFF_Q, OFF_K, OFF_V, OFF_G = 512, 1536, 2048, 2176, 2304
D_FF = 2816
EPS = 1e-6
NL = 2


class Sched:
    def __init__(self):
        self.streams = {e: [] for e in ("pe", "act", "dve", "pool", "sp")}
        self.cnt = {e: 0 for e in COMPUTE}
        self.known = {e: {} for e in self.streams}
        self.last_w = {}
        self.readers = {}
        self.dma_cnt = {}
        self.dma_rr = {q: 0 for q in ("sp", "act", "pool", "bg")}
        self.n_ins = 0

    def _need(self, eng, tickets):
        kn = self.known[eng]
        best = {}
        for t in tickets:
            if t is None:
                continue
            k, v = t
            if kn.get(k, 0) >= v:
                continue
            if best.get(k, 0) < v:
                best[k] = v
        for k, v in best.items():
            kn[k] = v
        return list(best.items())

    def _deps(self, eng, reads, writes, is_dma=False):
        ts = []
        for r in reads:
            ts.append(self.last_w.get(r))
        for w in writes:
            ts.append(self.last_w.get(w))
            for t in self.readers.get(w, ()):
                if (not is_dma) and t[0] == eng:
                    continue
                ts.append(t)
        if eng == "pe" and not is_dma:
            ts = [t for t in ts if t is not None and t[0] != "pe"]
        return ts

    def _commit(self, ticket, reads, writes):
        for r in reads:
            self.readers.setdefault(r, []).append(ticket)
        for w in writes:
            self.last_w[w] = ticket
            self.readers[w] = []

    def op(self, eng, fn, reads=(), writes=()):
        waits = self._need(eng, self._deps(eng, reads, writes))
        self.cnt[eng] += 1
        ticket = (eng, self.cnt[eng])
        self.streams[eng].append(("op", fn, waits, eng))
        self._commit(ticket, reads, writes)
        self.n_ins += 1
        return ticket

    def dma(self, q, fn, reads=(), writes=(), bg=False):
        qq = "bg" if bg else q
        i = self.dma_rr[qq]
        self.dma_rr[qq] = (i + 1) % N_DMA_SEMS
        key = ("dma", qq, i)
        n = self.dma_cnt.get(key, 0)
        deps = self._deps(q, reads, writes, is_dma=True)
        if n > 0:
            deps.append((key, 16 * n))
        waits = self._need(q, deps)
        self.dma_cnt[key] = n + 1
        ticket = (key, 16 * (n + 1))
        self.streams[q].append(("dma", fn, waits, key))
        self._commit(ticket, reads, writes)
        self.n_ins += 1
        return ticket

    def wait_all(self, eng, include_bg=True):
        ts = []
        for e in COMPUTE:
            if self.cnt[e]:
                ts.append((e, self.cnt[e]))
        for key, n in self.dma_cnt.items():
            if key[1] == "bg" and not include_bg:
                continue
            ts.append((key, 16 * n))
        waits = self._need(eng, ts)
        self.streams[eng].append(("wait", None, waits, None))

    def barrier(self):
        for e in ("pe", "act", "dve", "pool", "sp"):
            self.wait_all(e, include_bg=False)

    def emit(self, nc, stack):
        sems = {}
        for e in COMPUTE:
            sems[e] = stack.enter_context(nc.semaphore("s_" + e))
        for q in ("sp", "act", "pool", "bg"):
            for i in range(N_DMA_SEMS):
                if ("dma", q, i) in self.dma_cnt:
                    sems[("dma", q, i)] = stack.enter_context(nc.semaphore("d_%s%d" % (q, i)))
        block = stack.enter_context(nc.Block())
        streams = self.streams

        def run(engobj, name):
            for kind, fn, waits, key in streams[name]:
                for k, v in waits:
                    engobj.wait_ge(sems[k], v)
                if kind == "op":
                    fn(engobj).then_inc(sems[key], 1)
                elif kind == "dma":
                    fn(engobj).then_inc(sems[key], 16)

        @block.sync
        def _(e):
            run(e, "sp")

        @block.tensor
        def _(e):
            run(e, "pe")

        @block.scalar
        def _(e):
            run(e, "act")

        @block.vector
        def _(e):
            run(e, "dve")

        @block.gpsimd
        def _(e):
            run(e, "pool")


class Prog:
    def __init__(self, nc, dbg=None):
        self.nc = nc
        self.S = Sched()
        self.dbg = dbg or {}
        self.uid = 0
        self.ps_rr = 0

    def name(self, p):
        self.uid += 1
        return "%s_%d" % (p, self.uid)


def build(stage=99, dbg_names=()):
    nc = bass.Bass("TRN2", target_bir_lowering=False)
    P = Prog(nc)
    S = P.S

    def din(name, shape, dt=F32):
        return nc.dram_tensor(name, list(shape), dt, kind="ExternalInput").ap()

    def dout(name, shape, dt=F32):
        return nc.dram_tensor(name, list(shape), dt, kind="ExternalOutput").ap()

    def dscr(name, shape, dt=F32):
        return nc.dram_tensor(name, list(shape), dt, kind="Internal").ap()

    x0 = din("x0", [NTOK, D])
    ckv = din("ckv", [NL, 2, 256, 128])
    cvec = din("cvec", [2, D])
    w_mod = din("w_mod", [NL, D, 6 * D]); b_mod = din("b_mod", [NL, 6 * D])
    gvec = din("gvec", [NL, 4, D])
    w_in = din("w_in", [NL, D, D_IN])
    w_glu = din("w_glu", [NL, 512, 2048])
    w_b_out = din("w_b_out", [NL, 512, D]); w_c_out = din("w_c_out", [NL, 512, D])
    w_o = din("w_o", [NL, D, D])
    w_up = din("w_up", [NL, D, 2 * D_FF]); w_down = din("w_down", [NL, D_FF, D])
    conv_w = din("conv_w", [NL, 3, 2 * D_FF]); conv_b = din("conv_b", [NL, 2 * D_FF])
    g_sgu = din("g_sgu", [NL, 512])
    w_spT = din("w_spT", [NL, 8, 128, 128])
    b_sp = din("b_sp", [NL, 8, 128])
    sink = din("sink", [NL, 8])
    ropec = din("ropec", [128, 32, 64]); ropes = din("ropes", [128, 32, 64])

    s5p = din("s5p", [NL, 2, 128, 1104])
    dskT = din("dskT", [NL, 128, 4])
    nst = dout("nst", [NL, 2, 4, 128, 32])
    y = dout("y", [NTOK, D])
    nkv = dout("nkv", [NL, 2, 1024, 128])

    wb_mod = dscr("wb_mod", [NL, D, 6 * D], BF16)
    wb_in = dscr("wb_in", [NL, D, D_IN], BF16)
    wb_glu = dscr("wb_glu", [NL, 512, 2048], BF16)
    wb_b = dscr("wb_b", [NL, 512, D], BF16); wb_c = dscr("wb_c", [NL, 512, D], BF16)
    wb_o = dscr("wb_o", [NL, D, D], BF16)
    wb_up = dscr("wb_up", [NL, D, 2 * D_FF], BF16); wb_down = dscr("wb_down", [NL, D_FF, D], BF16)
    modrows = dscr("modrows", [NL, 2, 6, D])
    yfs = dscr("yfs", [4, 128, NTOK])
    yas = dscr("yas", [4, 128, NTOK], BF16)
    x1 = dscr("x1", [NTOK, D])
    xmid = dscr("xmid", [NTOK, D])

    dbg = {}
    for ent in dbg_names:
        dbg[ent[0]] = dout("dbg_" + ent[0], ent[1], ent[2] if len(ent) > 2 else F32)

    with contextlib.ExitStack() as top:
        def sb(name, shape, dt=F32, stack=top):
            return stack.enter_context(nc.sbuf_tensor(name, list(shape), dt))

        def psum(name, shape, dt=F32, stack=top):
            return stack.enter_context(nc.psum_tensor(name, list(shape), dt))

        ident = sb("ident", [128, 128], BF16)
        ones_bf = sb("ones_bf", [128, 128], BF16)
        S.op("pool", lambda e: e.memset(ident[:], 0.0), writes=["ident"])
        S.op("pool", lambda e: e.affine_select(out=ident[:], in_=ident[:], pattern=[[-1, 128]],
                                               compare_op=ALU.not_equal, fill=1.0, base=0,
                                               channel_multiplier=1), reads=["ident"], writes=["ident"])
        S.op("pool", lambda e: e.memset(ones_bf[:], 1.0), writes=["ones_bf"])

        psT = [psum("psT%d" % i, [128, 8, 128], BF16) for i in range(2)]
        psF = [psum("psF%d" % i, [128, 512], F32) for i in range(6)]
        st_rr = {"T": 0, "F": 0, "W": 0}

        def next_psT():
            i = st_rr["T"]; st_rr["T"] = (i + 1) % 2
            return psT[i], "psT%d" % i

        def next_psF(wide=False):
            if wide:
                i = st_rr["W"]; st_rr["W"] = (i + 1) % 6
                return psF[i], "psF%d" % i
            i = st_rr["F"]; st_rr["F"] = (i + 1) % 4
            return psF[2 + i], "psF%d" % (2 + i)

        scr_keys = {}

        def SK(dst, l):
            return scr_keys[(id(dst), l)]

        def conv_w2(dst, src, rows, l):
            ks = scr_keys.setdefault((id(dst), l), [])
            for r0 in range(0, rows, 256):
                r1 = min(rows, r0 + 256)
                ks.append(("scr", id(dst), l, r0))
                S.dma("pool", lambda e, l=l, r0=r0, r1=r1: e.dma_start(out=dst[l, r0:r1, :], in_=src[l, r0:r1, :]),
                      writes=[("scr", id(dst), l, r0)], bg=True)
        for l in range(NL):
            conv_w2(wb_mod, w_mod, D, l)
        for l in range(NL):
            conv_w2(wb_in, w_in, D, l)
            conv_w2(wb_glu, w_glu, 512, l)
            conv_w2(wb_b, w_b_out, 512, l)
            conv_w2(wb_c, w_c_out, 512, l)
            conv_w2(wb_o, w_o, D, l)
            conv_w2(wb_up, w_up, D, l)
            conv_w2(wb_down, w_down, D_FF, l)

        WSLOT = 8 * 512
        NW = 4
        G = {}
        wr = {"i": 0}

        def alloc_wring(stack):
            G["wring"] = [sb(P.name("wring"), [128, WSLOT], BF16, stack) for i in range(NW)]

        def alloc_rope(stack):
            G["ropec_t"] = sb(P.name("ropec_t"), [128, 32, 64], F32, stack)
            G["ropes_t"] = sb(P.name("ropes_t"), [128, 32, 64], F32, stack)
            rc, rs = G["ropec_t"], G["ropes_t"]
            S.dma("sp", lambda e: e.dma_start(out=rc[:], in_=ropec), writes=["ropec"])
            S.dma("sp", lambda e: e.dma_start(out=rs[:], in_=ropes), writes=["ropes"])

        def load_w(src2d, kt, ncols, scr_key):
            i = wr["i"]; wr["i"] = (i + 1) % NW
            t = G["wring"][i]
            view = t[:, 0:kt * ncols].rearrange("p (k n) -> p k n", k=kt)
            S.dma("pool", lambda e: e.dma_start(out=view, in_=src2d.rearrange("(k p) n -> p k n", p=128)),
                  reads=list(scr_key), writes=["wring%d" % i])
            return view, "wring%d" % i

        def phase_mod(l):
            with contextlib.ExitStack() as ph:
                alloc_wring(ph)
                cT = sb(P.name("cT"), [128, 2, 8], F32, ph)
                scT = sb(P.name("scT"), [128, 2, 8], F32, ph)
                lrep = sb(P.name("lrep"), [128, 2, 8, 128], BF16, ph)
                mch = sb(P.name("mch"), [128, D], F32, ph)
                bch = sb(P.name("bch"), [128, D], F32, ph)
                gch = sb(P.name("gch"), [128, D], F32, ph)
                och = sb(P.name("och"), [128, D], F32, ph)
                k_c, k_sc, k_l, k_m, k_b, k_g, k_o = [P.name("k") for _ in range(7)]
                S.dma("sp", lambda e: e.dma_start(out=cT[:], in_=cvec.rearrange("r (k p) -> p r k", p=128),
                                                  allow_slow_non_contiguous=True), writes=[k_c])
                S.op("act", lambda e: e.activation(out=scT[:], in_=cT[:], func=AF.Silu), reads=[k_c], writes=[k_sc])
                for r in range(2):
                    for k in range(8):
                        S.op("dve", lambda e, r=r, k=k: e.tensor_scalar(out=lrep[:, r, k, :], in0=ones_bf[:],
                                                                          scalar1=scT[:, r, k:k + 1], scalar2=None,
                                                                          op0=ALU.mult),
                             reads=[k_sc, "ones_bf"], writes=[k_l])
                plan = [(1, 0, 0, True), (0, 1, None, False), (2, 2, 1, False), (4, 3, 2, True), (3, 4, None, False), (5, 5, 3, False)]
                for r in range(2):
                    for (mi, oi, gi, plus1) in plan:
                        S.dma("sp", lambda e, mi=mi: e.dma_start(out=bch[:], in_=b_mod[l:l + 1, mi * D:(mi + 1) * D].broadcast_to([128, D])), writes=[k_b])
                        if gi is not None:
                            S.dma("sp", lambda e, gi=gi: e.dma_start(out=gch[:], in_=gvec[l, gi:gi + 1, :].broadcast_to([128, D])), writes=[k_g])
                        for hh in range(2):
                            cb = mi * 2 + hh
                            wv, wk = load_w(wb_mod[l, :, cb * 512:(cb + 1) * 512], 8, 512, SK(wb_mod, l))
                            pt, pk = next_psF()
                            for k in range(8):
                                S.op("pe", lambda e, r=r, k=k, wv=wv, pt=pt: e.matmul(pt[:], lhsT=lrep[:, r, k, :], rhs=wv[:, k, :],
                                                                                       start=(k == 0), stop=(k == 7)),
                                     reads=[k_l, wk], writes=[pk])
                            S.op("dve", lambda e, hh=hh, pt=pt: e.tensor_tensor(out=mch[:, hh * 512:(hh + 1) * 512], in0=pt[:],
                                                                                in1=bch[:, hh * 512:(hh + 1) * 512], op=ALU.add),
                                 reads=[pk, k_b], writes=[k_m])
                        if gi is None:
                            S.op("dve", lambda e: e.tensor_copy(out=och[:], in_=mch[:]), reads=[k_m], writes=[k_o])
                        elif plus1:
                            S.op("dve", lambda e: e.scalar_tensor_tensor(out=och[:], in0=mch[:], scalar=1.0, in1=gch[:], op0=ALU.add, op1=ALU.mult),
                                 reads=[k_m, k_g], writes=[k_o])
                        else:
                            S.op("dve", lambda e: e.tensor_tensor(out=och[:], in0=mch[:], in1=gch[:], op=ALU.mult), reads=[k_m, k_g], writes=[k_o])
                        S.dma("sp", lambda e, r=r, oi=oi: e.dma_start(out=modrows[l, r, oi:oi + 1, :], in_=och[0:1, :]),
                              reads=[k_o], writes=["modrows"])

        xt_ring = [sb("xt%d" % i, [128, D], F32) for i in range(2)]
        junk = sb("junk", [128, D], BF16)
        tmpf = sb("tmpf", [128, D], F32)
        hb_ring = [sb("hb%d" % i, [128, D], BF16) for i in range(2)]
        ss_t = sb("ss_t", [128, 8], F32)
        rows = sb("rows", [128, 3, D], F32)
        hT = sb("hT", [128, 8, 512], BF16)
        rr = {"xt": 0, "hb": 0}
        cur = {"rows": None}

        def load_rows(l, path, half=0):
            if cur["rows"] == (l, path, half):
                return
            cur["rows"] = (l, path, half)
            S.dma("sp", lambda e: e.dma_start(out=rows[:], in_=modrows[l, path:path + 1, 3 * half:3 * half + 3, :].broadcast_to([128, 3, D])),
                  reads=["modrows"], writes=["rows"])

        def rms_rstd(src_ap, src_key, col):
            S.op("act", lambda e: e.activation(out=junk[:], in_=src_ap, func=AF.Square, accum_out=ss_t[:, col:col + 1]),
                 reads=[src_key], writes=["junk", ("ss", col)])
            S.op("dve", lambda e: e.tensor_scalar(out=ss_t[:, col:col + 1], in0=ss_t[:, col:col + 1], scalar1=1.0 / D,
                                                  scalar2=EPS, op0=ALU.mult, op1=ALU.add),
                 reads=[("ss", col)], writes=[("ss", col)])
            S.op("act", lambda e: e.activation(out=ss_t[:, col:col + 1], in_=ss_t[:, col:col + 1], func=AF.Sqrt),
                 reads=[("ss", col)], writes=[("ss", col)])
            S.op("dve", lambda e: e.reciprocal(out=ss_t[:, col:col + 1], in_=ss_t[:, col:col + 1]),
                 reads=[("ss", col)], writes=[("ss", col)])

        def norm_block(src, tok0, ia, ib, nchunks=4, dstT=None, dst_key="hT", col0=0):
            dstT = hT if dstT is None else dstT
            stt = {}

            def s1(c):
                i = rr["xt"]; rr["xt"] = 1 - i
                xt = xt_ring[i]; xk = "xt%d" % i
                t0 = tok0 + c * 128
                S.dma("sp", lambda e, xt=xt, t0=t0: e.dma_start(out=xt[:], in_=src[t0:t0 + 128, :]),
                      reads=[("dram", id(src))], writes=[xk])
                col = 4 + c % 2
                rms_rstd(xt[:], xk, col)
                stt[c] = (xt, xk, col)

            def s2(c):
                xt, xk, col = stt[c]
                S.op("dve", lambda e, xt=xt, col=col: e.scalar_tensor_tensor(out=tmpf[:], in0=xt[:], scalar=ss_t[:, col:col + 1], in1=rows[:, ia, :],
                                                                            op0=ALU.mult, op1=ALU.mult),
                     reads=[xk, ("ss", col), "rows"], writes=["tmpf"])
                j = rr["hb"]; rr["hb"] = 1 - j
                hb = hb_ring[j]; hk = "hb%d" % j
                S.op("dve", lambda e, hb=hb: e.tensor_tensor(out=hb[:], in0=tmpf[:], in1=rows[:, ib, :], op=ALU.add),
                     reads=["tmpf", "rows"], writes=[hk])
                pt, pk = next_psT()
                for k in range(8):
                    S.op("pe", lambda e, k=k, hb=hb, pt=pt: e.transpose(out=pt[:, k, :], in_=hb[:, k * 128:(k + 1) * 128], identity=ident[:]),
                         reads=[hk, "ident"], writes=[pk])
                cc = col0 + c * 128
                S.op("act", lambda e, pt=pt, cc=cc: e.activation(out=dstT[:, :, cc:cc + 128], in_=pt[:], func=AF.Copy),
                     reads=[pk], writes=[dst_key])

            s1(0)
            for c in range(nchunks):
                if c + 1 < nchunks:
                    s1(c + 1)
                s2(c)

        uT_all = sb("uT_all", [128, 4, NTOK], BF16)
        KT_all = sb("KT_all", [128, NTOK], BF16)
        V_all = sb("V_all", [128, 40, 130], BF16)
        cKT = sb("cKT", [128, 256], BF16)
        cV = sb("cV", [128, 2, 130], BF16)
        S.op("pool", lambda e: e.memset(V_all[:], 1.0), writes=["V_all"])
        S.op("pool", lambda e: e.memset(cV[:], 1.0), writes=["cV"])
        kvf = sb("kvf", [128, 256], F32)
        ropt1 = sb("ropt1", [128, 512], F32)
        ropt2 = sb("ropt2", [128, 512], F32)
        kb = sb("kb", [128, 128], BF16)

        def rope(src_ap, src_key, H, cs, out_ap, out_key):
            n = H * 64
            cosb = G["ropec_t"][:, cs:cs + 1, :].broadcast_to([128, H, 64])
            s3 = src_ap.rearrange("p (h e) -> p h e", e=64)
            t1 = ropt1[:, 0:n].rearrange("p (h e) -> p h e", e=64)
            S.op("dve", lambda e: e.tensor_tensor(out=t1, in0=s3, in1=cosb, op=ALU.mult),
                 reads=[src_key, "ropec"], writes=["ropt1"])
            s5 = src_ap.rearrange("p (h b f e) -> p h b f e", b=2, f=2, e=16)
            t5 = ropt2[:, 0:n].rearrange("p (h b f e) -> p h b f e", b=2, f=2, e=16)
            sn5 = G["ropes_t"][:, cs:cs + 1, :].broadcast_to([128, H, 64]).rearrange("p h (b f e) -> p h b f e", b=2, f=2, e=16)
            for f in range(2):
                S.op("pool", lambda e, f=f: e.tensor_tensor(out=t5[:, :, :, f, :], in0=s5[:, :, :, 1 - f, :], in1=sn5[:, :, :, f, :], op=ALU.mult),
                     reads=[src_key, "ropes"], writes=[("ropt2", f)])
            S.op("dve", lambda e: e.tensor_tensor(out=out_ap, in0=ropt1[:, 0:n], in1=ropt2[:, 0:n], op=ALU.add),
                 reads=["ropt1", ("ropt2", 0), ("ropt2", 1)], writes=[out_key])

        def phase_A(l):
            src = x0 if l == 0 else x1
            with contextlib.ExitStack() as ph:
                alloc_rope(ph)
                wA = sb(P.name("wA"), [128, 8, 768], BF16, ph)
                kA = P.name("kA")
                S.dma("sp", lambda e: e.dma_start(out=wA[:, :, 0:512], in_=wb_in[l, :, 0:512].rearrange("(k p) n -> p k n", p=128)),
                      reads=SK(wb_in, l), writes=[kA])
                S.dma("sp", lambda e: e.dma_start(out=wA[:, :, 512:768], in_=wb_in[l, :, OFF_K:OFF_K + 256].rearrange("(k p) n -> p k n", p=128)),
                      reads=SK(wb_in, l), writes=[kA])
                for kv in range(2):
                    for t in range(2):
                        S.dma("sp", lambda e, kv=kv, t=t: e.dma_start(out=kvf[:, 0:128], in_=ckv[l, kv, t * 128:(t + 1) * 128, :]), writes=["kvf"])
                        if kv == 0:
                            S.op("dve", lambda e: e.tensor_copy(out=kb[:], in_=kvf[:, 0:128]), reads=["kvf"], writes=["kb"])
                            pt, pk = next_psT()
                            S.op("pe", lambda e, pt=pt: e.transpose(out=pt[:, 0, :], in_=kb[:], identity=ident[:]), reads=["kb", "ident"], writes=[pk])
                            S.op("act", lambda e, pt=pt, t=t: e.activation(out=cKT[:, t * 128:(t + 1) * 128], in_=pt[:, 0, :], func=AF.Copy),
                                 reads=[pk], writes=["cKT"])
                        else:
                            S.op("dve", lambda e, t=t: e.tensor_copy(out=cV[:, t, :].rearrange("p (h e) -> p h e", e=65)[:, :, 0:64],
                                                                      in_=kvf[:, 0:128].rearrange("p (h e) -> p h e", e=64)),
                                 reads=["kvf"], writes=["cV"])
                for blk in range(NBLK):
                    path = 0 if blk < 8 else 1
                    load_rows(l, path)
                    tok0 = blk * 512
                    norm_block(src, tok0, 0, 1)
                    for m in range(4):
                        pt, pk = next_psF()
                        for k in range(8):
                            S.op("pe", lambda e, m=m, k=k, pt=pt: e.matmul(pt[:], lhsT=wA[:, k, m * 128:(m + 1) * 128], rhs=hT[:, k, :],
                                                                          start=(k == 0), stop=(k == 7)),
                                 reads=[kA, "hT"], writes=[pk])
                        S.op("act", lambda e, m=m, pt=pt, tok0=tok0: e.activation(out=uT_all[:, m, tok0:tok0 + 512], in_=pt[:], func=AF.Copy),
                             reads=[pk], writes=["uT_all"])
                    for c in range(4):
                        cg = blk * 4 + c
                        pt, pk = next_psF()
                        for k in range(8):
                            S.op("pe", lambda e, c=c, k=k, pt=pt: e.matmul(pt[:, 0:256], lhsT=hT[:, k, c * 128:(c + 1) * 128], rhs=wA[:, k, 512:768],
                                                                          start=(k == 0), stop=(k == 7)),
                                 reads=[kA, "hT"], writes=[pk])
                        S.op("act", lambda e, pt=pt: e.activation(out=kvf[:], in_=pt[:, 0:256], func=AF.Copy), reads=[pk], writes=["kvf"])
                        if path == 1:
                            pt0 = (cg - 32) * 128
                            for kv in range(2):
                                S.dma("sp", lambda e, kv=kv, pt0=pt0: e.dma_start(out=nkv[l, kv, pt0:pt0 + 128, :], in_=kvf[:, kv * 128:(kv + 1) * 128]),
                                      reads=["kvf"], writes=["nkv"])
                            S.op("dve", lambda e: e.tensor_copy(out=kb[:], in_=kvf[:, 0:128]), reads=["kvf"], writes=["kb"])
                        else:
                            rope(kvf[:, 0:128], "kvf", 2, cg, kb[:], "kb")
                        ptT, pkT = next_psT()
                        S.op("pe", lambda e, ptT=ptT: e.transpose(out=ptT[:, 0, :], in_=kb[:], identity=ident[:]), reads=["kb", "ident"], writes=[pkT])
                        S.op("act", lambda e, ptT=ptT, cg=cg: e.activation(out=KT_all[:, cg * 128:(cg + 1) * 128], in_=ptT[:, 0, :], func=AF.Copy),
                             reads=[pkT], writes=["KT_all"])
                        S.op("dve", lambda e, cg=cg: e.tensor_copy(out=V_all[:, cg, :].rearrange("p (h e) -> p h e", e=65)[:, :, 0:64],
                                                                    in_=kvf[:, 128:256].rearrange("p (h e) -> p h e", e=64)),
                             reads=["kvf"], writes=["V_all"])

        NCOL = TS + 2 + 4 * 258
        h2s = dscr("h2s", [8, 128, NCOL], BF16)

        def seq_col(cg):
            if cg < 32:
                return 1 + cg * 128
            p_ = (cg - 32) // 2
            return TS + 2 + p_ * 258 + 1 + ((cg - 32) % 2) * 128

        maskp = sb("maskp", [128, 128], BF16)
        maskn = sb("maskn", [128, 128], BF16)
        S.op("pool", lambda e: e.memset(maskp[:], 1.0), writes=["maskp"])
        S.op("pool", lambda e: e.affine_select(out=maskp[:], in_=maskp[:], pattern=[[-1, 128]], compare_op=ALU.is_ge, fill=0.0,
                                               base=0, channel_multiplier=1), reads=["maskp"], writes=["maskp"])
        S.op("pool", lambda e: e.memset(maskn[:], 1.0), writes=["maskn"])
        S.op("pool", lambda e: e.affine_select(out=maskn[:], in_=maskn[:], pattern=[[1, 128]], compare_op=ALU.is_ge, fill=0.0,
                                               base=0, channel_multiplier=-1), reads=["maskn"], writes=["maskn"])

        def phase_C(l):
            src = x0 if l == 0 else x1
            with contextlib.ExitStack() as ph:
                alloc_wring(ph)
                alloc_rope(ph)
                yaT = sb(P.name("yaT"), [128, 4, 512], BF16, ph)
                ybT = sb(P.name("ybT"), [128, 4, 512], BF16, ph)
                oT = sb(P.name("oT"), [128, 4, 512], BF16, ph)
                qT = sb(P.name("qT"), [128, 4, 128], BF16, ph)
                pT = [sb(P.name("pT"), [128, 512], BF16, ph) for _ in range(2)]
                fA = sb(P.name("fA"), [128, 512], F32, ph)
                fB = sb(P.name("fB"), [128, 512], F32, ph)
                fC = sb(P.name("fC"), [128, 512], F32, ph)
                bA = sb(P.name("bA"), [128, 512], BF16, ph)
                bB = sb(P.name("bB"), [128, 512], BF16, ph)
                macc = sb(P.name("macc"), [128, 8, 512], F32, ph)
                mgT = sb(P.name("mgT"), [128, 8, 512], BF16, ph)
                wsp = sb(P.name("wsp"), [128, 8, 128], BF16, ph)
                bspT = sb(P.name("bspT"), [128, 8], F32, ph)
                gsg = sb(P.name("gsg"), [128, 512], BF16, ph)
                snk = sb(P.name("snk"), [128, 8], F32, ph)
                st = sb(P.name("st"), [128, 16], F32, ph)
                rows2 = sb(P.name("rows2"), [128, 2, D], BF16, ph)
                kk = {n: P.name(n) for n in ("rows2", "yaT", "ybT", "oT", "qT", "pT0", "pT1", "fA", "fB", "fC", "bA", "bB", "macc", "mgT",
                                             "wsp", "wspf", "bspT", "gsg", "snk", "st")}
                S.dma("pool", lambda e: e.dma_start(out=wsp[:], in_=w_spT[l].rearrange("h s q -> s h q")), writes=[kk["wsp"]])
                S.dma("sp", lambda e: e.dma_start(out=bspT[:], in_=b_sp[l].rearrange("h q -> q h"), allow_slow_non_contiguous=True), writes=[kk["bspT"]])
                S.dma("pool", lambda e: e.dma_start(out=gsg[:], in_=g_sgu[l:l + 1, :].broadcast_to([128, 512])), writes=[kk["gsg"]])
                S.dma("sp", lambda e: e.dma_start(out=snk[:], in_=sink[l:l + 1, :].broadcast_to([128, 8])), writes=[kk["snk"]])
                S.op("act", lambda e: e.activation(out=snk[:], in_=snk[:], func=AF.Exp), reads=[kk["snk"]], writes=[kk["snk"]])
                if stage < 5:
                    S.op("pool", lambda e: e.memset(yaT[:], 0.0), writes=[kk["yaT"]])
                scr_in = SK(wb_in, l)

                for blk in range(NBLK):
                    path = 0 if blk < 8 else 1
                    load_rows(l, path, 0)
                    if blk in (0, 8):
                        S.dma("pool", lambda e, path=path: e.dma_start(out=rows2[:], in_=modrows[l, path:path + 1, 3:5, :].broadcast_to([128, 2, D])),
                              reads=["modrows"], writes=[kk["rows2"]])
                    tok0 = blk * 512
                    norm_block(src, tok0, 0, 1)
                    if stage >= 5:
                        for k4 in range(4):
                            S.dma("sp", lambda e, k4=k4, tok0=tok0: e.dma_start(out=yaT[:, k4, :], in_=yas[k4, :, tok0:tok0 + 512]), reads=["yas"], writes=[kk["yaT"]])
                    wu, wuk = load_w(wb_in[l, :, 512:1024], 8, 512, scr_in)
                    wv, wvk = load_w(wb_in[l, :, 1024:1536], 8, 512, scr_in)
                    uvp = {}

                    def sgu_proj(c, wu=wu, wv=wv, wuk=wuk, wvk=wvk):
                        pu, puk = next_psF(True)
                        pv, pvk = next_psF(True)
                        for k in range(8):
                            S.op("pe", lambda e, c=c, k=k, pv=pv, wv=wv: e.matmul(pv[:], lhsT=hT[:, k, c * 128:(c + 1) * 128], rhs=wv[:, k, :], start=(k == 0), stop=(k == 7)),
                                 reads=["hT", wvk], writes=[pvk])
                        for k in range(8):
                            S.op("pe", lambda e, c=c, k=k, pu=pu, wu=wu: e.matmul(pu[:], lhsT=hT[:, k, c * 128:(c + 1) * 128], rhs=wu[:, k, :], start=(k == 0), stop=(k == 7)),
                                 reads=["hT", wuk], writes=[puk])
                        uvp[c] = (pu, puk, pv, pvk)
                    sgu_proj(0)
                    for c in range(4):
                        if c + 1 < 4:
                            sgu_proj(c + 1)
                        pu, puk, pv, pvk = uvp[c]
                        S.op("act", lambda e, pv=pv: e.activation(out=fA[:], in_=pv[:], func=AF.Gelu, accum_out=st[:, 0:1]), reads=[pvk], writes=[kk["fA"], kk["st"]])
                        S.op("act", lambda e: e.activation(out=junk[:, 0:512], in_=fA[:], func=AF.Square, accum_out=st[:, 1:2]), reads=[kk["fA"]], writes=["junk", kk["st"]])
                        S.op("dve", lambda e: e.tensor_scalar(out=st[:, 2:3], in0=st[:, 0:1], scalar1=1.0 / 512, scalar2=None, op0=ALU.mult), reads=[kk["st"]], writes=[kk["st"]])
                        S.op("dve", lambda e: e.tensor_tensor(out=st[:, 3:4], in0=st[:, 2:3], in1=st[:, 2:3], op=ALU.mult), reads=[kk["st"]], writes=[kk["st"]])
                        S.op("dve", lambda e: e.scalar_tensor_tensor(out=st[:, 4:5], in0=st[:, 1:2], scalar=1.0 / 512, in1=st[:, 3:4], op0=ALU.mult, op1=ALU.subtract), reads=[kk["st"]], writes=[kk["st"]])
                        S.op("dve", lambda e: e.tensor_scalar(out=st[:, 4:5], in0=st[:, 4:5], scalar1=EPS, scalar2=None, op0=ALU.add), reads=[kk["st"]], writes=[kk["st"]])
                        S.op("act", lambda e: e.activation(out=st[:, 4:5], in_=st[:, 4:5], func=AF.Sqrt), reads=[kk["st"]], writes=[kk["st"]])
                        S.op("dve", lambda e: e.reciprocal(out=st[:, 5:6], in_=st[:, 4:5]), reads=[kk["st"]], writes=[kk["st"]])
                        S.op("dve", lambda e: e.scalar_tensor_tensor(out=st[:, 6:7], in0=st[:, 2:3], scalar=-1.0, in1=st[:, 5:6], op0=ALU.mult, op1=ALU.mult), reads=[kk["st"]], writes=[kk["st"]])
                        S.op("act", lambda e: e.activation(out=fB[:], in_=fA[:], func=AF.Identity, bias=st[:, 6:7], scale=st[:, 5:6]), reads=[kk["fA"], kk["st"]], writes=[kk["fB"]])
                        S.op("dve", lambda e: e.tensor_tensor(out=bA[:], in0=fB[:], in1=gsg[:], op=ALU.mult), reads=[kk["fB"], kk["gsg"]], writes=[kk["bA"]])
                        pm, pmk = next_psF(True)
                        for h in range(8):
                            S.op("pe", lambda e, h=h, pm=pm: e.matmul(pm[:, h * 64:(h + 1) * 64], lhsT=wsp[:, h, :], rhs=bA[:, h * 64:(h + 1) * 64], start=True, stop=True),
                                 reads=[kk["wsp"], kk["bA"]], writes=[pmk])
                        S.op("act", lambda e, pu=pu: e.activation(out=fC[:], in_=pu[:], func=AF.Gelu), reads=[puk], writes=[kk["fC"]])
                        S.op("dve", lambda e, pm=pm: e.tensor_tensor(out=fB[:].rearrange("p (h e) -> p h e", e=64), in0=pm[:].rearrange("p (h e) -> p h e", e=64),
                                                                     in1=bspT[:, :].unsqueeze(2).broadcast_to([128, 8, 64]), op=ALU.add),
                             reads=[pmk, kk["bspT"]], writes=[kk["fB"]])
                        S.op("dve", lambda e: e.tensor_tensor(out=bB[:], in0=fB[:], in1=fC[:], op=ALU.mult), reads=[kk["fB"], kk["fC"]], writes=[kk["bB"]])
                        ptT, pkT = next_psT()
                        for j in range(4):
                            S.op("pe", lambda e, j=j, ptT=ptT: e.transpose(out=ptT[:, j, :], in_=bB[:, j * 128:(j + 1) * 128], identity=ident[:]), reads=[kk["bB"], "ident"], writes=[pkT])
                        S.op("act", lambda e, ptT=ptT, c=c: e.activation(out=ybT[:, :, c * 128:(c + 1) * 128], in_=ptT[:, 0:4, :], func=AF.Copy), reads=[pkT], writes=[kk["ybT"]])
                    wq, wqk = load_w(wb_in[l, :, OFF_Q:OFF_Q + 512], 8, 512, scr_in)
                    qTs = [qT[:], junk[:, 0:512].rearrange("p (g t) -> p g t", g=4)]
                    qTk = [kk["qT"], "junk"]

                    def qprep_a(c, blk=blk, path=path, wq=wq, wqk=wqk):
                        cg = blk * 4 + c
                        pq, pqk = next_psF()
                        for k in range(8):
                            S.op("pe", lambda e, c=c, k=k, pq=pq, wq=wq: e.matmul(pq[:], lhsT=hT[:, k, c * 128:(c + 1) * 128], rhs=wq[:, k, :], start=(k == 0), stop=(k == 7)),
                                 reads=["hT", wqk], writes=[pqk])
                        S.op("act", lambda e, pq=pq: e.activation(out=fA[:], in_=pq[:], func=AF.Copy), reads=[pqk], writes=[kk["fA"]])
                        qperm = bA[:].rearrange("p (g k e) -> p k g e", g=4, k=2, e=64)
                        if path == 0:
                            rope(fA[:], kk["fA"], 8, cg, fB[:], kk["fB"])
                            S.op("dve", lambda e, qperm=qperm: e.tensor_copy(out=qperm, in_=fB[:].rearrange("p (k g e) -> p k g e", k=2, g=4, e=64)), reads=[kk["fB"]], writes=[kk["bA"]])
                        else:
                            S.op("dve", lambda e, qperm=qperm: e.tensor_copy(out=qperm, in_=fA[:].rearrange("p (k g e) -> p k g e", k=2, g=4, e=64)), reads=[kk["fA"]], writes=[kk["bA"]])

                    def qprep_b(c):
                        ptT, pkT = next_psT()
                        for j in range(4):
                            S.op("pe", lambda e, j=j, ptT=ptT: e.transpose(out=ptT[:, j, :], in_=bA[:, j * 128:(j + 1) * 128], identity=ident[:]), reads=[kk["bA"], "ident"], writes=[pkT])
                        qd = qTs[c % 2]
                        S.op("act", lambda e, ptT=ptT, qd=qd: e.activation(out=qd, in_=ptT[:, 0:4, :], func=AF.Copy), reads=[pkT], writes=[qTk[c % 2]])
                    qprep_a(0); qprep_b(0)
                    for c in range(4):
                        cg = blk * 4 + c
                        if c + 1 < 4:
                            qprep_a(c + 1)
                        qcur = qTs[c % 2]; qcurk = qTk[c % 2]
                        tiles = []
                        if path == 0:
                            if cg > 0:
                                tiles.append(("a", cg - 1, "p"))
                            tiles.append(("a", cg, None))
                            if cg < 31:
                                tiles.append(("a", cg + 1, "n"))
                            tiles.append(("c", 0, None)); tiles.append(("c", 1, None))
                        else:
                            base = (cg // 2) * 2
                            tiles.append(("a", base, None)); tiles.append(("a", base + 1, None))
                        for kvh in range(2):
                            po = psF[kvh]; pok = "psF%d" % kvh
                            def emit_scores(ti, kvh=kvh, tiles=tiles, qcur=qcur, qcurk=qcurk):
                                kind, idx, msk = tiles[ti]
                                psc, psck = next_psF()
                                if kind == "a":
                                    kt_ap = KT_all[64 * kvh:64 * kvh + 64, idx * 128:(idx + 1) * 128]; ktk = "KT_all"
                                else:
                                    kt_ap = cKT[64 * kvh:64 * kvh + 64, idx * 128:(idx + 1) * 128]; ktk = "cKT"
                                S.op("pe", lambda e, psc=psc, kt_ap=kt_ap, kvh=kvh, qcur=qcur: e.matmul(psc[:], lhsT=kt_ap, rhs=qcur[64 * kvh:64 * kvh + 64, :, :], start=True, stop=True),
                                     reads=[ktk, qcurk], writes=[psck])
                                return psc, psck
                            nxt = emit_scores(0)
                            for ti, (kind, idx, msk) in enumerate(tiles):
                                psc, psck = nxt
                                if ti + 1 < len(tiles):
                                    nxt = emit_scores(ti + 1)
                                if kind == "a":
                                    v_ap = V_all[:, idx, kvh * 65:(kvh + 1) * 65]; vk = "V_all"
                                else:
                                    v_ap = cV[:, idx, kvh * 65:(kvh + 1) * 65]; vk = "cV"
                                pi = ti % 2
                                pt_ = pT[pi]; ptk = kk["pT%d" % pi]
                                S.op("act", lambda e, psc=psc, pt_=pt_: e.activation(out=pt_[:], in_=psc[:], func=AF.Exp, scale=0.125), reads=[psck], writes=[ptk])
                                if msk is not None:
                                    mk = maskp if msk == "p" else maskn
                                    S.op("pool", lambda e, pt_=pt_, mk=mk: e.tensor_tensor(out=pt_[:].rearrange("p (g q) -> p g q", g=4), in0=pt_[:].rearrange("p (g q) -> p g q", g=4),
                                                                                            in1=mk[:, :].unsqueeze(1).broadcast_to([128, 4, 128]), op=ALU.mult),
                                         reads=[ptk, "maskp", "maskn"], writes=[ptk])
                                for g in range(4):
                                    S.op("pe", lambda e, g=g, po=po, pt_=pt_, v_ap=v_ap, ti=ti, nt=len(tiles): e.matmul(po[:, g * 65:(g + 1) * 65], lhsT=pt_[:, g * 128:(g + 1) * 128], rhs=v_ap,
                                                                                                     start=(ti == 0 and g == 0), stop=(ti == nt - 1 and g == 3), skip_group_check=True),
                                         reads=[ptk, vk], writes=[pok])
                            po3 = po[:, 0:260].rearrange("p (g e) -> p g e", e=65)
                            S.op("dve", lambda e, po3=po3, kvh=kvh: e.tensor_tensor(out=st[:, 8:12], in0=po3[:, :, 64], in1=snk[:, kvh * 4:(kvh + 1) * 4], op=ALU.add),
                                 reads=[pok, kk["snk"]], writes=[kk["st"]])
                            S.op("dve", lambda e: e.reciprocal(out=st[:, 12:16], in_=st[:, 8:12]), reads=[kk["st"]], writes=[kk["st"]])
                            S.op("dve", lambda e, po3=po3, kvh=kvh: e.tensor_tensor(out=bB[:, kvh * 256:(kvh + 1) * 256].rearrange("p (g e) -> p g e", e=64), in0=po3[:, :, 0:64],
                                                                                   in1=st[:, 12:16].unsqueeze(2).broadcast_to([128, 4, 64]), op=ALU.mult),
                                 reads=[pok, kk["st"]], writes=[kk["bB"]])
                        ptT, pkT = next_psT()
                        for j in range(4):
                            S.op("pe", lambda e, j=j, ptT=ptT: e.transpose(out=ptT[:, j, :], in_=bB[:, j * 128:(j + 1) * 128], identity=ident[:]), reads=[kk["bB"], "ident"], writes=[pkT])
                        S.op("act", lambda e, ptT=ptT, c=c: e.activation(out=oT[:, :, c * 128:(c + 1) * 128], in_=ptT[:, 0:4, :], func=AF.Copy), reads=[pkT], writes=[kk["oT"]])
                        if c + 1 < 4:
                            qprep_b(c + 1)
                    if l == 0 and blk == 0:
                        dump("ybT", ybT[:], kk["ybT"]); dump("oT", oT[:], kk["oT"])
                    if l == 0 and blk == 8:
                        dump("ybTp", ybT[:], kk["ybT"]); dump("oTp", oT[:], kk["oT"])
                    for jj in range(2):
                        wz2, wz2k = load_w(wb_glu[l, :, 1024 + jj * 512:1024 + (jj + 1) * 512], 4, 512, SK(wb_glu, l))
                        wz1, wz1k = load_w(wb_glu[l, :, jj * 512:(jj + 1) * 512], 4, 512, SK(wb_glu, l))
                        wg, wgk = load_w(wb_in[l, :, OFF_G + jj * 512:OFF_G + (jj + 1) * 512], 8, 512, scr_in)
                        for j in range(4):
                            jt = jj * 4 + j
                            p2, p2k = next_psF(True); p1, p1k = next_psF(True); pg, pgk = next_psF(True)
                            for k in range(4):
                                S.op("pe", lambda e, j=j, k=k, p2=p2, wz2=wz2, tok0=tok0: e.matmul(p2[:], lhsT=wz2[:, k, j * 128:(j + 1) * 128], rhs=yaT[:, k, :], start=(k == 0), stop=(k == 3)), reads=[wz2k, kk["yaT"]], writes=[p2k])
                            for k in range(4):
                                S.op("pe", lambda e, j=j, k=k, p1=p1, wz1=wz1, tok0=tok0: e.matmul(p1[:], lhsT=wz1[:, k, j * 128:(j + 1) * 128], rhs=yaT[:, k, :], start=(k == 0), stop=(k == 3)), reads=[wz1k, kk["yaT"]], writes=[p1k])
                            for k in range(8):
                                S.op("pe", lambda e, j=j, k=k, pg=pg, wg=wg: e.matmul(pg[:], lhsT=wg[:, k, j * 128:(j + 1) * 128], rhs=hT[:, k, :], start=(k == 0), stop=(k == 7)), reads=[wgk, "hT"], writes=[pgk])
                            S.op("act", lambda e, p2=p2: e.activation(out=fA[:], in_=p2[:], func=AF.Sigmoid), reads=[p2k], writes=[kk["fA"]])
                            S.op("act", lambda e, pg=pg: e.activation(out=fB[:], in_=pg[:], func=AF.Sigmoid), reads=[pgk], writes=[kk["fB"]])
                            S.op("dve", lambda e, p1=p1: e.tensor_tensor(out=fA[:], in0=p1[:], in1=fA[:], op=ALU.mult), reads=[p1k, kk["fA"]], writes=[kk["fA"]])
                            S.op("dve", lambda e, jt=jt: e.tensor_tensor(out=macc[:, jt, :], in0=fA[:], in1=fB[:], op=ALU.mult), reads=[kk["fA"], kk["fB"]], writes=[kk["macc"]])
                        for bi, (wsrc, actT, akey) in enumerate(((wb_b, ybT, kk["ybT"]), (wb_c, oT, kk["oT"]))):
                            wy, wyk = load_w(wsrc[l, :, jj * 512:(jj + 1) * 512], 4, 512, SK(wsrc, l))
                            g0 = OFF_G + (bi + 1) * 1024 + jj * 512
                            wg, wgk = load_w(wb_in[l, :, g0:g0 + 512], 8, 512, scr_in)
                            for j in range(4):
                                jt = jj * 4 + j
                                py, pyk = next_psF(True); pg, pgk = next_psF(True)
                                for k in range(4):
                                    S.op("pe", lambda e, j=j, k=k, py=py, wy=wy, actT=actT: e.matmul(py[:], lhsT=wy[:, k, j * 128:(j + 1) * 128], rhs=actT[:, k, :], start=(k == 0), stop=(k == 3)), reads=[wyk, akey], writes=[pyk])
                                for k in range(8):
                                    S.op("pe", lambda e, j=j, k=k, pg=pg, wg=wg: e.matmul(pg[:], lhsT=wg[:, k, j * 128:(j + 1) * 128], rhs=hT[:, k, :], start=(k == 0), stop=(k == 7)), reads=[wgk, "hT"], writes=[pgk])
                                S.op("act", lambda e, pg=pg: e.activation(out=fB[:], in_=pg[:], func=AF.Sigmoid), reads=[pgk], writes=[kk["fB"]])
                                S.op("dve", lambda e, py=py: e.tensor_tensor(out=fA[:], in0=py[:], in1=fB[:], op=ALU.mult), reads=[pyk, kk["fB"]], writes=[kk["fA"]])
                                if bi == 0:
                                    S.op("dve", lambda e, jt=jt: e.tensor_tensor(out=macc[:, jt, :], in0=macc[:, jt, :], in1=fA[:], op=ALU.add), reads=[kk["fA"], kk["macc"]], writes=[kk["macc"]])
                                else:
                                    S.op("dve", lambda e, jt=jt: e.tensor_tensor(out=mgT[:, jt, :], in0=macc[:, jt, :], in1=fA[:], op=ALU.add), reads=[kk["fA"], kk["macc"]], writes=[kk["mgT"]])
                    if l == 0 and blk == 0:
                        dump("mgT", mgT[:], kk["mgT"])
                    wo0, wo0k = load_w(wb_o[l, :, 0:512], 8, 512, SK(wb_o, l))
                    wo1, wo1k = load_w(wb_o[l, :, 512:1024], 8, 512, SK(wb_o, l))
                    wo_ps = {}

                    def emit_wo(c):
                        lst = []
                        for hh, (wo, wok) in enumerate(((wo0, wo0k), (wo1, wo1k))):
                            pm_, pmk_ = next_psF(True)
                            for k in range(8):
                                S.op("pe", lambda e, c=c, k=k, pm_=pm_, wo=wo: e.matmul(pm_[:], lhsT=mgT[:, k, c * 128:(c + 1) * 128], rhs=wo[:, k, :], start=(k == 0), stop=(k == 7)), reads=[wok, kk["mgT"]], writes=[pmk_])
                            lst.append((pm_, pmk_))
                        wo_ps[c] = lst
                    emit_wo(0)
                    xpre = {}

                    def load_x(c, tok0=tok0):
                        i = rr["xt"]; rr["xt"] = 1 - i
                        xt = xt_ring[i]; xk = "xt%d" % i
                        t0 = tok0 + c * 128
                        S.dma("sp", lambda e, xt=xt, t0=t0: e.dma_start(out=xt[:], in_=src[t0:t0 + 128, :]), reads=[("dram", id(src))], writes=[xk])
                        xpre[c] = (xt, xk)
                    load_x(0)
                    for c in range(4):
                        t0 = tok0 + c * 128
                        if c + 1 < 4:
                            emit_wo(c + 1)
                            load_x(c + 1)
                        xt, xk = xpre[c]
                        (pm0, pmk0), (pm1, pmk1) = wo_ps[c]
                        S.op("act", lambda e, pm0=pm0: e.activation(out=junk[:, 0:512], in_=pm0[:], func=AF.Square, accum_out=ss_t[:, 1:2]), reads=[pmk0], writes=["junk", ("ss", 1)])
                        S.op("act", lambda e, pm1=pm1: e.activation(out=junk[:, 512:1024], in_=pm1[:], func=AF.Square, accum_out=ss_t[:, 6:7]), reads=[pmk1], writes=["junk", ("ss", 6)])
                        S.op("dve", lambda e: e.tensor_tensor(out=ss_t[:, 1:2], in0=ss_t[:, 1:2], in1=ss_t[:, 6:7], op=ALU.add), reads=[("ss", 1), ("ss", 6)], writes=[("ss", 1)])
                        S.op("dve", lambda e: e.tensor_scalar(out=ss_t[:, 1:2], in0=ss_t[:, 1:2], scalar1=1.0 / D, scalar2=EPS, op0=ALU.mult, op1=ALU.add), reads=[("ss", 1)], writes=[("ss", 1)])
                        S.op("act", lambda e: e.activation(out=ss_t[:, 1:2], in_=ss_t[:, 1:2], func=AF.Sqrt), reads=[("ss", 1)], writes=[("ss", 1)])
                        S.op("dve", lambda e: e.reciprocal(out=ss_t[:, 1:2], in_=ss_t[:, 1:2]), reads=[("ss", 1)], writes=[("ss", 1)])
                        for hh, (pm_, pmk_) in enumerate(((pm0, pmk0), (pm1, pmk1))):
                            S.op("dve", lambda e, pm_=pm_, hh=hh: e.scalar_tensor_tensor(out=tmpf[:, hh * 512:(hh + 1) * 512], in0=pm_[:], scalar=ss_t[:, 1:2], in1=rows[:, 2, hh * 512:(hh + 1) * 512], op0=ALU.mult, op1=ALU.mult),
                                 reads=[pmk_, ("ss", 1), "rows"], writes=["tmpf"])
                        S.op("dve", lambda e, xt=xt: e.tensor_tensor(out=xt[:], in0=xt[:], in1=tmpf[:], op=ALU.add), reads=["tmpf", xk], writes=[xk])
                        S.dma("sp", lambda e, xt=xt, t0=t0: e.dma_start(out=xmid[t0:t0 + 128, :], in_=xt[:]), reads=[xk], writes=[("dram", id(xmid))])
                        rms_rstd(xt[:], xk, 2)
                        S.op("dve", lambda e, xt=xt: e.scalar_tensor_tensor(out=tmpf[:], in0=xt[:], scalar=ss_t[:, 2:3], in1=rows2[:, 0, :], op0=ALU.mult, op1=ALU.mult),
                             reads=[xk, ("ss", 2), kk["rows2"]], writes=["tmpf"])
                        j2 = rr["hb"]; rr["hb"] = 1 - j2
                        hb2 = hb_ring[j2]; hk2 = "hb%d" % j2
                        S.op("dve", lambda e, hb2=hb2: e.tensor_tensor(out=hb2[:], in0=tmpf[:], in1=rows2[:, 1, :], op=ALU.add), reads=["tmpf", kk["rows2"]], writes=[hk2])
                        ptT, pkT = next_psT()
                        for k in range(8):
                            S.op("pe", lambda e, k=k, hb2=hb2, ptT=ptT: e.transpose(out=ptT[:, k, :], in_=hb2[:, k * 128:(k + 1) * 128], identity=ident[:]), reads=[hk2, "ident"], writes=[pkT])
                        S.op("act", lambda e, ptT=ptT, c=c: e.activation(out=hT[:, :, c * 128:(c + 1) * 128], in_=ptT[:], func=AF.Copy), reads=[pkT], writes=["hT"])
                        col = seq_col(blk * 4 + c)
                        S.dma("sp", lambda e, c=c, col=col: e.dma_start(out=h2s[:, :, col:col + 128].rearrange("k p t -> p k t"), in_=hT[:, :, c * 128:(c + 1) * 128]),
                              reads=["hT"], writes=["h2s"])

        def phase_D(l):
            dst = x1 if l == 0 else y
            with contextlib.ExitStack() as ph:
                alloc_wring(ph)
                wd = sb(P.name("wd"), [128, 22, 1024], BF16, ph)
                h2w = sb(P.name("h2w"), [128, 8, 4, 130], BF16, ph)
                cpar = sb(P.name("cpar"), [128, 4, 44], F32, ph)
                zt = sb(P.name("zt"), [128, 130], BF16, ph)
                g1 = sb(P.name("g1"), [128, 4, 128], F32, ph)
                g2 = sb(P.name("g2"), [128, 4, 128], F32, ph)
                gvT = uT_all[:, 0:3, :].rearrange("p a t -> p (a t)")[:, 0:22 * 512].rearrange("p (f t) -> p f t", f=22)
                kk = {n: P.name(n) for n in ("wd", "h2w", "cpar", "zt", "g1", "g2")}
                S.dma("sp", lambda e: e.dma_start(out=wd[:], in_=wb_down[l].rearrange("(k p) n -> p k n", p=128)), reads=SK(wb_down, l), writes=[kk["wd"]])
                for i in range(3):
                    S.dma("sp", lambda e, i=i: e.dma_start(out=cpar[:, i, :], in_=conv_w[l, i, :].rearrange("(f p) -> p f", p=128), allow_slow_non_contiguous=True), writes=[kk["cpar"]])
                S.dma("sp", lambda e: e.dma_start(out=cpar[:, 3, :], in_=conv_b[l, :].rearrange("(f p) -> p f", p=128), allow_slow_non_contiguous=True), writes=[kk["cpar"]])
                S.op("pool", lambda e: e.memset(zt[:], 0.0), writes=[kk["zt"]])
                pads = [0, TS + 1] + [TS + 2 + i * 258 for i in range(4)] + [TS + 2 + i * 258 + 257 for i in range(4)]
                for pc in pads:
                    S.dma("sp", lambda e, pc=pc: e.dma_start(out=h2s[:, :, pc:pc + 1].rearrange("k p o -> p k o"), in_=zt[:, 0:8].unsqueeze(2), allow_slow_non_contiguous=True),
                          reads=[kk["zt"]], writes=["h2s"])
                for blk in range(NBLK):
                    path = 0 if blk < 8 else 1
                    load_rows(l, path, 1)
                    for c in range(4):
                        col = seq_col(blk * 4 + c) - 1
                        S.dma("sp", lambda e, c=c, col=col: e.dma_start(out=h2w[:, :, c, :], in_=h2s[:, :, col:col + 130].rearrange("k p t -> p k t")),
                              reads=["h2s"], writes=[kk["h2w"]])
                    for ft in range(22):
                        wgt, wgk = load_w(wb_up[l, :, ft * 128:(ft + 1) * 128], 8, 128, SK(wb_up, l))
                        wvt, wvk = load_w(wb_up[l, :, D_FF + ft * 128:D_FF + (ft + 1) * 128], 8, 128, SK(wb_up, l))
                        for (wt, wk, fi, gt) in ((wgt, wgk, ft, g1), (wvt, wvk, 22 + ft, g2)):
                            pz, pzk = next_psF(True)
                            for c in range(4):
                                for k in range(8):
                                    S.op("pe", lambda e, c=c, k=k, pz=pz, wt=wt: e.matmul(pz[:, c * 128:c * 128 + 128], lhsT=wt[:, k, :], rhs=h2w[:, k, c, 1:129], start=(k == 0 and c == 0), stop=(k == 7 and c == 3), skip_group_check=True),
                                         reads=[wk, kk["h2w"]], writes=[pzk])
                            ph_, phk = next_psF(True)
                            for k in range(8):
                                S.op("pe", lambda e, k=k, ph_=ph_, wt=wt: e.matmul(ph_[:, 0:8].rearrange("p (c o) -> p c o", o=2), lhsT=wt[:, k, :], rhs=h2w[:, k, :, 0:130:129], start=(k == 0), stop=(k == 7)),
                                     reads=[wk, kk["h2w"]], writes=[phk])
                            gt_k = kk["g1"] if gt is g1 else kk["g2"]
                            S.op("act", lambda e, pz=pz, fi=fi, gt=gt: e.activation(out=gt[:].rearrange("p c t -> p (c t)"), in_=pz[:], func=AF.Identity, bias=cpar[:, 3, fi:fi + 1], scale=cpar[:, 1, fi:fi + 1]),
                                 reads=[pzk, kk["cpar"]], writes=[gt_k])
                            pz3 = pz[:].rearrange("p (c t) -> p c t", t=128)
                            ph3 = ph_[:, 0:8].rearrange("p (c o) -> p c o", o=2)
                            S.op("dve", lambda e, pz3=pz3, fi=fi, gt=gt: e.scalar_tensor_tensor(out=gt[:, :, 1:128], in0=pz3[:, :, 0:127], scalar=cpar[:, 0, fi:fi + 1], in1=gt[:, :, 1:128], op0=ALU.mult, op1=ALU.add),
                                 reads=[pzk, kk["cpar"], gt_k], writes=[gt_k])
                            S.op("dve", lambda e, pz3=pz3, fi=fi, gt=gt: e.scalar_tensor_tensor(out=gt[:, :, 0:127], in0=pz3[:, :, 1:128], scalar=cpar[:, 2, fi:fi + 1], in1=gt[:, :, 0:127], op0=ALU.mult, op1=ALU.add),
                                 reads=[pzk, kk["cpar"], gt_k], writes=[gt_k])
                            S.op("dve", lambda e, ph3=ph3, fi=fi, gt=gt: e.scalar_tensor_tensor(out=gt[:, :, 0:1], in0=ph3[:, :, 0:1], scalar=cpar[:, 0, fi:fi + 1], in1=gt[:, :, 0:1], op0=ALU.mult, op1=ALU.add),
                                 reads=[phk, kk["cpar"], gt_k], writes=[gt_k])
                            S.op("dve", lambda e, ph3=ph3, fi=fi, gt=gt: e.scalar_tensor_tensor(out=gt[:, :, 127:128], in0=ph3[:, :, 1:2], scalar=cpar[:, 2, fi:fi + 1], in1=gt[:, :, 127:128], op0=ALU.mult, op1=ALU.add),
                                 reads=[phk, kk["cpar"], gt_k], writes=[gt_k])
                        S.op("act", lambda e: e.activation(out=g1[:], in_=g1[:], func=AF.Silu), reads=[kk["g1"]], writes=[kk["g1"]])
                        S.op("dve", lambda e, ft=ft: e.tensor_tensor(out=gvT[:, ft, :], in0=g1[:].rearrange("p c t -> p (c t)"), in1=g2[:].rearrange("p c t -> p (c t)"), op=ALU.mult),
                             reads=[kk["g1"], kk["g2"]], writes=["uT_all"])
                    for c in range(4):
                        t0 = blk * 512 + c * 128
                        for hh in range(2):
                            pm_, pmk_ = next_psF(True)
                            for k in range(22):
                                S.op("pe", lambda e, c=c, k=k, pm_=pm_, hh=hh: e.matmul(pm_[:], lhsT=gvT[:, k, c * 128:(c + 1) * 128], rhs=wd[:, k, hh * 512:(hh + 1) * 512], start=(k == 0), stop=(k == 21)),
                                     reads=[kk["wd"], "uT_all"], writes=[pmk_])
                            S.op("act", lambda e, pm_=pm_, hh=hh: e.activation(out=tmpf[:, hh * 512:(hh + 1) * 512], in_=pm_[:], func=AF.Copy), reads=[pmk_], writes=["tmpf"])
                        rms_rstd(tmpf[:], "tmpf", 1)
                        i = rr["xt"]; rr["xt"] = 1 - i
                        xt = xt_ring[i]; xk = "xt%d" % i
                        S.dma("sp", lambda e, xt=xt, t0=t0: e.dma_start(out=xt[:], in_=xmid[t0:t0 + 128, :]), reads=[("dram", id(xmid))], writes=[xk])
                        S.op("dve", lambda e: e.scalar_tensor_tensor(out=tmpf[:], in0=tmpf[:], scalar=ss_t[:, 1:2], in1=rows[:, 2, :], op0=ALU.mult, op1=ALU.mult),
                             reads=["tmpf", ("ss", 1), "rows"], writes=["tmpf"])
                        S.op("dve", lambda e, xt=xt: e.tensor_tensor(out=xt[:], in0=xt[:], in1=tmpf[:], op=ALU.add), reads=["tmpf", xk], writes=[xk])
                        S.dma("sp", lambda e, xt=xt, t0=t0: e.dma_start(out=dst[t0:t0 + 128, :], in_=xt[:]), reads=[xk], writes=[("dram", id(dst))])

        PI = math.pi

        SC = 64
        NSC = NTOK // SC

        def phase_B(l):
            with contextlib.ExitStack() as ph:
                yfb = sb(P.name("yfb"), [128, 64], F32, ph)
                yab = sb(P.name("yab"), [128, 64], BF16, ph)
                dsk = sb(P.name("dsk"), [128, 4], F32, ph)
                S.dma("sp", lambda e: e.dma_start(out=dsk[:], in_=dskT[l]), writes=["dsk"])
                ctx = []
                for d in range(2):
                    eng = "dve" if d == 0 else "pool"
                    pp = sb(P.name("pp"), [128, 1104], F32, ph)
                    wk = sb(P.name("wk"), [128, 12, 16], F32, ph)
                    Bx = sb(P.name("Bx"), [128, 2, 16, 16], F32, ph)
                    tb = sb(P.name("tb"), [128, 16, 16], F32, ph)
                    AB = sb(P.name("AB"), [128, 2, 32], F32, ph)
                    Mt = sb(P.name("Mt"), [128, 2, 4, 128], BF16, ph)
                    Bcm = sb(P.name("Bcm"), [128, 2, 4, 128], BF16, ph)
                    Cbd = sb(P.name("Cbd"), [128, 2, 16, 128], BF16, ph)
                    Hc = sb(P.name("Hc"), [128, SC + 1, 48], F32, ph)
                    xs = sb(P.name("xs"), [128, SC, 32], F32, ph)
                    Hb = sb(P.name("Hb"), [128, 32, SC], BF16, ph)
                    t1 = sb(P.name("t1"), [128, 32], F32, ph)
                    t2 = sb(P.name("t2"), [128, 32], F32, ph)
                    ya = sb(P.name("ya"), [128, SC], F32, ph)
                    K = {n: P.name(n) for n in ("pp", "wk", "Bx", "tb", "AB", "Mt", "Bcm", "Cbd", "Hc", "xs", "Hb", "t1", "t2", "ya")}
                    S.dma("sp", lambda e, pp=pp, d=d: e.dma_start(out=pp[:], in_=s5p[l, d]), writes=[K["pp"]])
                    lre, lim, lst = pp[:, 0:16], pp[:, 16:32], pp[:, 32:48]
                    bre = pp[:, 48:304].rearrange("p (g e) -> p g e", e=16); bim = pp[:, 304:560].rearrange("p (g e) -> p g e", e=16)
                    cre = pp[:, 560:816].rearrange("p (g e) -> p g e", e=16); cim = pp[:, 816:1072].rearrange("p (g e) -> p g e", e=16)
                    W = lambda i, wk=wk: wk[:, i, :]

                    def dv(fn, pp=pp, K=K):
                        S.op("dve", fn, reads=[K["pp"], K["wk"]], writes=[K["wk"]])

                    def ac(fn, K=K):
                        S.op("act", fn, reads=[K["wk"], K["pp"]], writes=[K["wk"]])
                    ac(lambda e, W=W, lst=lst: e.activation(out=W(0), in_=lst, func=AF.Exp))
                    dv(lambda e, W=W, lre=lre: e.tensor_tensor(out=W(1), in0=lre, in1=W(0), op=ALU.mult))
                    dv(lambda e, W=W, lim=lim: e.tensor_tensor(out=W(2), in0=lim, in1=W(0), op=ALU.mult))
                    ac(lambda e, W=W: e.activation(out=W(3), in_=W(1), func=AF.Exp))
                    ac(lambda e, W=W: e.activation(out=W(4), in_=W(2), func=AF.Sin, scale=1.0 / 16))
                    ac(lambda e, W=W: e.activation(out=W(5), in_=W(2), func=AF.Sin, scale=1.0 / 32))
                    dv(lambda e, W=W: e.tensor_tensor(out=W(5), in0=W(5), in1=W(5), op=ALU.mult))
                    dv(lambda e, W=W: e.tensor_scalar(out=W(5), in0=W(5), scalar1=-2.0, scalar2=1.0, op0=ALU.mult, op1=ALU.add))
                    for _ in range(4):
                        dv(lambda e, W=W: e.tensor_tensor(out=W(6), in0=W(4), in1=W(5), op=ALU.mult))
                        dv(lambda e, W=W: e.tensor_tensor(out=W(7), in0=W(4), in1=W(4), op=ALU.mult))
                        dv(lambda e, W=W: e.tensor_tensor(out=W(5), in0=W(5), in1=W(5), op=ALU.mult))
                        dv(lambda e, W=W: e.tensor_tensor(out=W(5), in0=W(5), in1=W(7), op=ALU.subtract))
                        dv(lambda e, W=W: e.tensor_scalar(out=W(4), in0=W(6), scalar1=2.0, scalar2=None, op0=ALU.mult))
                    dv(lambda e, W=W: e.tensor_tensor(out=W(6), in0=W(3), in1=W(5), op=ALU.mult))
                    dv(lambda e, W=W: e.tensor_tensor(out=W(7), in0=W(3), in1=W(4), op=ALU.mult))
                    dv(lambda e, W=W, lre=lre: e.tensor_tensor(out=W(8), in0=lre, in1=lre, op=ALU.mult))
                    dv(lambda e, W=W, lim=lim: e.tensor_tensor(out=W(9), in0=lim, in1=lim, op=ALU.mult))
                    dv(lambda e, W=W: e.tensor_tensor(out=W(8), in0=W(8), in1=W(9), op=ALU.add))
                    dv(lambda e, W=W: e.reciprocal(out=W(8), in_=W(8)))
                    dv(lambda e, W=W: e.tensor_scalar(out=W(9), in0=W(6), scalar1=-1.0, scalar2=None, op0=ALU.add))
                    dv(lambda e, W=W, lre=lre: e.tensor_tensor(out=W(10), in0=W(9), in1=lre, op=ALU.mult))
                    dv(lambda e, W=W, lim=lim: e.tensor_tensor(out=W(11), in0=W(7), in1=lim, op=ALU.mult))
                    dv(lambda e, W=W: e.tensor_tensor(out=W(10), in0=W(10), in1=W(11), op=ALU.add))
                    dv(lambda e, W=W: e.tensor_tensor(out=W(10), in0=W(10), in1=W(8), op=ALU.mult))
                    dv(lambda e, W=W, lre=lre: e.tensor_tensor(out=W(11), in0=W(7), in1=lre, op=ALU.mult))
                    dv(lambda e, W=W, lim=lim: e.tensor_tensor(out=W(9), in0=W(9), in1=lim, op=ALU.mult))
                    dv(lambda e, W=W: e.tensor_tensor(out=W(11), in0=W(11), in1=W(9), op=ALU.subtract))
                    dv(lambda e, W=W: e.tensor_tensor(out=W(11), in0=W(11), in1=W(8), op=ALU.mult))
                    fre_b = wk[:, 10, :].unsqueeze(2).broadcast_to([128, 16, 16]); fim_b = wk[:, 11, :].unsqueeze(2).broadcast_to([128, 16, 16])
                    S.op("dve", lambda e, Bx=Bx, bre=bre, fre_b=fre_b: e.tensor_tensor(out=Bx[:, 0], in0=bre, in1=fre_b, op=ALU.mult), reads=[K["pp"], K["wk"]], writes=[K["Bx"]])
                    S.op("dve", lambda e, tb=tb, bim=bim, fim_b=fim_b: e.tensor_tensor(out=tb[:], in0=bim, in1=fim_b, op=ALU.mult), reads=[K["pp"], K["wk"]], writes=[K["tb"]])
                    S.op("dve", lambda e, Bx=Bx, tb=tb: e.tensor_tensor(out=Bx[:, 0], in0=Bx[:, 0], in1=tb[:], op=ALU.subtract), reads=[K["Bx"], K["tb"]], writes=[K["Bx"]])
                    S.op("dve", lambda e, Bx=Bx, bim=bim, fre_b=fre_b: e.tensor_tensor(out=Bx[:, 1], in0=bim, in1=fre_b, op=ALU.mult), reads=[K["pp"], K["wk"]], writes=[K["Bx"]])
                    S.op("dve", lambda e, tb=tb, bre=bre, fim_b=fim_b: e.tensor_tensor(out=tb[:], in0=bre, in1=fim_b, op=ALU.mult), reads=[K["pp"], K["wk"], K["Bx"]], writes=[K["tb"]])
                    S.op("dve", lambda e, Bx=Bx, tb=tb: e.tensor_tensor(out=Bx[:, 1], in0=Bx[:, 1], in1=tb[:], op=ALU.add), reads=[K["Bx"], K["tb"]], writes=[K["Bx"]])
                    S.op("dve", lambda e, AB=AB, W=W: e.tensor_copy(out=AB[:, 0, 0:16], in_=W(6)), reads=[K["wk"]], writes=[K["AB"]])
                    S.op("dve", lambda e, AB=AB, W=W: e.tensor_copy(out=AB[:, 0, 16:32], in_=W(6)), reads=[K["wk"]], writes=[K["AB"]])
                    S.op("dve", lambda e, AB=AB, W=W: e.tensor_scalar(out=AB[:, 1, 0:16], in0=W(7), scalar1=-1.0, scalar2=None, op0=ALU.mult), reads=[K["wk"]], writes=[K["AB"]])
                    S.op("dve", lambda e, AB=AB, W=W: e.tensor_copy(out=AB[:, 1, 16:32], in_=W(7)), reads=[K["wk"]], writes=[K["AB"]])
                    S.op("dve", lambda e, Mt=Mt: e.memset(Mt[:], 0.0), writes=[K["Mt"]])
                    S.op("dve", lambda e, Cbd=Cbd: e.memset(Cbd[:], 0.0), writes=[K["Cbd"]])
                    for part in range(2):
                        for h in range(2):
                            hs = slice(64 * h, 64 * h + 64)
                            S.op("dve", lambda e, Mt=Mt, Bx=Bx, part=part, h=h, hs=hs: e.tensor_copy(
                                out=Mt[hs, part].rearrange("p c (q h e) -> p c q h e", q=4, h=2, e=16)[:, :, :, h, :],
                                in_=Bx[hs, part].rearrange("p (c q) e -> p c q e", c=4)), reads=[K["Bx"], K["Mt"]], writes=[K["Mt"]])
                            for q in range(4):
                                csrc = (cre if part == 0 else cim)
                                S.op("dve", lambda e, Cbd=Cbd, csrc=csrc, part=part, h=h, hs=hs, q=q: e.tensor_scalar(
                                    out=Cbd[hs, part].rearrange("p (c q) n -> p c q n", q=4)[:, :, q, 32 * q + 16 * h:32 * q + 16 * h + 16],
                                    in0=csrc[hs].rearrange("p (c q) e -> p c q e", q=4)[:, :, q, :], scalar1=(1.0 if part == 0 else -1.0), scalar2=None, op0=ALU.mult),
                                     reads=[K["pp"], K["Cbd"]], writes=[K["Cbd"]])
                        ptT, pkT = next_psT()
                        for ct in range(4):
                            S.op("pe", lambda e, Mt=Mt, part=part, ct=ct, ptT=ptT: e.transpose(out=ptT[:, ct, :], in_=Mt[:, part, ct, :], identity=ident[:]), reads=[K["Mt"], "ident"], writes=[pkT])
                        S.op("act", lambda e, Bcm=Bcm, part=part, ptT=ptT: e.activation(out=Bcm[:, part], in_=ptT[:, 0:4, :], func=AF.Copy), reads=[pkT], writes=[K["Bcm"]])
                    ctx.append((eng, pp, AB, Bcm, Cbd, Hc, xs, Hb, t1, t2, ya, K))

                def seq_of(cg):
                    if cg < TS // SC:
                        return (0, 0, TS // SC - 1)
                    p_ = (cg - TS // SC) // (256 // SC)
                    f_ = TS // SC + p_ * (256 // SC)
                    return (1 + p_, f_, f_ + 256 // SC - 1)

                npc = 256 // SC
                bw = list(range(TS // SC - 1, -1, -1))
                for p_ in range(4):
                    f_ = TS // SC + p_ * npc
                    bw += list(range(f_ + npc - 1, f_ - 1, -1))
                order = [list(range(NSC)), bw]
                for d in range(2):
                    eng, pp, AB, Bcm, Cbd, Hc, xs, Hb, t1, t2, ya, K = ctx[d]
                    for cg in order[d]:
                        sq, first, last = seq_of(cg)
                        t0 = cg * SC
                        init_idx = 0 if d == 0 else SC
                        is_start = (cg == first) if d == 0 else (cg == last)
                        if is_start:
                            if sq == 0:
                                S.op(eng, lambda e, Hc=Hc, pp=pp, init_idx=init_idx: e.tensor_copy(out=Hc[:, init_idx, 0:32], in_=pp[:, 1072:1104]), reads=[K["pp"], K["Hc"]], writes=[K["Hc"]])
                                S.op(eng, lambda e, Hc=Hc, pp=pp, init_idx=init_idx: e.tensor_copy(out=Hc[:, init_idx, 32:48], in_=pp[:, 1072:1088]), reads=[K["pp"], K["Hc"]], writes=[K["Hc"]])
                            else:
                                S.op(eng, lambda e, Hc=Hc, init_idx=init_idx: e.memset(Hc[:, init_idx, :], 0.0), reads=[K["Hc"]], writes=[K["Hc"]])
                        else:
                            S.op(eng, lambda e, Hc=Hc, init_idx=init_idx: e.tensor_copy(out=Hc[:, init_idx, :], in_=Hc[:, SC - init_idx, :]), reads=[K["Hc"]], writes=[K["Hc"]])
                        for part in range(2):
                            for ct in range(4):
                                pc = part * 4 + ct
                                for q in range(4):
                                    S.op("pe", lambda e, Bcm=Bcm, part=part, ct=ct, q=q, t0=t0, pc=pc: e.matmul(psF[2 + q][:, pc * SC:(pc + 1) * SC], lhsT=Bcm[32 * q:32 * q + 32, part, ct, :],
                                                                                                             rhs=uT_all[32 * q:32 * q + 32, ct, t0:t0 + SC], start=True, stop=True, skip_group_check=True, tile_position=(32 * q, 0)),
                                         reads=[K["Bcm"], "uT_all"], writes=["psF%d" % (2 + q)])
                        for q in range(4):
                            S.op("act", lambda e, xs=xs, q=q: e.activation(out=xs[:].rearrange("p t (pc q) -> p q pc t", q=4)[:, q], in_=psF[2 + q][:, 0:8 * SC].rearrange("p (pc t) -> p pc t", pc=8), func=AF.Copy),
                                 reads=["psF%d" % (2 + q)], writes=[K["xs"]])
                        for i in range(SC):
                            t = i if d == 0 else SC - 1 - i
                            si = t if d == 0 else t + 1
                            di = t + 1 if d == 0 else t
                            S.op(eng, lambda e, t1=t1, AB=AB, Hc=Hc, si=si: e.tensor_tensor(out=t1[:], in0=AB[:, 0, :], in1=Hc[:, si, 0:32], op=ALU.mult), reads=[K["Hc"], K["AB"]], writes=[K["t1"]])
                            S.op(eng, lambda e, t2=t2, AB=AB, Hc=Hc, si=si: e.tensor_tensor(out=t2[:], in0=AB[:, 1, :], in1=Hc[:, si, 16:48], op=ALU.mult), reads=[K["Hc"], K["AB"]], writes=[K["t2"]])
                            S.op(eng, lambda e, t1=t1, xs=xs, t=t: e.tensor_tensor(out=t1[:], in0=t1[:], in1=xs[:, t, :], op=ALU.add), reads=[K["t1"], K["xs"]], writes=[K["t1"]])
                            S.op(eng, lambda e, t1=t1, t2=t2, Hc=Hc, di=di: e.tensor_tensor(out=Hc[:, di, 0:32], in0=t1[:], in1=t2[:], op=ALU.add), reads=[K["t1"], K["t2"]], writes=[K["Hc"]])
                            S.op(eng, lambda e, Hc=Hc, di=di: e.tensor_copy(out=Hc[:, di, 32:48], in_=Hc[:, di, 0:16]), reads=[K["Hc"]], writes=[K["Hc"]])
                        off = 1 if d == 0 else 0
                        S.op("act", lambda e, Hb=Hb, Hc=Hc, off=off: e.activation(out=Hb[:], in_=Hc[:, off:off + SC, 0:32].rearrange("p t j -> p j t"), func=AF.Copy), reads=[K["Hc"]], writes=[K["Hb"]])
                        if sq > 0 and ((d == 0 and cg == last) or (d == 1 and cg == first)):
                            fi = SC if d == 0 else 0
                            S.dma("sp", lambda e, Hc=Hc, fi=fi, sq=sq, d=d: e.dma_start(out=nst[l, d, sq - 1], in_=Hc[:, fi, 0:32]), reads=[K["Hc"]], writes=["nst"])
                        for ct in range(4):
                            py, pyk = psF[ct % 2], "psF%d" % (ct % 2)
                            n = 0
                            for part in range(2):
                                for q in range(4):
                                    S.op("pe", lambda e, py=py, Cbd=Cbd, Hb=Hb, part=part, ct=ct, q=q, n=n: e.matmul(py[:, 0:SC], lhsT=Cbd[:, part, 4 * ct + q, :], rhs=Hb[:, part * 16 + 4 * ct + q, :], start=(n == 0), stop=(n == 7)),
                                         reads=[K["Cbd"], K["Hb"]], writes=[pyk])
                                    n += 1
                            if d == 0:
                                S.op("act", lambda e, py=py, ya=ya: e.activation(out=ya[:], in_=py[:, 0:SC], func=AF.Copy), reads=[pyk], writes=[K["ya"]])
                                S.dma("sp", lambda e, ya=ya, ct=ct, t0=t0: e.dma_start(out=yfs[ct, :, t0:t0 + SC], in_=ya[:]), reads=[K["ya"]], writes=[("yfs", ct, t0)])
                            else:
                                S.dma("sp", lambda e, ct=ct, t0=t0: e.dma_start(out=yfb[:], in_=yfs[ct, :, t0:t0 + SC]), reads=[("yfs", ct, t0)], writes=["yfb"])
                                S.op("dve", lambda e, py=py, ya=ya: e.tensor_tensor(out=ya[:], in0=py[:, 0:SC], in1=yfb[:], op=ALU.add), reads=[pyk, "yfb"], writes=[K["ya"]])
                                S.op("dve", lambda e, ya=ya, ct=ct, t0=t0: e.scalar_tensor_tensor(out=ya[:], in0=uT_all[:, ct, t0:t0 + SC], scalar=dsk[:, ct:ct + 1], in1=ya[:], op0=ALU.mult, op1=ALU.add),
                                     reads=["uT_all", "dsk", K["ya"]], writes=[K["ya"]])
                                S.op("act", lambda e, ya=ya: e.activation(out=yab[:], in_=ya[:], func=AF.Gelu), reads=[K["ya"]], writes=["yab"])
                                S.dma("sp", lambda e, ct=ct, t0=t0: e.dma_start(out=yas[ct, :, t0:t0 + SC], in_=yab[:]), reads=["yab"], writes=["yas"])

        def phase_B2(l):
            with contextlib.ExitStack() as ph:
                dsk = sb(P.name("dsk"), [128, 4], F32, ph)
                S.dma("sp", lambda e: e.dma_start(out=dsk[:], in_=dskT[l]), writes=["dsk"])
                Bcm_b = sb(P.name("Bcm_b"), [128, 16, 2, 4, 128], BF16, ph)
                Wo = sb(P.name("Wo"), [128, 2, 16, 512], BF16, ph)
                Kbd = sb(P.name("Kbd"), [128, 16, 4, 128], BF16, ph)
                Cbd = sb(P.name("Cbd"), [128, 2, 16, 128], BF16, ph)
                ytb = [sb(P.name("ytb"), [32, 16, 128], BF16, ph) for _ in range(2)]
                rowsf = rows[:].rearrange("p a n -> p (a n)")
                pp = rowsf[:, 0:1104]
                Pw = rowsf[:, 1104:1648].rearrange("p (r k g) -> p r k g", r=2, k=17)
                Bx = rowsf[:, 1648:2160].rearrange("p (r g e) -> p r g e", r=2, g=16)
                wk = rowsf[:, 2160:2352].rearrange("p (i g) -> p i g", i=12)
                tb = rowsf[:, 2352:2608].rearrange("p (g e) -> p g e", g=16)
                V = rowsf[:, 2608:2864].rearrange("p (g e) -> p g e", g=16)
                pa = rowsf[:, 2864:2896].rearrange("p (r g) -> p r g", r=2)
                cur = rowsf[:, 2896:2944]
                t1 = rowsf[:, 2944:2976]
                t2 = rowsf[:, 2976:3008]
                AB = rowsf[:, 3008:3072].rearrange("p (r g) -> p r g", r=2)
                wsb = xt_ring[0][:].rearrange("p (j a) -> p j a", j=32)
                yfb = xt_ring[1][:, 0:512]
                V2 = xt_ring[1][:, 512:768].rearrange("p (g e) -> p g e", g=16)
                Xbd = hT[:].rearrange("p k (c n) -> p (k c) n", n=128).rearrange("p (r g) n -> p r g n", r=2)
                Mt = junk[:].rearrange("p (r c n) -> p r c n", r=2, c=4)
                Est = hb_ring[0][:].rearrange("p (j a) -> p j a", j=32)
                yab = hb_ring[1][:, 0:512]
                KB = {n: P.name(n) for n in ("Bcm_b", "Wo", "Kbd", "Xbd", "Cbd", "Mt", "Pw", "V", "V2", "pa", "wsb", "Est", "cur", "t1", "t2", "AB", "ytb0", "ytb1", "yfb", "yab")}
                KP = {n: P.name(n) for n in ("pp", "wk", "Bx", "tb")}
                S.op("dve", lambda e: e.memset(Xbd, 0.0), writes=[KB["Xbd"]])
                S.op("dve", lambda e: e.memset(Mt, 0.0), writes=[KB["Mt"]])
                for d in range(2):
                    eng = "dve" if d == 0 else "pool"
                    K = KP
                    S.dma("sp", lambda e, pp=pp, d=d: e.dma_start(out=pp, in_=s5p[l, d]), writes=[K["pp"]])
                    lre, lim, lst = pp[:, 0:16], pp[:, 16:32], pp[:, 32:48]
                    bre = pp[:, 48:304].rearrange("p (g e) -> p g e", e=16); bim = pp[:, 304:560].rearrange("p (g e) -> p g e", e=16)
                    cre = pp[:, 560:816].rearrange("p (g e) -> p g e", e=16); cim = pp[:, 816:1072].rearrange("p (g e) -> p g e", e=16)
                    W = lambda i, wk=wk: wk[:, i, :]

                    def dv(fn, pp=pp, K=K):
                        S.op("dve", fn, reads=[K["pp"], K["wk"]], writes=[K["wk"]])

                    def ac(fn, K=K):
                        S.op("act", fn, reads=[K["wk"], K["pp"]], writes=[K["wk"]])
                    ac(lambda e, W=W, lst=lst: e.activation(out=W(0), in_=lst, func=AF.Exp))
                    dv(lambda e, W=W, lre=lre: e.tensor_tensor(out=W(1), in0=lre, in1=W(0), op=ALU.mult))
                    dv(lambda e, W=W, lim=lim: e.tensor_tensor(out=W(2), in0=lim, in1=W(0), op=ALU.mult))
                    ac(lambda e, W=W: e.activation(out=W(3), in_=W(1), func=AF.Exp))
                    ac(lambda e, W=W: e.activation(out=W(4), in_=W(2), func=AF.Sin, scale=1.0 / 16))
                    ac(lambda e, W=W: e.activation(out=W(5), in_=W(2), func=AF.Sin, scale=1.0 / 32))
                    dv(lambda e, W=W: e.tensor_tensor(out=W(5), in0=W(5), in1=W(5), op=ALU.mult))
                    dv(lambda e, W=W: e.tensor_scalar(out=W(5), in0=W(5), scalar1=-2.0, scalar2=1.0, op0=ALU.mult, op1=ALU.add))
                    for _ in range(4):
                        dv(lambda e, W=W: e.tensor_tensor(out=W(6), in0=W(4), in1=W(5), op=ALU.mult))
                        dv(lambda e, W=W: e.tensor_tensor(out=W(7), in0=W(4), in1=W(4), op=ALU.mult))
                        dv(lambda e, W=W: e.tensor_tensor(out=W(5), in0=W(5), in1=W(5), op=ALU.mult))
                        dv(lambda e, W=W: e.tensor_tensor(out=W(5), in0=W(5), in1=W(7), op=ALU.subtract))
                        dv(lambda e, W=W: e.tensor_scalar(out=W(4), in0=W(6), scalar1=2.0, scalar2=None, op0=ALU.mult))
                    dv(lambda e, W=W: e.tensor_tensor(out=W(6), in0=W(3), in1=W(5), op=ALU.mult))
                    dv(lambda e, W=W: e.tensor_tensor(out=W(7), in0=W(3), in1=W(4), op=ALU.mult))
                    dv(lambda e, W=W, lre=lre: e.tensor_tensor(out=W(8), in0=lre, in1=lre, op=ALU.mult))
                    dv(lambda e, W=W, lim=lim: e.tensor_tensor(out=W(9), in0=lim, in1=lim, op=ALU.mult))
                    dv(lambda e, W=W: e.tensor_tensor(out=W(8), in0=W(8), in1=W(9), op=ALU.add))
                    dv(lambda e, W=W: e.reciprocal(out=W(8), in_=W(8)))
                    dv(lambda e, W=W: e.tensor_scalar(out=W(9), in0=W(6), scalar1=-1.0, scalar2=None, op0=ALU.add))
                    dv(lambda e, W=W, lre=lre: e.tensor_tensor(out=W(10), in0=W(9), in1=lre, op=ALU.mult))
                    dv(lambda e, W=W, lim=lim: e.tensor_tensor(out=W(11), in0=W(7), in1=lim, op=ALU.mult))
                    dv(lambda e, W=W: e.tensor_tensor(out=W(10), in0=W(10), in1=W(11), op=ALU.add))
                    dv(lambda e, W=W: e.tensor_tensor(out=W(10), in0=W(10), in1=W(8), op=ALU.mult))
                    dv(lambda e, W=W, lre=lre: e.tensor_tensor(out=W(11), in0=W(7), in1=lre, op=ALU.mult))
                    dv(lambda e, W=W, lim=lim: e.tensor_tensor(out=W(9), in0=W(9), in1=lim, op=ALU.mult))
                    dv(lambda e, W=W: e.tensor_tensor(out=W(11), in0=W(11), in1=W(9), op=ALU.subtract))
                    dv(lambda e, W=W: e.tensor_tensor(out=W(11), in0=W(11), in1=W(8), op=ALU.mult))
                    fre_b = wk[:, 10, :].unsqueeze(2).broadcast_to([128, 16, 16]); fim_b = wk[:, 11, :].unsqueeze(2).broadcast_to([128, 16, 16])
                    S.op("dve", lambda e, Bx=Bx, bre=bre, fre_b=fre_b: e.tensor_tensor(out=Bx[:, 0], in0=bre, in1=fre_b, op=ALU.mult), reads=[K["pp"], K["wk"]], writes=[K["Bx"]])
                    S.op("dve", lambda e, tb=tb, bim=bim, fim_b=fim_b: e.tensor_tensor(out=tb[:], in0=bim, in1=fim_b, op=ALU.mult), reads=[K["pp"], K["wk"]], writes=[K["tb"]])
                    S.op("dve", lambda e, Bx=Bx, tb=tb: e.tensor_tensor(out=Bx[:, 0], in0=Bx[:, 0], in1=tb[:], op=ALU.subtract), reads=[K["Bx"], K["tb"]], writes=[K["Bx"]])
                    S.op("dve", lambda e, Bx=Bx, bim=bim, fre_b=fre_b: e.tensor_tensor(out=Bx[:, 1], in0=bim, in1=fre_b, op=ALU.mult), reads=[K["pp"], K["wk"]], writes=[K["Bx"]])
                    S.op("dve", lambda e, tb=tb, bre=bre, fim_b=fim_b: e.tensor_tensor(out=tb[:], in0=bre, in1=fim_b, op=ALU.mult), reads=[K["pp"], K["wk"], K["Bx"]], writes=[K["tb"]])
                    S.op("dve", lambda e, Bx=Bx, tb=tb: e.tensor_tensor(out=Bx[:, 1], in0=Bx[:, 1], in1=tb[:], op=ALU.add), reads=[K["Bx"], K["tb"]], writes=[K["Bx"]])

                    lr, li = wk[:, 6, :], wk[:, 7, :]
                    S.op("dve", lambda e: e.memset(Pw[:, 0, 0, :], 1.0), reads=[KB["Pw"]], writes=[KB["Pw"]])
                    S.op("dve", lambda e: e.memset(Pw[:, 1, 0, :], 0.0), reads=[KB["Pw"]], writes=[KB["Pw"]])
                    for k in range(16):
                        S.op("dve", lambda e, k=k, lr=lr: e.tensor_tensor(out=pa[:, 0, :], in0=Pw[:, 0, k, :], in1=lr, op=ALU.mult), reads=[KB["Pw"], K["wk"]], writes=[KB["pa"]])
                        S.op("dve", lambda e, k=k, li=li: e.tensor_tensor(out=pa[:, 1, :], in0=Pw[:, 1, k, :], in1=li, op=ALU.mult), reads=[KB["Pw"], K["wk"]], writes=[KB["pa"]])
                        S.op("dve", lambda e, k=k: e.tensor_tensor(out=Pw[:, 0, k + 1, :], in0=pa[:, 0, :], in1=pa[:, 1, :], op=ALU.subtract), reads=[KB["pa"]], writes=[KB["Pw"]])
                        S.op("dve", lambda e, k=k, li=li: e.tensor_tensor(out=pa[:, 0, :], in0=Pw[:, 0, k, :], in1=li, op=ALU.mult), reads=[KB["Pw"], K["wk"]], writes=[KB["pa"]])
                        S.op("dve", lambda e, k=k, lr=lr: e.tensor_tensor(out=pa[:, 1, :], in0=Pw[:, 1, k, :], in1=lr, op=ALU.mult), reads=[KB["Pw"], K["wk"]], writes=[KB["pa"]])
                        S.op("dve", lambda e, k=k: e.tensor_tensor(out=Pw[:, 1, k + 1, :], in0=pa[:, 0, :], in1=pa[:, 1, :], op=ALU.add), reads=[KB["pa"]], writes=[KB["Pw"]])
                    S.op("dve", lambda e: e.tensor_copy(out=AB[:, 0, 0:16], in_=Pw[:, 0, 16, :]), reads=[KB["Pw"], KB["AB"]], writes=[KB["AB"]])
                    S.op("dve", lambda e: e.tensor_copy(out=AB[:, 0, 16:32], in_=Pw[:, 0, 16, :]), reads=[KB["Pw"], KB["AB"]], writes=[KB["AB"]])
                    S.op("dve", lambda e: e.tensor_scalar(out=AB[:, 1, 0:16], in0=Pw[:, 1, 16, :], scalar1=-1.0, scalar2=None, op0=ALU.mult), reads=[KB["Pw"], KB["AB"]], writes=[KB["AB"]])
                    S.op("dve", lambda e: e.tensor_copy(out=AB[:, 1, 16:32], in_=Pw[:, 1, 16, :]), reads=[KB["Pw"], KB["AB"]], writes=[KB["AB"]])
                    S.op("dve", lambda e: e.memset(Cbd[:], 0.0), reads=[KB["Cbd"]], writes=[KB["Cbd"]])
                    S.op("dve", lambda e: e.memset(Wo[:], 0.0), reads=[KB["Wo"]], writes=[KB["Wo"]])
                    for part in range(2):
                        csrc = (cre if part == 0 else cim)
                        for h in range(2):
                            hs = slice(64 * h, 64 * h + 64)
                            for q in range(4):
                                S.op("dve", lambda e, csrc=csrc, part=part, h=h, hs=hs, q=q: e.tensor_scalar(
                                    out=Cbd[hs, part].rearrange("p (c q) n -> p c q n", q=4)[:, :, q, 32 * q + 16 * h:32 * q + 16 * h + 16],
                                    in0=csrc[hs].rearrange("p (c q) e -> p c q e", q=4)[:, :, q, :], scalar1=(1.0 if part == 0 else -1.0), scalar2=None, op0=ALU.mult),
                                     reads=[K["pp"], KB["Cbd"]], writes=[KB["Cbd"]])

                    def pwb(ri, k):
                        return Pw[:, ri, k, :].unsqueeze(2).broadcast_to([128, 16, 16])
                    for e_ in range(16):
                        b_ = (15 - e_) if d == 0 else e_
                        for part in range(2):
                            i0, i1 = (0, 1) if part == 0 else (1, 0)
                            S.op("dve", lambda e, e_=e_, i0=i0, Bx=Bx: e.tensor_tensor(out=V[:], in0=Bx[:, i0], in1=pwb(0, e_), op=ALU.mult), reads=[K["Bx"], KB["Pw"]], writes=[KB["V"]])
                            S.op("dve", lambda e, e_=e_, i1=i1, Bx=Bx: e.tensor_tensor(out=V2[:], in0=Bx[:, i1], in1=pwb(1, e_), op=ALU.mult), reads=[K["Bx"], KB["Pw"]], writes=[KB["V2"]])
                            S.op("dve", lambda e, part=part: e.tensor_tensor(out=V[:], in0=V[:], in1=V2[:], op=(ALU.subtract if part == 0 else ALU.add)), reads=[KB["V"], KB["V2"]], writes=[KB["V"]])
                            for h in range(2):
                                hs = slice(64 * h, 64 * h + 64)
                                S.op("dve", lambda e, part=part, h=h, hs=hs: e.tensor_copy(
                                    out=Mt[hs, part].rearrange("p c (q h e) -> p c q h e", q=4, h=2, e=16)[:, :, :, h, :],
                                    in_=V[hs].rearrange("p (c q) e -> p c q e", c=4)), reads=[KB["V"], KB["Mt"]], writes=[KB["Mt"]])
                                for q in range(4):
                                    S.op("dve", lambda e, part=part, h=h, hs=hs, q=q: e.tensor_copy(
                                        out=Xbd[hs, part].rearrange("p (c q) n -> p c q n", q=4)[:, :, q, 32 * q + 16 * h:32 * q + 16 * h + 16],
                                        in_=V[hs].rearrange("p (c q) e -> p c q e", q=4)[:, :, q, :]), reads=[KB["V"], KB["Xbd"]], writes=[KB["Xbd"]])
                            ptT, pkT = next_psT()
                            for ct in range(4):
                                S.op("pe", lambda e, part=part, ct=ct, ptT=ptT: e.transpose(out=ptT[:, ct, :], in_=Mt[:, part, ct, :], identity=ident[:]), reads=[KB["Mt"], "ident"], writes=[pkT])
                            S.op("act", lambda e, part=part, ptT=ptT, b_=b_: e.activation(out=Bcm_b[:, b_, part], in_=ptT[:, 0:4, :], func=AF.Copy), reads=[pkT], writes=[KB["Bcm_b"]])
                        pk_, pkk = next_psF()
                        for ct in range(4):
                            n = 0
                            for part in range(2):
                                for q in range(4):
                                    S.op("pe", lambda e, pk_=pk_, part=part, ct=ct, q=q, n=n: e.matmul(pk_[:, ct * 128:(ct + 1) * 128], lhsT=Xbd[:, part, 4 * ct + q, :], rhs=Cbd[:, part, 4 * ct + q, :],
                                                                                              start=(n == 0 and ct == 0), stop=(n == 7 and ct == 3), skip_group_check=True),
                                         reads=[KB["Xbd"], KB["Cbd"]], writes=[pkk])
                                    n += 1
                        S.op("act", lambda e, pk_=pk_, e_=e_: e.activation(out=Kbd[:, e_].rearrange("p c n -> p (c n)"), in_=pk_[:], func=AF.Copy), reads=[pkk], writes=[KB["Kbd"]])
                    for k in range(1, 17):
                        b_ = (k - 1) if d == 0 else (16 - k)
                        for part in range(2):
                            if part == 0:
                                S.op("dve", lambda e, k=k: e.tensor_tensor(out=V[:], in0=cre, in1=pwb(0, k), op=ALU.mult), reads=[K["pp"], KB["Pw"]], writes=[KB["V"]])
                                S.op("dve", lambda e, k=k: e.tensor_tensor(out=V2[:], in0=cim, in1=pwb(1, k), op=ALU.mult), reads=[K["pp"], KB["Pw"]], writes=[KB["V2"]])
                                S.op("dve", lambda e: e.tensor_tensor(out=V[:], in0=V[:], in1=V2[:], op=ALU.subtract), reads=[KB["V"], KB["V2"]], writes=[KB["V"]])
                            else:
                                S.op("dve", lambda e, k=k: e.tensor_tensor(out=V[:], in0=cre, in1=pwb(1, k), op=ALU.mult), reads=[K["pp"], KB["Pw"]], writes=[KB["V"]])
                                S.op("dve", lambda e, k=k: e.tensor_tensor(out=V2[:], in0=cim, in1=pwb(0, k), op=ALU.mult), reads=[K["pp"], KB["Pw"]], writes=[KB["V2"]])
                                S.op("dve", lambda e: e.scalar_tensor_tensor(out=V[:], in0=V[:], scalar=-1.0, in1=V2[:], op0=ALU.mult, op1=ALU.subtract), reads=[KB["V"], KB["V2"]], writes=[KB["V"]])
                            for h in range(2):
                                hs = slice(64 * h, 64 * h + 64)
                                S.op("dve", lambda e, part=part, h=h, hs=hs, b_=b_: e.tensor_copy(
                                    out=Wo[hs, part].rearrange("p g (b c) -> p g b c", b=16)[:, :, b_, 16 * h:16 * h + 16], in_=V[hs]),
                                     reads=[KB["V"], KB["Wo"]], writes=[KB["Wo"]])
                    blocks = list(range(NBLK)) if d == 0 else [7, 6, 5, 4, 3, 2, 1, 0, 9, 8]
                    def emit_in(blk, slot, d=d):
                        tok0 = blk * 512
                        wsb_ = wsbs[slot]
                        u3 = [uT_all[:, ct, tok0:tok0 + 512].rearrange("p (a b) -> p a b", b=16) for ct in range(4)]
                        for pc in range(8):
                            part, ct = pc // 4, pc % 4
                            for b_ in range(16):
                                for q in range(4):
                                    S.op("pe", lambda e, part=part, ct=ct, b_=b_, q=q, pc=pc, tok0=tok0: e.matmul(
                                        psF[2 + q][:, pc * 32:(pc + 1) * 32], lhsT=Bcm_b[32 * q:32 * q + 32, b_, part, ct, :],
                                        rhs=uT_all[32 * q:32 * q + 32, ct, tok0:tok0 + 512].rearrange("p (a b) -> p a b", b=16)[:, :, b_],
                                        start=(b_ == 0), stop=(b_ == 15), skip_group_check=True, tile_position=(32 * q, 0)),
                                         reads=[KB["Bcm_b"], "uT_all"], writes=["psF%d" % (2 + q)])
                        for q in range(4):
                            S.op("act", lambda e, q=q, wsb_=wsb_: e.activation(out=wsb_.rearrange("p (pc q) a -> p q pc a", q=4)[:, q], in_=psF[2 + q][:, 0:256].rearrange("p (pc a) -> p pc a", pc=8), func=AF.Copy),
                                 reads=["psF%d" % (2 + q)], writes=[KB["wsb"] + str(slot)])

                    def emit_scan(blk, slot, d=d, eng=eng, pp=pp, K=K):
                        wsb_ = wsbs[slot]
                        alist = list(range(32)) if d == 0 else list(range(31, -1, -1))
                        for a in alist:
                            if blk < 8:
                                is_init = (blk == 0 and a == 0) if d == 0 else (blk == 7 and a == 31)
                                is_fin = False; sq = 0
                            else:
                                is_init = (a % 16 == 0) if d == 0 else (a % 16 == 15)
                                is_fin = (a % 16 == 15) if d == 0 else (a % 16 == 0)
                                sq = 1 + (blk - 8) * 2 + a // 16
                            if is_init:
                                if sq == 0:
                                    S.op(eng, lambda e, pp=pp: e.tensor_copy(out=cur[:, 0:32], in_=pp[:, 1072:1104]), reads=[K["pp"], KB["cur"]], writes=[KB["cur"]])
                                    S.op(eng, lambda e, pp=pp: e.tensor_copy(out=cur[:, 32:48], in_=pp[:, 1072:1088]), reads=[K["pp"], KB["cur"]], writes=[KB["cur"]])
                                else:
                                    S.op(eng, lambda e: e.memset(cur[:], 0.0), reads=[KB["cur"]], writes=[KB["cur"]])
                            S.op(eng, lambda e, a=a: e.tensor_copy(out=Est[:, :, a], in_=cur[:, 0:32]), reads=[KB["cur"], KB["Est"]], writes=[KB["Est"]])
                            S.op(eng, lambda e: e.tensor_tensor(out=t1[:], in0=AB[:, 0, :], in1=cur[:, 0:32], op=ALU.mult), reads=[KB["cur"], KB["AB"]], writes=[KB["t1"]])
                            S.op(eng, lambda e: e.tensor_tensor(out=t2[:], in0=AB[:, 1, :], in1=cur[:, 16:48], op=ALU.mult), reads=[KB["cur"], KB["AB"]], writes=[KB["t2"]])
                            S.op(eng, lambda e, a=a, wsb_=wsb_: e.tensor_tensor(out=t1[:], in0=t1[:], in1=wsb_[:, :, a], op=ALU.add), reads=[KB["t1"], KB["wsb"] + str(slot)], writes=[KB["t1"]])
                            S.op(eng, lambda e: e.tensor_tensor(out=cur[:, 0:32], in0=t1[:], in1=t2[:], op=ALU.add), reads=[KB["t1"], KB["t2"]], writes=[KB["cur"]])
                            S.op(eng, lambda e: e.tensor_copy(out=cur[:, 32:48], in_=cur[:, 0:16]), reads=[KB["cur"]], writes=[KB["cur"]])
                            if is_fin:
                                S.dma("sp", lambda e, sq=sq, d=d: e.dma_start(out=nst[l, d, sq - 1], in_=cur[:, 0:32]), reads=[KB["cur"]], writes=["nst"])

                    def emit_out(blk, d=d, fence=()):
                        tok0 = blk * 512
                        for ct in range(4):
                            yb_ = psF[ct % 2]; ybk = "psF%d" % (ct % 2)
                            y3 = yb_[:, 0:512].rearrange("p (a b) -> p a b", b=16)
                            for k in range(16):
                                if d == 0:
                                    osl, isl = slice(k, 16), slice(0, 16 - k)
                                else:
                                    osl, isl = slice(0, 16 - k), slice(k, 16)
                                S.op("pe", lambda e, ct=ct, k=k, y3=y3, osl=osl, isl=isl, tok0=tok0: e.matmul(
                                    y3[:, :, osl], lhsT=Kbd[:, k, ct, :], rhs=uT_all[:, ct, tok0:tok0 + 512].rearrange("p (a b) -> p a b", b=16)[:, :, isl],
                                    start=(k == 0), stop=False, skip_group_check=True), reads=[KB["Kbd"], "uT_all"] + (list(fence) if (ct == 0 and k == 0) else []), writes=[ybk])
                            yt = ytb[ct % 2]; ytk = KB["ytb%d" % (ct % 2)]
                            for q in range(4):
                                gp = 4 * ct + q
                                ps_ = psTf[q % 2]; psk = "psT%d" % (q % 2)
                                for part in range(2):
                                    S.op("pe", lambda e, ps_=ps_, part=part, gp=gp: e.matmul(ps_[0:32, 0:512], lhsT=Est[:, part * 16 + gp, :], rhs=Wo[:, part, gp, :],
                                                                                           start=(part == 0), stop=(part == 1)), reads=[KB["Est"], KB["Wo"]], writes=[psk])
                                S.op("act", lambda e, ps_=ps_, yt=yt, q=q: e.activation(out=yt[:, :, 32 * q:32 * q + 32], in_=ps_[0:32, 0:512].rearrange("p (b c) -> p b c", c=32), func=AF.Copy),
                                     reads=[psk], writes=[ytk])
                            for b_ in range(16):
                                S.op("pe", lambda e, y3=y3, yt=yt, b_=b_: e.matmul(y3[:, :, b_], lhsT=yt[:, b_, :], rhs=ident[0:32, 0:32], start=False, stop=(b_ == 15), skip_group_check=True),
                                     reads=[ytk, "ident"], writes=[ybk])
                            if d == 0:
                                S.op("act", lambda e, yb_=yb_: e.activation(out=yfb[:], in_=yb_[:], func=AF.Copy), reads=[ybk], writes=[KB["yfb"]])
                                S.dma("sp", lambda e, ct=ct, tok0=tok0: e.dma_start(out=yfs[ct, :, tok0:tok0 + 512], in_=yfb[:]), reads=[KB["yfb"]], writes=[("yfs", ct, tok0)])
                            else:
                                S.dma("sp", lambda e, ct=ct, tok0=tok0: e.dma_start(out=yfb[:], in_=yfs[ct, :, tok0:tok0 + 512]), reads=[("yfs", ct, tok0)], writes=[KB["yfb"]])
                                S.op("dve", lambda e, yb_=yb_: e.tensor_tensor(out=yfb[:], in0=yb_[:], in1=yfb[:], op=ALU.add), reads=[ybk, KB["yfb"]], writes=[KB["yfb"]])
                                S.op("dve", lambda e, ct=ct, tok0=tok0: e.scalar_tensor_tensor(out=yfb[:], in0=uT_all[:, ct, tok0:tok0 + 512], scalar=dsk[:, ct:ct + 1], in1=yfb[:], op0=ALU.mult, op1=ALU.add),
                                     reads=["uT_all", "dsk", KB["yfb"]], writes=[KB["yfb"]])
                                S.op("act", lambda e: e.activation(out=yab[:], in_=yfb[:], func=AF.Gelu), reads=[KB["yfb"]], writes=[KB["yab"]])
                                S.dma("sp", lambda e, ct=ct, tok0=tok0: e.dma_start(out=yas[ct, :, tok0:tok0 + 512], in_=yab[:]), reads=[KB["yab"]], writes=["yas"])


                    wsbs = [wsb, tmpf[:].rearrange("p (j a) -> p j a", j=32)]
                    psTf = [psT[i][:].rearrange("p k n -> p (k n)").bitcast(F32) for i in range(2)]
                    emit_in(blocks[0], 0)
                    for bi, blk in enumerate(blocks):
                        if bi + 1 < len(blocks):
                            emit_in(blocks[bi + 1], (bi + 1) % 2)
                        emit_scan(blk, bi % 2)
                        emit_out(blk, fence=([KB["wsb"] + str((bi + 1) % 2)] if bi + 1 < len(blocks) else ()))

        def dump(name, ap, key):
            if name in dbg:
                S.dma("sp", lambda e: e.dma_start(out=dbg[name], in_=ap), reads=[key], writes=["dbgout_" + name])

        for l in range(NL):
            phase_mod(l)
            S.barrier()
            if stage >= 2:
                phase_A(l)
                S.barrier()
                if l == 0:
                    dump("uT", uT_all[:], "uT_all")
                    dump("KT", KT_all[:], "KT_all")
                    dump("V", V_all[:], "V_all")
            if stage >= 3:
                if stage >= 5:
                    if stage == 6:
                        phase_B(l)
                    else:
                        phase_B2(l)
                    S.barrier()
                phase_C(l)
                S.barrier()
                if l == 0:
                    dump("xmid", xmid, ("dram", id(xmid)))
            if stage >= 4:
                phase_D(l)
                S.barrier()
                if l == 0:
                    dump("x1", x1, ("dram", id(x1)))
            if stage < 4:
                break

        if "modrows" in dbg:
            S.dma("sp", lambda e: e.dma_start(out=dbg["modrows"], in_=modrows), reads=["modrows"], writes=["dbgout"])

        S.wait_all("sp")
        S.emit(nc, top)
    return nc


def host_prep(inputs, core):
    b = core
    f = lambda a: np.ascontiguousarray(np.asarray(a, dtype=np.float32))
    m = {}
    m["x0"] = f(np.concatenate([inputs["x_sample"][b], inputs["x_prompt"][4 * b:4 * b + 4].reshape(1024, D)], axis=0))
    m["ckv"] = f(np.stack([inputs["cache_k"][b].reshape(NL, 256, 128), inputs["cache_v"][b].reshape(NL, 256, 128)], axis=1))
    m["cvec"] = f(np.stack([inputs["c"][b], inputs["c_ctx"]], axis=0))
    m["w_mod"] = f(inputs["w_mod"]); m["b_mod"] = f(inputs["b_mod"])
    m["gvec"] = f(np.stack([inputs["g_pre_mix"], inputs["g_post_mix"], inputs["g_pre_ffn"], inputs["g_post_ffn"]], axis=1))
    for k in ("w_in", "w_glu", "w_b_out", "w_c_out", "w_o", "w_up", "w_down", "conv_w", "conv_b", "g_sgu", "sink"):
        m[k] = f(inputs[k])
    m["w_spT"] = f(np.transpose(inputs["w_spatial"], (0, 1, 3, 2)))
    m["b_sp"] = f(inputs["b_spatial"])
    def sm(a):
        a = np.asarray(a, np.float32)
        a = a.reshape((16, 2, 64) + a.shape[2:])
        return np.ascontiguousarray(np.moveaxis(a, 0, 2)).reshape((128, 16) + a.shape[3:])
    s5 = np.zeros((NL, 2, 128, 1104), np.float32)
    for l in range(NL):
        for d in range(2):
            s5[l, d, :, 0:16] = sm(inputs["lam_re"][l, d]); s5[l, d, :, 16:32] = sm(inputs["lam_im"][l, d])
            s5[l, d, :, 32:48] = sm(np.repeat(inputs["log_step"][l, d][:, None], 64, axis=1))
            s5[l, d, :, 48:304] = sm(inputs["b_re"][l, d]).reshape(128, 256); s5[l, d, :, 304:560] = sm(inputs["b_im"][l, d]).reshape(128, 256)
            s5[l, d, :, 560:816] = sm(np.transpose(inputs["c_re"][l, d], (0, 2, 1))).reshape(128, 256)
            s5[l, d, :, 816:1072] = sm(np.transpose(inputs["c_im"][l, d], (0, 2, 1))).reshape(128, 256)
            s5[l, d, :, 1072:1088] = sm(inputs["state_ssm_re"][b, l, d]); s5[l, d, :, 1088:1104] = sm(inputs["state_ssm_im"][b, l, d])
    m["s5p"] = s5
    m["dskT"] = f(np.transpose(inputs["d_skip"].reshape(NL, 4, 128), (0, 2, 1)))
    p = np.arange(128)[:, None]; c = np.arange(32)[None, :]
    t = 128 * c + p
    row = (t // 64).astype(np.float32); col = (t % 64).astype(np.float32)
    inv = (10000.0 ** (-np.arange(0, 32, 2, dtype=np.float32) / 32)).astype(np.float32)
    ar = row[:, :, None] * inv[None, None, :]; ac = col[:, :, None] * inv[None, None, :]
    cosr, sinr, cosc, sinc = np.cos(ar), np.sin(ar), np.cos(ac), np.sin(ac)
    m["ropec"] = f(np.concatenate([cosr, cosr, cosc, cosc], axis=-1))
    m["ropes"] = f(np.concatenate([-sinr, sinr, -sinc, sinc], axis=-1))
    return m


_NC_CACHE = {}


def kernel(**inputs):
    inputs = {k: np.asarray(v) for k, v in inputs.items()}
    if "nc" not in _NC_CACHE:
        _NC_CACHE["nc"] = build(stage=99)
    nc = _NC_CACHE["nc"]
    in_maps = [host_prep(inputs, c) for c in range(8)]
    res = run_bass_kernel_spmd(nc, in_maps, core_ids=list(range(8)))
    ys = np.zeros((8, 4096, D), np.float32)
    yp = np.zeros((32, 256, D), np.float32)
    nk = np.zeros((32, NL, 256, 2, 64), np.float32)
    nv = np.zeros((32, NL, 256, 2, 64), np.float32)
    nsr = np.zeros((32, NL, 2, 32, 64), np.float32)
    nsi = np.zeros((32, NL, 2, 32, 64), np.float32)
    for c in range(8):
        r = res.results[c]
        yy = np.asarray(r["y"], np.float32)
        ys[c] = yy[:4096]
        yp[4 * c:4 * c + 4] = yy[4096:].reshape(4, 256, D)
        st_ = np.asarray(r["nst"], np.float32)
        for l in range(NL):
            for d in range(2):
                for p_ in range(4):
                    a = st_[l, d, p_]
                    nsr[4 * c + p_, l, d] = a[:, 0:16].reshape(2, 64, 16).transpose(2, 0, 1).reshape(32, 64)
                    nsi[4 * c + p_, l, d] = a[:, 16:32].reshape(2, 64, 16).transpose(2, 0, 1).reshape(32, 64)
        kv = np.asarray(r["nkv"], np.float32)
        for l in range(NL):
            nk[4 * c:4 * c + 4, l] = kv[l, 0].reshape(4, 256, 2, 64)
            nv[4 * c:4 * c + 4, l] = kv[l, 1].reshape(4, 256, 2, 64)
    return (yp, ys, nk, nv, nsr, nsi)
```
